# Optimizing a Trainium2 kernel written in Bass

```python
import math
import jax, jax.numpy as jnp
from jax import lax
import numpy as np

D_MODEL = 2048
BATCH = 16
SEQ = 2048
DEPTH = 1

CHUNK = 64
Q_BLOCK = 128
PLE_DIM = 256
EPS = 1e-6
DN_HEADS = 8
DN_HEAD_DIM = 128
DN_WIDTH = DN_HEADS * DN_HEAD_DIM
CONV_K = 4
MLA_HEADS = 8
MLA_NOPE = 128
MLA_ROPE = 64
MLA_V = 128
KV_RANK = 512
MLA_WIDTH = MLA_HEADS * MLA_V
ROPE_BASE = 10000.0
D_FF = 4 * D_MODEL
IN_SIZES = (DN_WIDTH, DN_WIDTH, DN_WIDTH, DN_WIDTH, DN_HEADS, DN_HEADS,
            MLA_HEADS * (MLA_NOPE + MLA_ROPE), KV_RANK, MLA_ROPE, D_MODEL, D_MODEL)
D_IN = sum(IN_SIZES)

kernel_name = "hybrid_gdn_mla_parallel_block"


def rms_norm(x, g):
    xf = x.astype(jnp.float32)
    y = xf * lax.rsqrt(jnp.mean(xf * xf, axis=-1, keepdims=True) + EPS)
    return (y * g.astype(jnp.float32)).astype(x.dtype)


def l2norm(t):
    t = t.astype(jnp.float32)
    return t * lax.rsqrt(jnp.sum(t * t, axis=-1, keepdims=True) + EPS)


def causal_conv(u, w):
    k, c = w.shape
    return lax.conv_general_dilated(u, w[:, None, :].astype(u.dtype), window_strides=(1,),
                                    padding=[(k - 1, 0)], dimension_numbers=('NWC', 'WIO', 'NWC'),
                                    feature_group_count=c)


def rope(x, pos):
    half = x.shape[-1] // 2
    inv = ROPE_BASE ** (-jnp.arange(half, dtype=jnp.float32) / half)
    ang = pos.astype(jnp.float32)[..., None] * inv
    ang = ang.reshape(ang.shape[:2] + (1,) * (x.ndim - 3) + (half,))
    cos = jnp.cos(ang).astype(x.dtype)
    sin = jnp.sin(ang).astype(x.dtype)
    x1, x2 = x[..., :half], x[..., half:]
    return jnp.concatenate([x1 * cos - x2 * sin, x2 * cos + x1 * sin], axis=-1)


def gated_delta_rule(q, k, v, g, beta):
    b, s, h, dk = q.shape
    dv = v.shape[-1]
    n = s // CHUNK

    def blk(t):
        t = t.reshape((b, n, CHUNK, h) + t.shape[3:])
        return jnp.moveaxis(t, 3, 1)

    q = blk(q) * (dk ** -0.5)
    k, v, g, beta = blk(k), blk(v), blk(g), blk(beta)
    gc = jnp.cumsum(g, axis=-1)
    idx = jnp.arange(CHUNK)
    incl = idx[:, None] >= idx[None, :]
    strict = idx[:, None] > idx[None, :]
    decay = jnp.exp(jnp.where(incl, gc[..., :, None] - gc[..., None, :], -jnp.inf))
    kb = k * beta[..., None]
    vb = v * beta[..., None]
    a = jnp.where(strict, jnp.einsum('bhnid,bhnjd->bhnij', kb, k) * decay, 0.0)
    eye = jnp.eye(CHUNK, dtype=a.dtype)
    t_inv = lax.linalg.triangular_solve(eye + a, jnp.broadcast_to(eye, a.shape),
                                        left_side=True, lower=True)
    w = t_inv @ (kb * jnp.exp(gc)[..., None])
    u = t_inv @ vb
    qk = jnp.einsum('bhnid,bhnjd->bhnij', q, k) * decay
    q_dec = q * jnp.exp(gc)[..., None]
    k_dec = k * jnp.exp(gc[..., -1:] - gc)[..., None]
    g_last = jnp.exp(gc[..., -1])

    def step(state, xs):
        w_n, u_n, q_n, k_n, qk_n, gl_n = xs
        v_new = u_n - w_n @ state
        o = q_n @ state + qk_n @ v_new
        state = state * gl_n[..., None, None] + jnp.einsum('bhcd,bhce->bhde', k_n, v_new)
        return state, o

    xs = tuple(jnp.moveaxis(t_, 2, 0) for t_ in (w, u, q_dec, k_dec, qk, g_last))
    s0 = jnp.zeros((b, h, dk, dv), q.dtype)
    _, o = lax.scan(step, s0, xs)
    return o.transpose(1, 0, 3, 2, 4).reshape(b, s, h, dv)


def mla_attention(qn, qr, kn, kr, v):
    b, s, h, _ = qn.shape
    nb = s // Q_BLOCK
    scale = (MLA_NOPE + MLA_ROPE) ** -0.5
    k_chunk = jnp.arange(s) // CHUNK

    def blocks(t):
        return jnp.moveaxis(t.reshape((b, nb, Q_BLOCK) + t.shape[2:]), 1, 0)

    def one(args):
        qn_b, qr_b, j = args
        sc = (jnp.einsum('bqhd,bkhd->bhqk', qn_b, kn)
              + jnp.einsum('bqhr,bkr->bhqk', qr_b, kr)).astype(jnp.float32) * scale
        q_chunk = (j * Q_BLOCK + jnp.arange(Q_BLOCK)) // CHUNK
        allowed = k_chunk[None, :] <= q_chunk[:, None]
        pr = jax.nn.softmax(jnp.where(allowed, sc, -jnp.inf), axis=-1)
        return jnp.einsum('bhqk,bkhd->bqhd', pr.astype(v.dtype), v)

    o = lax.map(one, (blocks(qn), blocks(qr), jnp.arange(nb)))
    return jnp.moveaxis(o, 0, 1).reshape(b, s, h * v.shape[-1])


def setup_inputs(seed: int = 0) -> dict:
    key = jax.random.key(seed)
    ks = iter(jax.random.split(key, 40))
    f32 = jnp.float32

    def nrm(shape, fan_in):
        return jax.random.normal(next(ks), shape, f32) * (fan_in ** -0.5)

    def gain(n):
        return 1.0 + 0.02 * jax.random.normal(next(ks), (DEPTH, n), f32)

    x = jax.random.normal(next(ks), (BATCH, SEQ, D_MODEL), f32)
    p = jax.random.normal(next(ks), (DEPTH, BATCH, SEQ, PLE_DIM), f32)
    offset = jax.random.randint(next(ks), (BATCH,), 0, 4096, dtype=jnp.int32)
    positions = offset[:, None] + jnp.arange(SEQ, dtype=jnp.int32)[None, :]
    dt = jnp.exp(jax.random.uniform(next(ks), (DEPTH, DN_HEADS), f32,
                                    minval=math.log(1e-3), maxval=math.log(0.1)))
    dt_bias = dt + jnp.log(-jnp.expm1(-dt))
    a_log = jnp.log(jax.random.uniform(next(ks), (DEPTH, DN_HEADS), f32, minval=1.0, maxval=16.0))
    return {
        "x": x,
        "p": p,
        "positions": positions,
        "mix_norm": gain(D_MODEL),
        "w_in": nrm((DEPTH, D_MODEL, D_IN), D_MODEL),
        "conv_w": nrm((DEPTH, CONV_K, 3 * DN_WIDTH), CONV_K),
        "dt_bias": dt_bias,
        "a_log": a_log,
        "dn_out_norm": gain(DN_HEAD_DIM),
        "ckv_norm": gain(KV_RANK),
        "w_kv_up": nrm((DEPTH, KV_RANK, MLA_HEADS * (MLA_NOPE + MLA_V)), KV_RANK),
        "q_nope_norm": gain(MLA_NOPE),
        "q_rope_norm": gain(MLA_ROPE),
        "k_nope_norm": gain(MLA_NOPE),
        "k_rope_norm": gain(MLA_ROPE),
        "w_branch_a": nrm((DEPTH, DN_WIDTH, D_MODEL), DN_WIDTH),
        "w_branch_b": nrm((DEPTH, MLA_WIDTH, D_MODEL), MLA_WIDTH),
        "w_out": nrm((DEPTH, D_MODEL, D_MODEL), D_MODEL),
        "mlp_norm": gain(D_MODEL),
        "w_mlp_up": nrm((DEPTH, D_MODEL, D_FF), D_MODEL),
        "w_mlp_down": nrm((DEPTH, D_FF, D_MODEL), D_FF),
        "ple_norm": gain(D_MODEL),
        "w_ple_gate": nrm((DEPTH, D_MODEL, D_MODEL), D_MODEL),
        "w_ple": nrm((DEPTH, PLE_DIM, D_MODEL), PLE_DIM),
    }


def reference(x, p, positions, mix_norm, w_in, conv_w, dt_bias, a_log, dn_out_norm, ckv_norm,
              w_kv_up, q_nope_norm, q_rope_norm, k_nope_norm, k_rope_norm, w_branch_a,
              w_branch_b, w_out, mlp_norm, w_mlp_up, w_mlp_down, ple_norm, w_ple_gate, w_ple):
    b, s, _ = x.shape
    split_points = [int(c) for c in np.cumsum(IN_SIZES)[:-1]]
    for i in range(DEPTH):
        h = rms_norm(x, mix_norm[i])
        proj = h @ w_in[i]
        (dn_q, dn_k, dn_v, dn_z, dn_b, dn_a, mla_q, mla_ckv, mla_kr,
         gate_a, gate_b) = jnp.split(proj, split_points, axis=-1)

        qkv = jax.nn.silu(causal_conv(jnp.concatenate([dn_q, dn_k, dn_v], axis=-1), conv_w[i]))
        cq, ck, cv = jnp.split(qkv, 3, axis=-1)
        hs = (b, s, DN_HEADS, DN_HEAD_DIM)
        q_a = l2norm(cq.reshape(hs))
        k_a = l2norm(ck.reshape(hs))
        v_a = cv.reshape(hs).astype(jnp.float32)
        beta = jax.nn.sigmoid(dn_b.astype(jnp.float32))
        g = -jnp.exp(a_log[i].astype(jnp.float32)) * jax.nn.softplus(
            dn_a.astype(jnp.float32) + dt_bias[i].astype(jnp.float32))
        o_a = gated_delta_rule(q_a, k_a, v_a, g, beta).astype(x.dtype)
        o_a = (rms_norm(o_a, dn_out_norm[i]) * jax.nn.silu(dn_z.reshape(hs))).reshape(b, s, DN_WIDTH)

        mq = mla_q.reshape(b, s, MLA_HEADS, MLA_NOPE + MLA_ROPE)
        qn = rms_norm(mq[..., :MLA_NOPE], q_nope_norm[i])
        qr = rope(rms_norm(mq[..., MLA_NOPE:], q_rope_norm[i]), positions)
        kv = (rms_norm(mla_ckv, ckv_norm[i]) @ w_kv_up[i]).reshape(b, s, MLA_HEADS, MLA_NOPE + MLA_V)
        kn = rms_norm(kv[..., :MLA_NOPE], k_nope_norm[i])
        v_b = kv[..., MLA_NOPE:]
        kr = rope(rms_norm(mla_kr, k_rope_norm[i]), positions)
        o_b = mla_attention(qn, qr, kn, kr, v_b)

        y = (jax.nn.sigmoid(gate_a) * (o_a @ w_branch_a[i])
             + jax.nn.sigmoid(gate_b) * (o_b @ w_branch_b[i]))
        x = x + y @ w_out[i]

        hm = rms_norm(x, mlp_norm[i])
        x = x + jnp.square(jax.nn.relu(hm @ w_mlp_up[i])) @ w_mlp_down[i]

        ple_gate = jax.nn.sigmoid(rms_norm(x, ple_norm[i]) @ w_ple_gate[i])
        x = x + ple_gate * (p[i] @ w_ple[i])
    return x
```

```python
import math
from contextlib import ExitStack
import numpy as np
import concourse.bass as bass
import concourse.mybir as mybir
from concourse.bass_utils import run_bass_kernel_spmd

F32 = mybir.dt.float32
BF16 = mybir.dt.bfloat16
I32 = mybir.dt.int32
AF = mybir.ActivationFunctionType
ALU = mybir.AluOpType
AX = mybir.AxisListType

D = 2048
KC = 16
DIN = 10320
DFF = 8192
H = 8
EPS = 1e-6
T = 256
NB = 2
NEG = -30000.0
STOP = None
TWO_PI = 2.0 * math.pi


class Op:
    __slots__ = ("eng", "fn", "deps", "signal", "pos", "dma", "slot", "ndma", "ticket", "idx")


class Sched:
    ENGS = ("pe", "act", "dve", "pool", "sp")

    def __init__(self):
        self.ops = []
        self.last_w = {}
        self.readers = {}
        self.cnt = {e: 0 for e in self.ENGS}
        self.arena_users = {}
        self.arena_prev = {}

    def arena_phase(self):
        self.arena_prev.update(self.arena_users)
        self.arena_users = {}

    def add(self, eng, fn, reads=(), writes=(), slot=None, ndma=1, arena=False):
        op = Op()
        op.eng = eng; op.fn = fn; op.deps = set(); op.signal = slot is not None
        op.dma = slot is not None; op.slot = slot; op.ndma = ndma; op.ticket = None
        op.pos = self.cnt[eng]; self.cnt[eng] += 1
        op.idx = len(self.ops)
        psr = [k for k in reads if isinstance(k, str) and k.startswith("ps")]
        if psr:
            reads = [k for k in reads if k not in psr]
            writes = list(writes) + psr
        cand = []
        for k in reads:
            w = self.last_w.get(k)
            if w is not None:
                cand.append(w)
        for k in writes:
            w = self.last_w.get(k)
            if w is not None:
                cand.append(w)
            r = self.readers.get(k)
            if r:
                cand.extend(r.values())
        if arena:
            cand.extend(self.arena_prev.values())
        for p in cand:
            if p is op:
                continue
            need = True
            if not p.dma and p.eng == eng:
                if eng == "pe" and not op.dma:
                    need = False
                elif False and op.pos - p.pos > 2:
                    need = False
            if need:
                op.deps.add(p.idx)
                p.signal = True
        for k in reads:
            self.readers.setdefault(k, {})[eng if not op.dma else ("dma", op.idx)] = op
        for k in writes:
            self.last_w[k] = op
            self.readers[k] = {}
        if arena:
            self.arena_users[eng if not op.dma else ("dma", op.slot)] = op
        self.ops.append(op)
        return op

    def emit(self, nc, es, final_slots=()):
        sems = {e: es.enter_context(nc.semaphore("s_" + e)) for e in ("pe", "act", "dve", "pool")}
        slot_sems = {}
        slot_cnt = {}
        eng_cnt = {e: 0 for e in self.ENGS}
        for op in self.ops:
            if op.dma:
                if op.slot not in slot_sems:
                    slot_sems[op.slot] = es.enter_context(nc.semaphore("d_" + str(op.slot)))
                    slot_cnt[op.slot] = 0
                slot_cnt[op.slot] += 16 * op.ndma
                op.ticket = (slot_sems[op.slot], slot_cnt[op.slot])
            elif op.signal:
                eng_cnt[op.eng] += 1
                op.ticket = (sems[op.eng], eng_cnt[op.eng])
        streams = {e: [o for o in self.ops if o.eng == e] for e in self.ENGS}
        ops = self.ops

        def run(e, lst, finals=False):
            waited = {}
            for op in lst:
                need = {}
                for d in op.deps:
                    s, t = ops[d].ticket
                    if need.get(s, (None, 0))[1] < t:
                        need[s] = (s, t)
                for s, t in need.values():
                    if waited.get(s, 0) < t:
                        e.wait_ge(s, t)
                        waited[s] = t
                r = op.fn(e)
                if op.dma:
                    for ins in r:
                        ins.then_inc(op.ticket[0], 16)
                elif op.signal:
                    r.then_inc(op.ticket[0], 1)
            if finals:
                for sl in final_slots:
                    if sl in slot_sems:
                        e.wait_ge(slot_sems[sl], slot_cnt[sl])

        with nc.Block() as block:
            @block.sync
            def _(e):
                run(e, streams["sp"], True)

            @block.gpsimd
            def _(e):
                run(e, streams["pool"])

            @block.tensor
            def _(e):
                run(e, streams["pe"])

            @block.scalar
            def _(e):
                run(e, streams["act"])

            @block.vector
            def _(e):
                run(e, streams["dve"])


def build(NSEQ, S):
    nc = bass.Bass("TRN2", target_bir_lowering=False)
    es = ExitStack()
    sc = Sched()
    NTT = S // T

    def din(name, shape, dt=F32):
        return nc.dram_tensor(name, shape, dt, kind="ExternalInput").ap()

    x_d = din("x", [NSEQ, S, D])
    p_d = din("p", [NSEQ, S, 256])
    pos_d = din("positions", [NSEQ, S], I32)
    w_in = din("w_in", [D, DIN])
    conv_w = din("conv_w", [4, 3072])
    w_kv = din("w_kv_up", [512, 2048])
    w_ba = din("w_branch_a", [1024, D])
    w_bb = din("w_branch_b", [1024, D])
    w_out = din("w_out", [D, D])
    w_up = din("w_mlp_up", [D, DFF])
    w_dn = din("w_mlp_down", [DFF, D])
    w_pg = din("w_ple_gate", [D, D])
    w_ple = din("w_ple", [256, D])
    vec = {}
    for nm, n in (("mix_norm", D), ("mlp_norm", D), ("ple_norm", D), ("dt_bias", 8), ("a_log", 8),
                  ("dn_out_norm", 128), ("ckv_norm", 512), ("q_nope_norm", 128), ("q_rope_norm", 64),
                  ("k_nope_norm", 128), ("k_rope_norm", 64), ("inv_freq", 32)):
        vec[nm] = din(nm, [n])
    out_d = nc.dram_tensor("out", [NSEQ, S, D], F32, kind="ExternalOutput").ap()

    def sb(name, shape, dt=F32):
        return es.enter_context(nc.sbuf_tensor(name, shape, dt))

    def rowsz(t):
        n = 1
        for s_ in t.shape[1:]:
            n *= s_
        return n

    def bview(t, off, dims, np_=128):
        return bass.AP(t, off, [[rowsz(t), np_]] + dims)

    ident_f = sb("ident_f", [128, 128]); ident_b = sb("ident_b", [128, 128], BF16)
    ones_f = sb("ones_f", [128, 128]); negones_f = sb("negones_f", [128, 128]); ones_b = sb("ones_b", [128, 128], BF16)
    Umask = sb("Umask", [128, 128]); SUmask = sb("SUmask", [128, 128]); SLmask = SUmask
    maskb = sb("maskb", [128, 128])
    rows1 = sb("rows1", [128, 128]); rows2 = sb("rows2", [16, 128])
    colsA = sb("colsA", [128, 128]); colsB = sb("colsB", [128, 16])
    bc_dtb = sb("bc_dtb", [128, 8]); bc_alog = sb("bc_alog", [128, 8]); negA = sb("negA", [128, 8])
    col_dno = sb("col_dno", [128, 1]); half_dno = sb("half_dno", [128, 1])
    bc_ckv = sb("bc_ckv", [128, 512]); bc_qn = sb("bc_qn", [128, 128]); bc_qr = sb("bc_qr", [128, 64])
    bc_kn = sb("bc_kn", [128, 128]); bc_kr = sb("bc_kr", [128, 64]); bc_inv = sb("bc_inv", [128, 32])

    knT = sb("knT", [128, H, S], BF16)
    krT = sb("krT", [64, S], BF16)
    Vc = sb("Vc", [128, S // 128, H, 128], BF16)
    St = sb("St", [128, H, 128]); St_b = sb("St_b", [128, H, 128], BF16)
    carry = sb("carry", [128, 24, 3])
    hT = sb("hT", [128, KC, T], BF16)
    oaT = sb("oaT", [128, H, T], BF16)
    NPAN = 3
    pans = [sb(f"pan{i}", [128, KC, 512], BF16) for i in range(NPAN)]
    Pt = [sb(f"Pt{i}", [128, T], BF16) for i in range(3)]
    stat = sb("stat", [128, 64])
    betas = sb("betas", [128, NB, 8]); hbeta = sb("hbeta", [128, NB, 8]); gts = sb("gts", [128, NB, 8])
    eg = sb("eg", [128, NB, 24]); bge = sb("bge", [128, NB, 8])
    cs = sb("cs", [128, NB, 64])
    posi = sb("posi", [128, NB], I32); posf = sb("posf", [128, NB])


    AW = 15360
    arena = sb("arena", [128, AW])

    def af(off, n, shape=None):
        v = arena[:, off:off + n]
        return v if shape is None else v.rearrange(shape[0], **shape[1])

    def ab(off, n, shape=None):
        v = arena[:, off:off + n].bitcast(BF16)
        return v if shape is None else v.rearrange(shape[0], **shape[1])

    HT = ("p (h t) -> p h t", dict(h=H))
    qnT = ab(0, 1024, HT); qrT = ab(1024, 1024, HT)
    qT = ab(2048, 1024, HT); kT = ab(3072, 1024, HT); zT = ab(4096, 1024, HT)
    B4 = ("p (b h d) -> p b h d", dict(b=NB, h=H))
    kbd = ab(5120, 1024, B4); kdec = ab(6144, 1024, B4); vbt = ab(7168, 1024, B4)
    SCR = 8192
    xld = af(SCR, 4096, ("p (s d) -> p s d", dict(s=NB)))
    xs = ab(SCR + 4096, 2048, ("p (s d) -> p s d", dict(s=NB)))
    rawb = af(SCR, 2048); sqb = af(SCR + 2048, 2048)
    cn = ab(SCR + 4096, 256); qnb = ab(SCR + 4352, 512, ("p (h d) -> p h d", dict(h=H)))
    qrb = af(SCR + 4864, 512, ("p (h d) -> p h d", dict(h=H)))
    qrr = ab(SCR + 5376, 256, ("p (h d) -> p h d", dict(h=H)))
    knb = ab(SCR + 5632, 512, ("p (h d) -> p h d", dict(h=H)))
    cT = ab(SCR + 6144, 256, ("p (c t) -> p c t", dict(c=4)))
    miscb = af(SCR + 6400, 80); krn = af(SCR + 6480, 64); krr = ab(SCR + 6544, 32)
    rtmp_t = sb("rtmp_t", [128, 64]); rtmp = rtmp_t[:, :]
    ubuf = [af(SCR + i * 260, 259) for i in range(2)]
    accb = [af(SCR + 520 + i * 256, 256) for i in range(2)]
    tnhb = [af(SCR + 1032 + i * 256, 256) for i in range(2)]
    s2b = [af(SCR + 1544 + i * 256, 256) for i in range(2)]
    sq2b = [af(SCR + 2056 + i * 256, 256) for i in range(2)]
    rstb = [af(SCR + 2568 + i * 256, 256) for i in range(2)]
    vTb = ab(SCR + 3080, 1024, HT)
    G4 = ("p (h d) -> p h d", dict(h=4))
    gUb = af(SCR, 512, G4); decb = af(SCR + 512, 512, G4)
    Ab = [af(SCR + 1024 + i * 512, 512, G4) for i in range(2)]
    Mb = [af(SCR + 2048 + i * 512, 512, G4) for i in range(2)]
    Qb = [af(SCR + 3072 + i * 512, 512, G4) for i in range(2)]
    qkb = ab(SCR + 4096, 256, G4); qkTb = ab(SCR + 4352, 256, G4); wTb = ab(SCR + 4608, 256, G4)
    ub = af(SCR + 4864, 512, G4); vnb = ab(SCR + 5376, 256, G4); o2b = af(SCR + 5632, 512, G4)
    ob = af(SCR + 6144, 512, G4); onb = ab(SCR + 6656, 256, G4); Qfin = ab(SCR + 6912, 256, G4)
    x1 = af(0, 4096, ("p (s d) -> p s d", dict(s=NB)))
    uT = ab(4096, 4096, ("p (c t) -> p c t", dict(c=32)))
    xs2 = xs
    yT = ab(SCR + 6144 - 2048 - 2048, 2048, ("p (c t) -> p c t", dict(c=KC)))
    gtb = [af(SCR + 6144 + i * 256, 256) for i in range(4)]
    pst = af(SCR, 512, ("p (s d) -> p s d", dict(s=NB))); pbf = ab(SCR + 512, 256, ("p (s d) -> p s d", dict(s=NB)))
    pT = ab(SCR + 768, 256, ("p (c t) -> p c t", dict(c=2)))
    rlb = [af(SCR + 1024 + i * 256, 256) for i in range(2)]
    wple_sb = ab(SCR + 2048, 2048, ("p (k c) -> p k c", dict(k=8)))
    rinvt = af(SCR, 256)
    obT = ab(5120, 1024, HT)

    banks = [es.enter_context(nc.psum_tensor(f"bank{i}", [128, 512], F32)) for i in range(8)]
    ring = [0]

    def nbank():
        b = ring[0]; ring[0] = (b + 1) % 6
        return b

    def pkeys(b, half=None):
        return [f"ps{b}"]

    A = dict(arena=True)

    def MM(out, pairs, reads, writes, arena=True):
        n = len(pairs)

        def fn(e):
            r = None
            for i, (l, rr) in enumerate(pairs):
                r = e.matmul(out, lhsT=l, rhs=rr, start=(i == 0), stop=(i == n - 1))
            return r
        return sc.add("pe", fn, reads, writes, arena=arena)

    def TR(out, in_, ident, reads, writes, arena=True):
        return sc.add("pe", lambda e: e.transpose(out, in_, ident), reads, writes, arena=arena)

    def TRS(items, ident, reads, writes, arena=True):
        def fn(e):
            r = None
            for o_, i_ in items:
                r = e.transpose(o_, i_, ident)
            return r
        return sc.add("pe", fn, reads, writes, arena=arena)

    def ACT(out, in_, func, reads, writes, arena=True, **kw):
        return sc.add("act", lambda e: e.activation(out=out, in_=in_, func=func, **kw), reads, writes, arena=arena)

    def DVE(name, reads, writes, arena=True, **kw):
        return sc.add("dve", lambda e: getattr(e, name)(**kw), reads, writes, arena=arena)

    def rsqrt_cols(src, dst, n, scale, bias, reads, writes):
        ACT(rtmp[:, 0:n], src, AF.Sqrt, reads, ["rtmp"], scale=scale, bias=bias)
        DVE("reciprocal", ["rtmp"], writes, out=dst, in_=rtmp[:, 0:n])

    def setup():
        P_ = "pool"
        sc.add(P_, lambda e: e.memset(ident_f[:], 0.0), [], ["ident_f"], arena=False)
        sc.add(P_, lambda e: e.affine_select(out=ident_f[:], in_=ident_f[:], compare_op=ALU.not_equal, fill=1.0,
                                             base=0, pattern=[[-1, 128]], channel_multiplier=1), ["ident_f"], ["ident_f"], arena=False)
        sc.add(P_, lambda e: e.memset(ones_f[:], 1.0), [], ["ones_f"], arena=False)
        sc.add(P_, lambda e: e.memset(negones_f[:], -1.0), [], ["negones_f"], arena=False)
        sc.add(P_, lambda e: e.memset(ones_b[:], 1.0), [], ["ones_b"], arena=False)
        sc.add(P_, lambda e: e.affine_select(out=Umask[:], in_=ones_f[:], compare_op=ALU.is_ge, fill=0.0,
                                             base=0, pattern=[[1, 128]], channel_multiplier=-1), ["ones_f"], ["Umask"], arena=False)
        sc.add(P_, lambda e: e.affine_select(out=SUmask[:], in_=ones_f[:], compare_op=ALU.is_gt, fill=0.0,
                                             base=0, pattern=[[-1, 128]], channel_multiplier=1), ["ones_f"], ["SUmask"], arena=False)
        sc.add(P_, lambda e: e.memset(maskb[:], 0.0), [], ["maskb"], arena=False)
        sc.add(P_, lambda e: e.affine_select(out=maskb[:], in_=maskb[:], compare_op=ALU.is_ge, fill=NEG,
                                             base=0, pattern=[[-1, 128]], channel_multiplier=1), ["maskb"], ["maskb"], arena=False)
        sc.add("dve", lambda e: e.tensor_copy(out=ident_b[:], in_=ident_f[:]), ["ident_f"], ["ident_b"], arena=False)
        cw = conv_w.rearrange("k (c p) -> (k c) p", p=128)
        sc.add("sp", lambda e: [e.dma_start(out=rows1[0:96, :], in_=cw),
                                e.dma_start(out=rows1[96:112, :], in_=vec["mix_norm"].rearrange("(c p) -> c p", p=128)),
                                e.dma_start(out=rows1[112:128, :], in_=vec["mlp_norm"].rearrange("(c p) -> c p", p=128)),
                                e.dma_start(out=rows2[:, :], in_=vec["ple_norm"].rearrange("(c p) -> c p", p=128))],
               [], ["rows"], slot="c_rows", ndma=4, arena=False)

        def bc(nm, t, n):
            return lambda e: [e.dma_start(out=t[:], in_=bass.AP(vec[nm].tensor, 0, [[0, 128], [1, n]]))]
        for nm, t, n in (("dt_bias", bc_dtb, 8), ("a_log", bc_alog, 8), ("ckv_norm", bc_ckv, 512), ("q_nope_norm", bc_qn, 128),
                         ("q_rope_norm", bc_qr, 64), ("k_nope_norm", bc_kn, 128), ("k_rope_norm", bc_kr, 64), ("inv_freq", bc_inv, 32)):
            sc.add("sp", bc(nm, t, n), [], ["bc_" + nm], slot="c_" + nm, arena=False)
        sc.add("sp", lambda e: [e.dma_start(out=col_dno[:], in_=bass.AP(vec["dn_out_norm"].tensor, 0, [[1, 128], [1, 1]]))],
               [], ["col_dno"], slot="c_dno", arena=False)
        if STOP == 'S0':
            return
        b = nbank()
        TR(banks[b][:, 0:128], rows1[:], ident_f[:], ["rows", "ident_f"], pkeys(b), arena=False)
        ACT(colsA[:], banks[b][:, 0:128], AF.Copy, pkeys(b), ["colsA"], arena=False)
        b = nbank()
        TR(banks[b][:, 0:16], rows2[:], ident_f[0:16, 0:16], ["rows", "ident_f"], pkeys(b), arena=False)
        ACT(colsB[:], banks[b][:, 0:16], AF.Copy, pkeys(b), ["colsB"], arena=False)
        ACT(negA[:], bc_alog[:], AF.Exp, ["bc_a_log"], ["negA0"], arena=False)
        DVE("tensor_scalar_mul", ["negA0"], ["negA"], arena=False, out=negA[:], in0=negA[:], scalar1=-1.0)
        DVE("tensor_scalar_mul", ["col_dno"], ["half_dno"], arena=False, out=half_dno[:], in0=col_dno[:], scalar1=0.5)

    panel_specs = []
    pan_ptr = [0]
    pan_issued = [0]

    def wsrc(w_ap, ncols_total, row0, nk, col0, n):
        return bass.AP(w_ap.tensor, row0 * ncols_total + col0, [[ncols_total, 128], [128 * ncols_total, nk], [1, n]])

    def spec_simple(w_ap, ncols_total, row0, nk, col0, n):
        def fn(slot):
            return lambda e: [e.dma_start(out=pans[slot][:, 0:nk, 0:n], in_=wsrc(w_ap, ncols_total, row0, nk, col0, n))]
        return (fn, 1)

    def spec_multi(parts):
        def fn(slot):
            return lambda e: [e.dma_start(out=pans[slot][:, kd:kd + nk, cd:cd + n], in_=wsrc(w, nt, r0, nk, c0, n))
                              for (w, nt, r0, nk, c0, n, kd, cd) in parts]
        return (fn, len(parts))

    def tile_panel_list():
        L = []
        for i in range(3):
            L.append(spec_simple(w_in, DIN, 0, KC, 4112 + 512 * i, 512))
        L.append(spec_simple(w_in, DIN, 0, KC, 5648, 512))
        L.append(spec_multi([(w_in, DIN, 0, KC, 4096, 16, 0, 0), (w_in, DIN, 0, KC, 6160, 64, 0, 16)]))
        L.append(spec_multi([(w_kv, 2048, 0, 4, 512 * i, 512, 4 * i, 0) for i in range(4)]))
        for h in range(H):
            L.append(spec_multi([(w_in, DIN, 0, KC, j * 1024 + h * 128, 128, 0, j * 128) for j in range(4)]))
        for g in range(4):
            L.append(spec_multi([(w_ba, D, 0, 8, g * 512, 512, 0, 0), (w_bb, D, 0, 8, g * 512, 512, 8, 0)]))
            L.append(spec_simple(w_in, DIN, 0, KC, 6224 + g * 512, 512))
            L.append(spec_simple(w_in, DIN, 0, KC, 8272 + g * 512, 512))
        for g in range(4):
            L.append(spec_simple(w_out, D, 0, KC, g * 512, 512))
        for hf in range(2):
            for i in range(8):
                L.append(spec_simple(w_up, DFF, 0, KC, hf * 4096 + i * 512, 512))
            for cb in range(4):
                for sp_ in range(2):
                    L.append(spec_simple(w_dn, D, hf * 4096 + sp_ * 2048, KC, cb * 512, 512))
        for g in range(4):
            L.append(spec_simple(w_pg, D, 0, KC, g * 512, 512))
        return L

    def get_panels(n):
        i = pan_ptr[0]; pan_ptr[0] += n
        assert n <= NPAN
        while pan_issued[0] < len(panel_specs) and pan_issued[0] < i + NPAN:
            j = pan_issued[0]; pan_issued[0] += 1
            fn, nd = panel_specs[j]
            slot = j % NPAN
            sc.add("pool", fn(slot), [], [f"pan{slot}"], slot=f"pan{slot}", ndma=nd, arena=False)
        return [(pans[(i + k) % NPAN], f"pan{(i + k) % NPAN}") for k in range(n)]

    def get_panel():
        return get_panels(1)[0]

    def norm_to_hT(src, src_keys, gcols, goff, xs_v, tag):
        for s_ in range(NB):
            ACT(xs_v[:, s_, :], src[:, s_, :], AF.Square, [src_keys[s_]], [f"xs{s_}", f"ss{s_}"],
                accum_out=stat[:, s_:s_ + 1])
        for s_ in range(NB):
            ACT(stat[:, 4 + s_:5 + s_], stat[:, s_:s_ + 1], AF.Sqrt, [f"ss{s_}"], [f"sd{s_}"], scale=1.0 / D, bias=EPS)
            DVE("reciprocal", [f"sd{s_}"], [f"rs{s_}"], out=stat[:, 8 + s_:9 + s_], in_=stat[:, 4 + s_:5 + s_])
            DVE("tensor_scalar", [src_keys[s_], f"rs{s_}", f"xs{s_}"], [f"xs{s_}"], out=xs_v[:, s_, :], in0=src[:, s_, :],
                scalar1=stat[:, 8 + s_:9 + s_], scalar2=None, op0=ALU.mult)
        if STOP == 'N0':
            return
        for c4 in range(4):
            b = nbank()
            pb = banks[b][:].bitcast(BF16)
            items = []
            for cc in range(4):
                c = c4 * 4 + cc
                for s_ in range(NB):
                    items.append((pb[:, cc * 256 + s_ * 128: cc * 256 + (s_ + 1) * 128], xs_v[:, s_, c * 128:(c + 1) * 128]))
            TRS(items, ident_b[:], ["xs0", "xs1", "ident_b"], pkeys(b))
            for cc in range(4):
                c = c4 * 4 + cc
                if c4 % 2 == 0:
                    ACT(hT[:, c, :], pb[:, cc * 256:(cc + 1) * 256], AF.Copy, pkeys(b) + [gcols[1]], [f"hT{c}"],
                        scale=gcols[0][:, goff + c:goff + c + 1])
                else:
                    DVE("tensor_scalar", pkeys(b) + [gcols[1]], [f"hT{c}"], out=hT[:, c, :], in0=pb[:, cc * 256:(cc + 1) * 256],
                        scalar1=gcols[0][:, goff + c:goff + c + 1], scalar2=None, op0=ALU.mult)

    HTK = [f"hT{c}" for c in range(KC)]

    def tile_body(q, tt):
        t0 = tt * T
        sc.arena_phase()
        for s_ in range(NB):
            sc.add("sp", (lambda s_: lambda e: [e.dma_start(out=xld[:, s_, :], in_=x_d[q, t0 + s_ * 128:t0 + (s_ + 1) * 128, :])])(s_),
                   [], [f"xld{s_}"], slot=f"xld{s_}", arena=True)
        sc.add("sp", lambda e: [e.dma_start(out=posi[:, :], in_=bass.AP(pos_d.tensor, q * S + t0, [[1, 128], [128, NB]]),
                                            allow_slow_non_contiguous=True)],
               [], ["posi"], slot="posi", arena=False)
        norm_to_hT(xld, ["xld0", "xld1"], (colsA, "colsA"), 96, xs, "n1")

        if STOP == 'N':
            return
        sc.arena_phase()
        DVE("tensor_copy", ["posi"], ["posf"], arena=False, out=posf[:, :], in_=posi[:, :])
        for b_ in range(NB):
            uu = rtmp
            DVE("tensor_scalar", ["posf", "bc_inv_freq", "rtmp"], ["rtmp"], out=uu[:, 0:32], in0=bc_inv[:, :],
                scalar1=posf[:, b_:b_ + 1], scalar2=1.0 / TWO_PI, op0=ALU.mult, op1=ALU.mult)
            DVE("tensor_scalar_add", ["rtmp"], ["rtmp2"], out=uu[:, 32:64], in0=uu[:, 0:32], scalar1=0.25)
            DVE("tensor_copy", ["rtmp", "rtmp2"], ["krn"], out=krn[:, :].bitcast(I32), in_=uu[:, :])
            DVE("tensor_copy", ["krn"], ["sqb"], out=sqb[:, 0:64], in_=krn[:, :].bitcast(I32))
            DVE("tensor_sub", ["rtmp", "rtmp2", "sqb"], ["rtmp", "rtmp2"], out=uu[:, :], in0=uu[:, :], in1=sqb[:, 0:64])
            DVE("scalar_tensor_tensor", ["rtmp", "rtmp2"], ["sqb"], out=sqb[:, 0:64], in0=uu[:, :], scalar=0.0, in1=uu[:, :],
                op0=ALU.is_lt, op1=ALU.add)
            ACT(cs[:, b_, :], sqb[:, 0:64], AF.Sin, ["sqb"], [f"cs{b_}"], scale=-TWO_PI, bias=math.pi)

        def rope(dst, src, nh, b_, rd, wr):
            sin_ = cs[:, b_, 0:32].unsqueeze(1).broadcast_to([128, nh, 32])
            cos_ = cs[:, b_, 32:64].unsqueeze(1).broadcast_to([128, nh, 32])
            t1 = sqb[:, 0:nh * 32].rearrange("p (h d) -> p h d", h=nh)
            t2 = sqb[:, 256:256 + nh * 32].rearrange("p (h d) -> p h d", h=nh)
            t3 = sqb[:, 512:512 + nh * 32].rearrange("p (h d) -> p h d", h=nh)
            t4 = sqb[:, 768:768 + nh * 32].rearrange("p (h d) -> p h d", h=nh)
            x1_ = src[:, :, 0:32]; x2_ = src[:, :, 32:64]
            DVE("tensor_tensor", rd + [f"cs{b_}", "sqb"], ["sqb"], out=t1, in0=x1_, in1=cos_, op=ALU.mult)
            DVE("tensor_tensor", rd + [f"cs{b_}", "sqb"], ["sqb"], out=t2, in0=x2_, in1=sin_, op=ALU.mult)
            DVE("tensor_tensor", rd + [f"cs{b_}", "sqb"], ["sqb"], out=t3, in0=x2_, in1=cos_, op=ALU.mult)
            DVE("tensor_tensor", rd + [f"cs{b_}", "sqb"], ["sqb"], out=t4, in0=x1_, in1=sin_, op=ALU.mult)
            DVE("tensor_sub", ["sqb"], wr, out=dst[:, :, 0:32], in0=t1, in1=t2)
            DVE("tensor_add", ["sqb"] + wr, wr, out=dst[:, :, 32:64], in0=t3, in1=t4)

        qpan = get_panels(3)
        for s_ in range(NB):
            for i in range(3):
                pan, pk = qpan[i]
                b = nbank()
                MM(banks[b][:, :], [(hT[:, kc, s_ * 128:(s_ + 1) * 128], pan[:, kc, :]) for kc in range(KC)], HTK + [pk], pkeys(b))
                if i % 2 == 0:
                    ACT(rawb[:, i * 512:(i + 1) * 512], banks[b][:, :], AF.Copy, pkeys(b), [f"raw{i}"])
                else:
                    DVE("tensor_copy", pkeys(b), [f"raw{i}"], out=rawb[:, i * 512:(i + 1) * 512], in_=banks[b][:, :])
            RAW = ["raw0", "raw1", "raw2", "raw3"]
            ACT(sqb[:, 0:1536], rawb[:, 0:1536], AF.Square, RAW, ["sqb"])
            sq3 = sqb[:, 0:1536].rearrange("p (h d) -> p h d", h=H)
            raw3 = rawb[:, 0:1536].rearrange("p (h d) -> p h d", h=H)
            DVE("tensor_reduce", ["sqb"], ["stat16"], out=stat[:, 16:24], in_=sq3[:, :, 0:128], axis=AX.X, op=ALU.add)
            DVE("tensor_reduce", ["sqb"], ["stat24"], out=stat[:, 24:32], in_=sq3[:, :, 128:192], axis=AX.X, op=ALU.add)
            rsqrt_cols(stat[:, 16:24], stat[:, 32:40], 8, 1.0 / 128, EPS, ["stat16"], ["stat32"])
            rsqrt_cols(stat[:, 24:32], stat[:, 40:48], 8, 1.0 / 64, EPS, ["stat24"], ["stat40"])
            DVE("tensor_tensor", RAW + ["stat32", "sqb"], ["sqb"], out=sqb[:, 0:1024].rearrange("p (h d) -> p h d", h=H),
                in0=raw3[:, :, 0:128], in1=stat[:, 32:40].unsqueeze(2).broadcast_to([128, H, 128]), op=ALU.mult)
            DVE("tensor_tensor", ["sqb", "bc_q_nope_norm"], ["qnb"], out=qnb, in0=sqb[:, 0:1024].rearrange("p (h d) -> p h d", h=H),
                in1=bc_qn[:, :].unsqueeze(1).broadcast_to([128, H, 128]), op=ALU.mult)
            DVE("tensor_tensor", RAW + ["stat40", "sqb"], ["sqb"], out=sqb[:, 1024:1536].rearrange("p (h d) -> p h d", h=H),
                in0=raw3[:, :, 128:192], in1=stat[:, 40:48].unsqueeze(2).broadcast_to([128, H, 64]), op=ALU.mult)
            DVE("tensor_tensor", ["sqb", "bc_q_rope_norm"], ["qrb"], out=qrb, in0=sqb[:, 1024:1536].rearrange("p (h d) -> p h d", h=H),
                in1=bc_qr[:, :].unsqueeze(1).broadcast_to([128, H, 64]), op=ALU.mult)
            rope(qrr, qrb, H, s_, ["qrb", "sqb", "sqb"], ["qrr"])
            b = nbank(); pb = banks[b][:].bitcast(BF16)
            TRS([(pb[:, h * 128:(h + 1) * 128], qnb[:, h, :]) for h in range(H)], ident_b[:], ["qnb", "ident_b"], pkeys(b))
            ACT(qnT[:, :, s_ * 128:(s_ + 1) * 128], pb[:, :].rearrange("p (h t) -> p h t", h=H), AF.Copy, pkeys(b), ["qnT"])
            b = nbank(); pb = banks[b][:].bitcast(BF16)
            TRS([(pb[0:64, h * 128:(h + 1) * 128], qrr[:, h, :]) for h in range(H)], ident_b[:], ["qrr", "ident_b"], pkeys(b))
            DVE("tensor_copy", pkeys(b), ["qrT"], out=qrT[0:64, :, s_ * 128:(s_ + 1) * 128],
                in_=pb[0:64, :].rearrange("p (h t) -> p h t", h=H))

        (cpan, cpk), (mpan, mpk), (kvpan, kvpk) = get_panels(3)
        for s_ in range(NB):
            blk = tt * NB + s_
            tsl = slice(t0 + s_ * 128, t0 + (s_ + 1) * 128)
            b = nbank()
            MM(banks[b][:, 0:80], [(hT[:, kc, s_ * 128:(s_ + 1) * 128], mpan[:, kc, 0:80]) for kc in range(KC)], HTK + [mpk], pkeys(b))
            ACT(miscb[:, :], banks[b][:, 0:80], AF.Copy, pkeys(b), ["miscb"])
            ACT(stat[:, 48:56], miscb[:, 0:8], AF.Tanh, ["miscb"], ["tb"], scale=0.5)
            DVE("tensor_scalar", ["tb"], [f"beta{s_}"], arena=False, out=betas[:, s_, :], in0=stat[:, 48:56], scalar1=0.5, scalar2=0.5,
                op0=ALU.mult, op1=ALU.add)
            DVE("tensor_scalar_mul", [f"beta{s_}"], [f"hbeta{s_}"], arena=False, out=hbeta[:, s_, :], in0=betas[:, s_, :], scalar1=0.5)
            DVE("tensor_add", ["miscb", "bc_dt_bias"], ["ga"], out=stat[:, 56:64], in0=miscb[:, 8:16], in1=bc_dtb[:, :])
            ACT(stat[:, 56:64], stat[:, 56:64], AF.Exp, ["ga"], ["ga"])
            ACT(stat[:, 56:64], stat[:, 56:64], AF.Ln, ["ga"], ["ga"], bias=1.0)
            DVE("tensor_mul", ["ga", "negA"], [f"g{s_}"], arena=False, out=gts[:, s_, :], in0=stat[:, 56:64], in1=negA[:, :])
            b2 = nbank()
            MM(banks[b2][:, 0:8], [(Umask[:, :], gts[:, s_, :])], [f"g{s_}", "Umask"], pkeys(b2, 0), arena=False)
            MM(banks[b2][:, 8:16], [(SUmask[:, :], gts[:, s_, :])], [f"g{s_}", "SUmask"], pkeys(b2, 0), arena=False)
            MM(banks[b2][:, 16:24], [(ones_f[:, :], gts[:, s_, :])], [f"g{s_}", "ones_f"], pkeys(b2, 0), arena=False)
            ACT(eg[:, s_, :], banks[b2][:, 0:24], AF.Exp, pkeys(b2, 0), [f"eg{s_}"], arena=False)
            DVE("tensor_mul", [f"eg{s_}", f"beta{s_}"], [f"bge{s_}"], arena=False, out=bge[:, s_, :], in0=eg[:, s_, 0:8], in1=betas[:, s_, :])
            ACT(sqb[:, 1536:1600], miscb[:, 16:80], AF.Square, ["miscb"], ["sqb", "st_kr"], accum_out=stat[:, 15:16])
            rsqrt_cols(stat[:, 15:16], stat[:, 14:15], 1, 1.0 / 64, EPS, ["st_kr"], ["rs_kr"])
            DVE("scalar_tensor_tensor", ["miscb", "rs_kr", "bc_k_rope_norm"], ["krn"], out=krn[:, :], in0=miscb[:, 16:80],
                scalar=stat[:, 14:15], in1=bc_kr[:, :], op0=ALU.mult, op1=ALU.mult)
            rope(krr.rearrange("p (h d) -> p h d", h=1), krn.rearrange("p (h d) -> p h d", h=1), 1, s_, ["krn"], ["krr"])
            b3 = nbank(); pb = banks[b3][:].bitcast(BF16)
            TR(pb[0:64, 0:128], krr[:, :], ident_b[:], ["krr", "ident_b"], pkeys(b3))
            ACT(krT[0:64, tsl], pb[0:64, 0:128], AF.Copy, pkeys(b3), ["krT"])
            b = nbank()
            MM(banks[b][:, :], [(hT[:, kc, s_ * 128:(s_ + 1) * 128], cpan[:, kc, :]) for kc in range(KC)], HTK + [cpk], pkeys(b))
            ACT(sqb[:, 0:512], banks[b][:, :], AF.Square, pkeys(b), ["sqb", "st_c"], accum_out=stat[:, 13:14])
            rsqrt_cols(stat[:, 13:14], stat[:, 12:13], 1, 1.0 / 512, EPS, ["st_c"], ["rs_c"])
            DVE("scalar_tensor_tensor", pkeys(b) + ["rs_c", "bc_ckv_norm"], ["cn"], out=cn[:, :], in0=banks[b][:, :],
                scalar=stat[:, 12:13], in1=bc_ckv[:, :], op0=ALU.mult, op1=ALU.mult)
            b = nbank(); pb = banks[b][:].bitcast(BF16)
            TRS([(pb[:, c * 128:(c + 1) * 128], cn[:, c * 128:(c + 1) * 128]) for c in range(4)], ident_b[:], ["cn", "ident_b"], pkeys(b))
            ACT(cT, pb[:, 0:512].rearrange("p (c t) -> p c t", c=4), AF.Copy, pkeys(b), ["cT"])
            for i in range(4):
                b = nbank()
                MM(banks[b][:, :], [(cT[:, kc, :], kvpan[:, 4 * i + kc, :]) for kc in range(4)], ["cT", kvpk], pkeys(b))
                if i % 2 == 0:
                    ACT(rawb[:, i * 512:(i + 1) * 512], banks[b][:, :], AF.Copy, pkeys(b), [f"raw{i}"])
                else:
                    DVE("tensor_copy", pkeys(b), [f"raw{i}"], out=rawb[:, i * 512:(i + 1) * 512], in_=banks[b][:, :])
            RAW = ["raw0", "raw1", "raw2", "raw3"]
            kv3 = rawb[:, :].rearrange("p (h d) -> p h d", h=H)
            ACT(sqb[:, :], rawb[:, :], AF.Square, RAW, ["sqb"])
            DVE("tensor_reduce", ["sqb"], ["stat16"], out=stat[:, 16:24], in_=sqb[:, :].rearrange("p (h d) -> p h d", h=H)[:, :, 0:128],
                axis=AX.X, op=ALU.add)
            rsqrt_cols(stat[:, 16:24], stat[:, 32:40], 8, 1.0 / 128, EPS, ["stat16"], ["stat32"])
            DVE("tensor_tensor", RAW + ["stat32", "sqb"], ["sqb"], out=sqb[:, 0:1024].rearrange("p (h d) -> p h d", h=H),
                in0=kv3[:, :, 0:128], in1=stat[:, 32:40].unsqueeze(2).broadcast_to([128, H, 128]), op=ALU.mult)
            DVE("tensor_tensor", ["sqb", "bc_k_nope_norm"], ["knb"], out=knb, in0=sqb[:, 0:1024].rearrange("p (h d) -> p h d", h=H),
                in1=bc_kn[:, :].unsqueeze(1).broadcast_to([128, H, 128]), op=ALU.mult)
            ACT(Vc[:, blk, :, :], kv3[:, :, 128:256], AF.Copy, RAW, ["Vc"])
            b = nbank(); pb = banks[b][:].bitcast(BF16)
            TRS([(pb[:, h * 128:(h + 1) * 128], knb[:, h, :]) for h in range(H)], ident_b[:], ["knb", "ident_b"], pkeys(b))
            DVE("tensor_copy", pkeys(b), ["knT"], out=knT[:, :, tsl], in_=pb[:, :].rearrange("p (h t) -> p h t", h=H))

        if STOP == 'M1':
            return
        sc.arena_phase()
        cnt = [0]
        for h in range(H):
            pan, pk = get_panel()
            for j in range(4):
                b = nbank()
                hb = cnt[0] % 2
                pout = banks[b][:, hb * 256:(hb + 1) * 256]
                MM(pout, [(pan[:, kc, j * 128:(j + 1) * 128], hT[:, kc, :]) for kc in range(KC)], HTK + [pk], pkeys(b, hb))
                if j == 3:
                    r = cnt[0] % 2
                    ACT(tnhb[r][:, :], pout, AF.Tanh, pkeys(b, hb), [f"tnh{r}"], scale=0.5)
                    DVE("scalar_tensor_tensor", pkeys(b, hb) + [f"tnh{r}"], ["zT"], out=zT[:, h, :], in0=tnhb[r][:, :], scalar=1.0,
                        in1=pout, op0=ALU.add, op1=ALU.mult)
                    cnt[0] += 1
                    continue
                r = cnt[0] % 2; cnt[0] += 1
                ch = j * 8 + h
                ACT(ubuf[r][:, 3:259], pout, AF.Copy, pkeys(b, hb), [f"ub{r}"])
                DVE("tensor_copy", ["carry", f"ub{r}"], [f"ub{r}"], out=ubuf[r][:, 0:3], in_=carry[:, ch, :])
                DVE("tensor_copy", [f"ub{r}"], ["carry"], out=carry[:, ch, :], in_=ubuf[r][:, 256:259])
                DVE("tensor_scalar", [f"ub{r}", "colsA"], [f"acc{r}"], out=accb[r][:, :], in0=ubuf[r][:, 3:259],
                    scalar1=colsA[:, 3 * 24 + ch:3 * 24 + ch + 1], scalar2=None, op0=ALU.mult)
                for k_ in (2, 1, 0):
                    DVE("scalar_tensor_tensor", [f"ub{r}", "colsA", f"acc{r}"], [f"acc{r}"], out=accb[r][:, :], in0=ubuf[r][:, k_:k_ + 256],
                        scalar=colsA[:, k_ * 24 + ch:k_ * 24 + ch + 1], in1=accb[r][:, :], op0=ALU.mult, op1=ALU.add)
                ACT(tnhb[r][:, :], accb[r][:, :], AF.Tanh, [f"acc{r}"], [f"tnh{r}"], scale=0.5)
                if j == 2:
                    DVE("scalar_tensor_tensor", [f"acc{r}", f"tnh{r}"], ["vTb"], out=vTb[:, h, :], in0=tnhb[r][:, :], scalar=1.0,
                        in1=accb[r][:, :], op0=ALU.add, op1=ALU.mult)
                    b2 = nbank(); pb = banks[b2][:].bitcast(BF16)
                    TRS([(pb[:, s_ * 128:(s_ + 1) * 128], vTb[:, h, s_ * 128:(s_ + 1) * 128]) for s_ in range(NB)], ident_b[:],
                        ["vTb", "ident_b"], pkeys(b2, 0))
                    for s_ in range(NB):
                        ACT(vbt[:, s_, h, :], pb[:, s_ * 128:(s_ + 1) * 128], AF.Copy, pkeys(b2, 0) + [f"hbeta{s_}"], ["vbt"],
                            scale=hbeta[:, s_, h:h + 1])
                    continue
                DVE("scalar_tensor_tensor", [f"acc{r}", f"tnh{r}"], [f"s2{r}"], out=s2b[r][:, :], in0=tnhb[r][:, :], scalar=1.0,
                    in1=accb[r][:, :], op0=ALU.add, op1=ALU.mult)
                ACT(sq2b[r][:, :], s2b[r][:, :], AF.Square, [f"s2{r}"], [f"sq2{r}"])
                b2 = nbank()
                MM(banks[b2][:, 0:256], [(ones_f[:, :], sq2b[r][:, :])], [f"sq2{r}", "ones_f"], pkeys(b2, 0))
                if j == 0:
                    ACT(rstb[r][:, :], banks[b2][:, 0:256], AF.Sqrt, pkeys(b2, 0), [f"rst{r}"], scale=128.0, bias=512.0 * EPS)
                else:
                    ACT(rstb[r][:, :], banks[b2][:, 0:256], AF.Sqrt, pkeys(b2, 0), [f"rst{r}"], scale=1.0, bias=4.0 * EPS)
                DVE("reciprocal", [f"rst{r}"], [f"rst{r}"], out=rstb[r][:, :], in_=rstb[r][:, :])
                dstT = qT if j == 0 else kT
                DVE("tensor_mul", [f"s2{r}", f"rst{r}"], ["qT" if j == 0 else "kT"], out=dstT[:, h, :], in0=s2b[r][:, :], in1=rstb[r][:, :])
                if j == 1:
                    b3 = nbank(); pb = banks[b3][:].bitcast(BF16)
                    TRS([(pb[:, s_ * 128:(s_ + 1) * 128], kT[:, h, s_ * 128:(s_ + 1) * 128]) for s_ in range(NB)], ident_b[:],
                        ["kT", "ident_b"], pkeys(b3, 0))
                    for s_ in range(NB):
                        ACT(kbd[:, s_, h, :], pb[:, s_ * 128:(s_ + 1) * 128], AF.Copy, pkeys(b3, 0) + [f"bge{s_}"], ["kbd"],
                            scale=bge[:, s_, h:h + 1])
                        DVE("tensor_scalar", pkeys(b3, 0) + [f"eg{s_}"], ["kdec"], out=kdec[:, s_, h, :], in0=pb[:, s_ * 128:(s_ + 1) * 128],
                            scalar1=eg[:, s_, 8 + h:9 + h], scalar2=None, op0=ALU.mult)

        if STOP == 'M2':
            return
        sc.arena_phase()
        for s_ in range(NB):
            bs = slice(s_ * 128, (s_ + 1) * 128)
            for hg in range(2):
                hs = [hg * 4 + i for i in range(4)]
                for i, h in enumerate(hs):
                    DVE("tensor_scalar", ["Umask", f"g{s_}", "gUb"], ["gUb"], out=gUb[:, i, :], in0=Umask[:, :], scalar1=gts[:, s_, h:h + 1],
                        scalar2=None, op0=ALU.mult)
                bD = nbank()
                for i, h in enumerate(hs):
                    MM(banks[bD][:, i * 128:(i + 1) * 128], [(gUb[:, i, :], ones_f[:, :]), (negones_f[:, :], gUb[:, i, :]), (ident_f[:, :], maskb[:, :])],
                       ["gUb", "ones_f", "negones_f", "ident_f", "maskb"], pkeys(bD))
                ACT(decb, banks[bD][:, :].rearrange("p (h d) -> p h d", h=4), AF.Exp, pkeys(bD), ["decb"])
                bK = nbank(); bQ = nbank()
                for i, h in enumerate(hs):
                    MM(banks[bK][:, i * 128:(i + 1) * 128], [(kT[:, h, bs], kT[:, h, bs])], ["kT"], pkeys(bK))
                    MM(banks[bQ][:, i * 128:(i + 1) * 128], [(qT[:, h, bs], kT[:, h, bs])], ["kT", "qT"], pkeys(bQ))
                for i, h in enumerate(hs):
                    DVE("scalar_tensor_tensor", pkeys(bK) + ["decb", f"beta{s_}", "A0"], ["A0"], out=Ab[0][:, i, :], in0=banks[bK][:, i * 128:(i + 1) * 128],
                        scalar=betas[:, s_, h:h + 1], in1=decb[:, i, :], op0=ALU.mult, op1=ALU.mult)
                DVE("tensor_tensor", ["A0", "SUmask"], ["A0"], out=Ab[0], in0=Ab[0], in1=SLmask[:, :].unsqueeze(1).broadcast_to([128, 4, 128]), op=ALU.mult)
                DVE("tensor_tensor", pkeys(bQ) + ["decb"], ["qkb"], out=qkb, in0=banks[bQ][:, :].rearrange("p (h d) -> p h d", h=4), in1=decb, op=ALU.mult)
                bT = nbank()
                for i in range(4):
                    MM(banks[bT][:, i * 128:(i + 1) * 128], [(Ab[0][:, i, :], ident_f[:, :])], ["A0", "ident_f"], pkeys(bT))
                ACT(Mb[0], banks[bT][:, :].rearrange("p (h d) -> p h d", h=4), AF.Copy, pkeys(bT), ["M0"])
                bT2 = nbank(); pbT = banks[bT2][:].bitcast(BF16)
                TRS([(pbT[:, i * 128:(i + 1) * 128], qkb[:, i, :]) for i in range(4)], ident_b[:], ["qkb", "ident_b"], pkeys(bT2))
                DVE("tensor_copy", pkeys(bT2), ["qkTb"], out=qkTb, in_=pbT[:, 0:512].rearrange("p (h d) -> p h d", h=4))
                DVE("tensor_tensor", ["M0", "ident_f"], ["Q0"], out=Qb[0], in0=ident_f[:, :].unsqueeze(1).broadcast_to([128, 4, 128]), in1=Mb[0], op=ALU.subtract)
                cur = 0
                for k_ in range(1, 7):
                    nx = 1 - cur
                    bA = nbank()
                    for i in range(4):
                        MM(banks[bA][:, i * 128:(i + 1) * 128], [(Mb[cur][:, i, :], Ab[cur][:, i, :])], [f"M{cur}", f"A{cur}"], pkeys(bA))
                    ACT(Ab[nx], banks[bA][:, :].rearrange("p (h d) -> p h d", h=4), AF.Copy, pkeys(bA), [f"A{nx}"])
                    if k_ < 6:
                        bM = nbank()
                        for i in range(4):
                            MM(banks[bM][:, i * 128:(i + 1) * 128], [(Ab[cur][:, i, :], Mb[cur][:, i, :])], [f"M{cur}", f"A{cur}"], pkeys(bM))
                        DVE("tensor_copy", pkeys(bM), [f"M{nx}"], out=Mb[nx], in_=banks[bM][:, :].rearrange("p (h d) -> p h d", h=4))
                    bQ2 = nbank()
                    for i in range(4):
                        MM(banks[bQ2][:, i * 128:(i + 1) * 128], [(ident_f[:, :], Qb[cur][:, i, :]), (Ab[nx][:, i, :], Qb[cur][:, i, :])],
                           [f"A{nx}", f"Q{cur}", "ident_f"], pkeys(bQ2))
                    if k_ < 6:
                        DVE("tensor_copy", pkeys(bQ2), [f"Q{nx}"], out=Qb[nx], in_=banks[bQ2][:, :].rearrange("p (h d) -> p h d", h=4))
                    else:
                        DVE("tensor_copy", pkeys(bQ2), ["Qfin"], out=Qfin, in_=banks[bQ2][:, :].rearrange("p (h d) -> p h d", h=4))
                    cur = nx
                TT_ = Qfin; TK = "Qfin"
                bW = nbank(); bU = nbank()
                for i, h in enumerate(hs):
                    MM(banks[bW][:, i * 128:(i + 1) * 128], [(kbd[:, s_, h, :], TT_[:, i, :])], ["kbd", TK], pkeys(bW))
                    MM(banks[bU][:, i * 128:(i + 1) * 128], [(TT_[:, i, :], vbt[:, s_, h, :])], ["vbt", TK], pkeys(bU))
                ACT(wTb, banks[bW][:, :].rearrange("p (h d) -> p h d", h=4), AF.Copy, pkeys(bW), ["wTb"])
                ACT(ub, banks[bU][:, :].rearrange("p (h d) -> p h d", h=4), AF.Copy, pkeys(bU), ["ub"])
                bV = nbank()
                for i, h in enumerate(hs):
                    MM(banks[bV][:, i * 128:(i + 1) * 128], [(wTb[:, i, :], St_b[:, h, :])], ["wTb", f"Stb{hg}"], pkeys(bV))
                DVE("tensor_tensor", pkeys(bV) + ["ub"], ["vnb"], out=vnb, in0=ub, in1=banks[bV][:, :].rearrange("p (h d) -> p h d", h=4), op=ALU.subtract)
                bO1 = nbank(); bO2 = nbank()
                for i, h in enumerate(hs):
                    MM(banks[bO1][:, i * 128:(i + 1) * 128], [(qT[:, h, bs], St_b[:, h, :])], ["qT", f"Stb{hg}"], pkeys(bO1))
                    MM(banks[bO2][:, i * 128:(i + 1) * 128], [(qkTb[:, i, :], vnb[:, i, :])], ["qkTb", "vnb"], pkeys(bO2))
                ACT(o2b, banks[bO2][:, :].rearrange("p (h d) -> p h d", h=4), AF.Copy, pkeys(bO2), ["o2b"])
                for i, h in enumerate(hs):
                    DVE("scalar_tensor_tensor", pkeys(bO1) + ["o2b", f"eg{s_}", "ob"], ["ob"], out=ob[:, i, :], in0=banks[bO1][:, i * 128:(i + 1) * 128],
                        scalar=eg[:, s_, h:h + 1], in1=o2b[:, i, :], op0=ALU.mult, op1=ALU.add)
                bS = nbank()
                for i, h in enumerate(hs):
                    MM(banks[bS][:, i * 128:(i + 1) * 128], [(kdec[:, s_, h, :], vnb[:, i, :])], ["kdec", "vnb"], pkeys(bS))
                for i, h in enumerate(hs):
                    DVE("scalar_tensor_tensor", pkeys(bS) + [f"eg{s_}", f"St{hg}"], [f"St{hg}"], arena=False, out=St[:, h, :], in0=St[:, h, :],
                        scalar=eg[:, s_, 16 + h:17 + h], in1=banks[bS][:, i * 128:(i + 1) * 128], op0=ALU.mult, op1=ALU.add)
                ACT(St_b[:, hg * 4:hg * 4 + 4, :], St[:, hg * 4:hg * 4 + 4, :], AF.Copy, [f"St{hg}"], [f"Stb{hg}"], arena=False)
                ACT(o2b, ob, AF.Square, ["ob", "o2b"], ["o2b"])
                DVE("tensor_reduce", ["o2b"], ["stat16"], out=stat[:, 16:20], in_=o2b, axis=AX.X, op=ALU.add)
                rsqrt_cols(stat[:, 16:20], stat[:, 32:36], 4, 1.0 / 128, EPS, ["stat16"], ["stat32"])
                DVE("tensor_tensor", ["ob", "stat32"], ["onb"], out=onb, in0=ob, in1=stat[:, 32:36].unsqueeze(2).broadcast_to([128, 4, 128]), op=ALU.mult)
                bN = nbank(); pbN = banks[bN][:].bitcast(BF16)
                TRS([(pbN[:, i * 128:(i + 1) * 128], onb[:, i, :]) for i in range(4)], ident_b[:], ["onb", "ident_b"], pkeys(bN, 0))
                for i, h in enumerate(hs):
                    DVE("scalar_tensor_tensor", pkeys(bN, 0) + ["zT", "half_dno"], ["oaT"], out=oaT[:, h, bs], in0=pbN[:, i * 128:(i + 1) * 128],
                        scalar=half_dno[:, 0:1], in1=zT[:, h, bs], op0=ALU.mult, op1=ALU.mult)

        if STOP == 'B':
            return
        SCALE = 192.0 ** -0.5
        nkb = tt * NB + NB
        pcnt = [0]
        for h in range(H):
            OT = banks[6][:, 0:256]; RT = banks[7][:, 0:256]
            for kb in range(nkb):
                m = kb - tt * NB
                q0 = 0 if m <= 0 else m * 128
                nq = T - q0
                ksl = slice(kb * 128, (kb + 1) * 128)
                b = nbank()
                MM(banks[b][:, 0:nq], [(knT[:, h, ksl], qnT[:, h, q0:T]), (krT[0:64, ksl], qrT[0:64, h, q0:T])],
                   ["knT", "krT", "qnT", "qrT"], pkeys(b, 0))
                pi = pcnt[0] % 3; pcnt[0] += 1
                ACT(Pt[pi][:, 0:nq], banks[b][:, 0:nq], AF.Exp, pkeys(b, 0), [f"Pt{pi}"], arena=False, scale=SCALE)
                if m >= 0:
                    DVE("memset", [f"Pt{pi}"], [f"Pt{pi}"], arena=False, ap=Pt[pi][64:128, 0:64], constant=0.0)
                first = (kb == 0)
                last = (kb == nkb - 1)

                def fn(e, pi=pi, h=h, kb=kb, q0=q0, nq=nq, first=first, last=last, OT=OT, RT=RT):
                    e.matmul(OT[:, q0:T], lhsT=Vc[:, kb, h, :], rhs=Pt[pi][:, 0:nq], start=first, stop=last)
                    return e.matmul(RT[:, q0:T], lhsT=ones_b[:, :], rhs=Pt[pi][:, 0:nq], start=first, stop=last)
                sc.add("pe", fn, [f"Pt{pi}", "Vc", "ones_b"], pkeys(6) + pkeys(7), arena=False)
            DVE("reciprocal", pkeys(7), ["rinv"], arena=True, out=rinvt[:, :], in_=RT)
            DVE("tensor_tensor", pkeys(6) + ["rinv"], ["obT"], arena=True, out=obT[:, h, :], in0=OT, in1=rinvt[:, :], op=ALU.mult)

        if STOP == 'C':
            return
        sc.arena_phase()
        for s_ in range(NB):
            sc.add("sp", (lambda s_: lambda e: [e.dma_start(out=x1[:, s_, :], in_=x_d[q, t0 + s_ * 128:t0 + (s_ + 1) * 128, :])])(s_),
                   [], [f"x1_{s_}"], slot=f"x1_{s_}", arena=True)
        for g in range(4):
            (wab, wabk), (gap, gapk), (gbp, gbpk) = get_panels(3)
            for cc in range(4):
                c = g * 4 + cc
                csl = slice(cc * 128, (cc + 1) * 128)
                bA = nbank(); bB = nbank()
                MM(banks[bA][:, 0:256], [(wab[:, kc, csl], oaT[:, kc, :]) for kc in range(8)], ["oaT", wabk], pkeys(bA, 0), arena=False)
                MM(banks[bA][:, 256:512], [(wab[:, 8 + kc, csl], obT[:, kc, :]) for kc in range(8)], ["obT", wabk], pkeys(bA, 1), arena=True)
                MM(banks[bB][:, 0:256], [(gap[:, kc, csl], hT[:, kc, :]) for kc in range(KC)], HTK + [gapk], pkeys(bB, 0), arena=False)
                MM(banks[bB][:, 256:512], [(gbp[:, kc, csl], hT[:, kc, :]) for kc in range(KC)], HTK + [gbpk], pkeys(bB, 1), arena=False)
                ACT(gtb[0][:, :], banks[bB][:, 0:256], AF.Tanh, pkeys(bB, 0), ["gt0"], scale=0.5)
                ACT(gtb[1][:, :], banks[bB][:, 256:512], AF.Tanh, pkeys(bB, 1), ["gt1"], scale=0.5)
                DVE("scalar_tensor_tensor", pkeys(bA, 0) + ["gt0", "gt2"], ["gt2"], out=gtb[2][:, :], in0=gtb[0][:, :], scalar=1.0,
                    in1=banks[bA][:, 0:256], op0=ALU.add, op1=ALU.mult)
                DVE("scalar_tensor_tensor", pkeys(bA, 1) + ["gt1", "gt3"], ["gt3"], out=gtb[3][:, :], in0=gtb[1][:, :], scalar=1.0,
                    in1=banks[bA][:, 256:512], op0=ALU.add, op1=ALU.mult)
                DVE("tensor_add", ["gt2", "gt3"], [f"yT{c}"], out=yT[:, c, :], in0=gtb[2][:, :], in1=gtb[3][:, :])
        YK = [f"yT{c}" for c in range(KC)]
        for g in range(4):
            pan, pk = get_panel()
            for s_ in range(NB):
                b = nbank()
                MM(banks[b][:, :], [(yT[:, kc, s_ * 128:(s_ + 1) * 128], pan[:, kc, :]) for kc in range(KC)], YK + [pk], pkeys(b))
                DVE("scalar_tensor_tensor", pkeys(b) + [f"x1_{s_}"], [f"x1_{s_}"], out=x1[:, s_, g * 512:(g + 1) * 512], in0=banks[b][:, :], scalar=0.5,
                    in1=x1[:, s_, g * 512:(g + 1) * 512], op0=ALU.mult, op1=ALU.add)

        if STOP == 'D':
            return
        sc.arena_phase()
        norm_to_hT(x1, ["x1_0", "x1_1"], (colsA, "colsA"), 112, xs, "n2")
        rc = [0]
        for hf in range(2):
            for i in range(8):
                pan, pk = get_panel()
                for cc in range(4):
                    c = i * 4 + cc
                    b = nbank()
                    hb = cc % 2
                    MM(banks[b][:, hb * 256:(hb + 1) * 256], [(pan[:, kc, cc * 128:(cc + 1) * 128], hT[:, kc, :]) for kc in range(KC)], HTK + [pk], pkeys(b, hb))
                    r = rc[0] % 2; rc[0] += 1
                    DVE("tensor_scalar_max", pkeys(b, hb) + [f"rl{r}"], [f"rl{r}"], out=rlb[r][:, :], in0=banks[b][:, hb * 256:(hb + 1) * 256], scalar1=0.0)
                    ACT(uT[:, c, :], rlb[r][:, :], AF.Square, [f"rl{r}"], [f"uT{c}"])
            UK = [f"uT{c}" for c in range(32)]
            for cb in range(4):
                bs2 = [nbank() for _ in range(NB)]
                for sp_ in range(2):
                    pan, pk = get_panel()
                    for s_ in range(NB):
                        def fn(e, pan=pan, s_=s_, sp_=sp_, bb=bs2[s_]):
                            r_ = None
                            for kc in range(KC):
                                r_ = e.matmul(banks[bb][:, :], lhsT=uT[:, sp_ * 16 + kc, s_ * 128:(s_ + 1) * 128], rhs=pan[:, kc, :],
                                              start=(sp_ == 0 and kc == 0), stop=(sp_ == 1 and kc == KC - 1))
                            return r_
                        sc.add("pe", fn, UK + [pk], pkeys(bs2[s_]), arena=True)
                for s_ in range(NB):
                    DVE("tensor_add", pkeys(bs2[s_]) + [f"x1_{s_}"], [f"x1_{s_}"], out=x1[:, s_, cb * 512:(cb + 1) * 512], in0=banks[bs2[s_]][:, :],
                        in1=x1[:, s_, cb * 512:(cb + 1) * 512])

        if STOP == 'E':
            return
        norm_to_hT(x1, ["x1_0", "x1_1"], (colsB, "colsB"), 0, xs, "n3")
        sc.add("sp", lambda e: [e.dma_start(out=pst[:, s_, :], in_=p_d[q, t0 + s_ * 128:t0 + (s_ + 1) * 128, :]) for s_ in range(NB)],
               [], ["pst"], slot="pst", ndma=NB, arena=True)
        DVE("tensor_copy", ["pst"], ["pbf"], out=pbf, in_=pst)
        b = nbank(); pb = banks[b][:].bitcast(BF16)
        TRS([(pb[:, c * 256 + s_ * 128:c * 256 + (s_ + 1) * 128], pbf[:, s_, c * 128:(c + 1) * 128]) for c in range(2) for s_ in range(NB)],
            ident_b[:], ["pbf", "ident_b"], pkeys(b, 0))
        ACT(pT, pb[:, 0:512].rearrange("p (c t) -> p c t", c=2), AF.Copy, pkeys(b, 0), ["pT"])
        wpl, wplk = wple_sb, "wple_sb"
        sc.add("pool", lambda e: [e.dma_start(out=wple_sb[:, 2 * i:2 * i + 2, :], in_=wsrc(w_ple, D, 0, 2, 512 * i, 512)) for i in range(4)],
               [], ["wple_sb"] + [f"yT{c}" for c in range(KC)], slot="wple", ndma=4, arena=True)
        for g in range(4):
            pan, pk = get_panel()
            for s_ in range(NB):
                bG = nbank(); bE = nbank()
                MM(banks[bG][:, :], [(hT[:, kc, s_ * 128:(s_ + 1) * 128], pan[:, kc, :]) for kc in range(KC)], HTK + [pk], pkeys(bG))
                MM(banks[bE][:, :], [(pT[:, kc, s_ * 128:(s_ + 1) * 128], wpl[:, 2 * g + kc, :]) for kc in range(2)], ["pT", wplk], pkeys(bE))
                gt_ = af(SCR + 6144, 512); gt2_ = af(SCR + 6656, 512)
                ACT(gt_, banks[bG][:, :], AF.Tanh, pkeys(bG) + ["gt0", "gt1"], ["gt0", "gt1"], scale=0.5)
                DVE("scalar_tensor_tensor", pkeys(bE) + ["gt0", "gt1", "gt2", "gt3"], ["gt2", "gt3"], out=gt2_, in0=gt_, scalar=1.0,
                    in1=banks[bE][:, :], op0=ALU.add, op1=ALU.mult)
                DVE("scalar_tensor_tensor", ["gt2", "gt3", f"x1_{s_}"], [f"x1_{s_}"], out=x1[:, s_, g * 512:(g + 1) * 512], in0=gt2_, scalar=0.5,
                    in1=x1[:, s_, g * 512:(g + 1) * 512], op0=ALU.mult, op1=ALU.add)
        for s_ in range(NB):
            sc.add("sp", (lambda s_: lambda e: [e.dma_start(out=out_d[q, t0 + s_ * 128:t0 + (s_ + 1) * 128, :], in_=x1[:, s_, :])])(s_),
                   [f"x1_{s_}"], [], slot=f"out{s_}", arena=True)

    setup()
    for q in range(NSEQ if STOP not in ('S0', 'S1') else 0):
        sc.add("dve", lambda e: e.memset(St[:].rearrange("p h d -> p (h d)"), 0.0), [], ["St0", "St1"], arena=False)
        sc.add("dve", lambda e: e.memset(St_b[:].rearrange("p h d -> p (h d)"), 0.0), [], ["Stb0", "Stb1"], arena=False)
        sc.add("dve", lambda e: e.memset(carry[:].rearrange("p c k -> p (c k)"), 0.0), [], ["carry"], arena=False)
        for tt in range(NTT):
            panel_specs.extend(tile_panel_list())
            tile_body(q, tt)
    assert STOP or pan_ptr[0] == len(panel_specs), (pan_ptr[0], len(panel_specs))
    sc.emit(nc, es, final_slots=["out0", "out1"])
    es.close()
    return nc


_CACHE = {}


def _inv_freq():
    half = 32
    return (10000.0 ** (-np.arange(half, dtype=np.float32) / half)).astype(np.float32)


def _maps(inputs, NSEQ, ncores):
    x = np.asarray(inputs["x"]); p = np.asarray(inputs["p"])[0]; pos = np.asarray(inputs["positions"]).astype(np.int32)
    shared = {}
    for nm in ("w_in", "conv_w", "w_kv_up", "w_branch_a", "w_branch_b", "w_out", "w_mlp_up", "w_mlp_down", "w_ple_gate", "w_ple",
               "mix_norm", "mlp_norm", "ple_norm", "dt_bias", "a_log", "dn_out_norm", "ckv_norm", "q_nope_norm", "q_rope_norm",
               "k_nope_norm", "k_rope_norm"):
        a = np.ascontiguousarray(np.asarray(inputs[nm])[0], dtype=np.float32)
        shared[nm] = a
    shared["inv_freq"] = _inv_freq()
    maps = []
    for c in range(ncores):
        m = dict(shared)
        m["x"] = np.ascontiguousarray(x[c * NSEQ:(c + 1) * NSEQ], dtype=np.float32)
        m["p"] = np.ascontiguousarray(p[c * NSEQ:(c + 1) * NSEQ], dtype=np.float32)
        m["positions"] = np.ascontiguousarray(pos[c * NSEQ:(c + 1) * NSEQ])
        maps.append(m)
    return maps


def kernel(**inputs):
    x = np.asarray(inputs["x"])
    Bt, S, _ = x.shape
    ncores = 8 if Bt % 8 == 0 else 1
    NSEQ = Bt // ncores
    key = (NSEQ, S)
    if key not in _CACHE:
        _CACHE[key] = build(NSEQ, S)
    nc = _CACHE[key]
    res = run_bass_kernel_spmd(nc, _maps(inputs, NSEQ, ncores), core_ids=list(range(ncores)))
    return np.concatenate([r["out"] for r in res.results], axis=0).astype(np.float32)
```

```python
import math
from contextlib import ExitStack
import numpy as np
import concourse.bass as bass
import concourse.mybir as mybir
from concourse.bass_utils import run_bass_kernel_spmd

F32 = mybir.dt.float32
BF16 = mybir.dt.bfloat16
I32 = mybir.dt.int32
AF = mybir.ActivationFunctionType
ALU = mybir.AluOpType
AX = mybir.AxisListType

D = 2048
KC = 16
DIN = 10320
DFF = 8192
H = 8
EPS = 1e-6
T = 256
NB = 2
NEG = -30000.0
STOP = None
TWO_PI = 2.0 * math.pi


class Op:
    __slots__ = ("eng", "fn", "deps", "signal", "pos", "dma", "slot", "ndma", "ticket", "idx")


class Sched:
    ENGS = ("pe", "act", "dve", "pool", "sp")

    def __init__(self):
        self.ops = []
        self.last_w = {}
        self.readers = {}
        self.cnt = {e: 0 for e in self.ENGS}
        self.arena_users = {}
        self.arena_prev = {}

    def arena_phase(self):
        self.arena_prev.update(self.arena_users)
        self.arena_users = {}

    def add(self, eng, fn, reads=(), writes=(), slot=None, ndma=1, arena=False):
        op = Op()
        op.eng = eng; op.fn = fn; op.deps = set(); op.signal = slot is not None
        op.dma = slot is not None; op.slot = slot; op.ndma = ndma; op.ticket = None
        op.pos = self.cnt[eng]; self.cnt[eng] += 1
        op.idx = len(self.ops)
        psr = [k for k in reads if isinstance(k, str) and k.startswith("ps")]
        if psr:
            reads = [k for k in reads if k not in psr]
            writes = list(writes) + psr
        cand = []
        for k in reads:
            w = self.last_w.get(k)
            if w is not None:
                cand.append(w)
        for k in writes:
            w = self.last_w.get(k)
            if w is not None:
                cand.append(w)
            r = self.readers.get(k)
            if r:
                cand.extend(r.values())
        if arena:
            cand.extend(self.arena_prev.values())
        for p in cand:
            if p is op:
                continue
            need = True
            if not p.dma and p.eng == eng:
                if eng == "pe" and not op.dma:
                    need = False
                elif False and op.pos - p.pos > 2:
                    need = False
            if need:
                op.deps.add(p.idx)
                p.signal = True
        for k in reads:
            self.readers.setdefault(k, {})[eng if not op.dma else ("dma", op.idx)] = op
        for k in writes:
            self.last_w[k] = op
            self.readers[k] = {}
        if arena:
            self.arena_users[eng if not op.dma else ("dma", op.slot)] = op
        self.ops.append(op)
        return op

    def emit(self, nc, es, final_slots=()):
        sems = {e: es.enter_context(nc.semaphore("s_" + e)) for e in ("pe", "act", "dve", "pool")}
        slot_sems = {}
        slot_cnt = {}
        eng_cnt = {e: 0 for e in self.ENGS}
        for op in self.ops:
            if op.dma:
                if op.slot not in slot_sems:
                    slot_sems[op.slot] = es.enter_context(nc.semaphore("d_" + str(op.slot)))
                    slot_cnt[op.slot] = 0
                slot_cnt[op.slot] += 16 * op.ndma
                op.ticket = (slot_sems[op.slot], slot_cnt[op.slot])
            elif op.signal:
                eng_cnt[op.eng] += 1
                op.ticket = (sems[op.eng], eng_cnt[op.eng])
        streams = {e: [o for o in self.ops if o.eng == e] for e in self.ENGS}
        ops = self.ops

        def run(e, lst, finals=False):
            waited = {}
            for op in lst:
                need = {}
                for d in op.deps:
                    s, t = ops[d].ticket
                    if need.get(s, (None, 0))[1] < t:
                        need[s] = (s, t)
                for s, t in need.values():
                    if waited.get(s, 0) < t:
                        e.wait_ge(s, t)
                        waited[s] = t
                r = op.fn(e)
                if op.dma:
                    for ins in r:
                        ins.then_inc(op.ticket[0], 16)
                elif op.signal:
                    r.then_inc(op.ticket[0], 1)
            if finals:
                for sl in final_slots:
                    if sl in slot_sems:
                        e.wait_ge(slot_sems[sl], slot_cnt[sl])

        with nc.Block() as block:
            @block.sync
            def _(e):
                run(e, streams["sp"], True)

            @block.gpsimd
            def _(e):
                run(e, streams["pool"])

            @block.tensor
            def _(e):
                run(e, streams["pe"])

            @block.scalar
            def _(e):
                run(e, streams["act"])

            @block.vector
            def _(e):
                run(e, streams["dve"])


def build(NSEQ, S):
    nc = bass.Bass("TRN2", target_bir_lowering=False)
    es = ExitStack()
    sc = Sched()
    NTT = S // T

    def din(name, shape, dt=F32):
        return nc.dram_tensor(name, shape, dt, kind="ExternalInput").ap()

    x_d = din("x", [NSEQ, S, D])
    p_d = din("p", [NSEQ, S, 256])
    pos_d = din("positions", [NSEQ, S], I32)
    w_in = din("w_in", [D, DIN])
    conv_w = din("conv_w", [4, 3072])
    w_kv = din("w_kv_up", [512, 2048])
    w_ba = din("w_branch_a", [1024, D])
    w_bb = din("w_branch_b", [1024, D])
    w_out = din("w_out", [D, D])
    w_up = din("w_mlp_up", [D, DFF])
    w_dn = din("w_mlp_down", [DFF, D])
    w_pg = din("w_ple_gate", [D, D])
    w_ple = din("w_ple", [256, D])
    vec = {}
    for nm, n in (("mix_norm", D), ("mlp_norm", D), ("ple_norm", D), ("dt_bias", 8), ("a_log", 8),
                  ("dn_out_norm", 128), ("ckv_norm", 512), ("q_nope_norm", 128), ("q_rope_norm", 64),
                  ("k_nope_norm", 128), ("k_rope_norm", 64), ("inv_freq", 32)):
        vec[nm] = din(nm, [n])
    out_d = nc.dram_tensor("out", [NSEQ, S, D], F32, kind="ExternalOutput").ap()

    def sb(name, shape, dt=F32):
        return es.enter_context(nc.sbuf_tensor(name, shape, dt))

    def rowsz(t):
        n = 1
        for s_ in t.shape[1:]:
            n *= s_
        return n

    def bview(t, off, dims, np_=128):
        return bass.AP(t, off, [[rowsz(t), np_]] + dims)

    ident_f = sb("ident_f", [128, 128]); ident_b = sb("ident_b", [128, 128], BF16)
    ones_f = sb("ones_f", [128, 128]); negones_f = sb("negones_f", [128, 128]); ones_b = sb("ones_b", [128, 128], BF16)
    Umask = sb("Umask", [128, 128]); SUmask = sb("SUmask", [128, 128]); SLmask = SUmask
    maskb = sb("maskb", [128, 128])
    rows1 = sb("rows1", [128, 128]); rows2 = sb("rows2", [16, 128])
    colsA = sb("colsA", [128, 128]); colsB = sb("colsB", [128, 16])
    bc_dtb = sb("bc_dtb", [128, 8]); bc_alog = sb("bc_alog", [128, 8]); negA = sb("negA", [128, 8])
    col_dno = sb("col_dno", [128, 1]); half_dno = sb("half_dno", [128, 1])
    bc_ckv = sb("bc_ckv", [128, 512]); bc_qn = sb("bc_qn", [128, 128]); bc_qr = sb("bc_qr", [128, 64])
    bc_kn = sb("bc_kn", [128, 128]); bc_kr = sb("bc_kr", [128, 64]); bc_inv = sb("bc_inv", [128, 32])

    knT = sb("knT", [128, H, S], BF16)
    krT = sb("krT", [64, S], BF16)
    Vc = sb("Vc", [128, S // 128, H, 128], BF16)
    St = sb("St", [128, H, 128]); St_b = sb("St_b", [128, H, 128], BF16)
    carry = sb("carry", [128, 24, 3])
    hT = sb("hT", [128, KC, T], BF16)
    oaT = sb("oaT", [128, H, T], BF16)
    NPAN = 3
    pans = [sb(f"pan{i}", [128, KC, 512], BF16) for i in range(NPAN)]
    Pt = [sb(f"Pt{i}", [128, T], BF16) for i in range(3)]
    stat = sb("stat", [128, 64])
    betas = sb("betas", [128, NB, 8]); hbeta = sb("hbeta", [128, NB, 8]); gts = sb("gts", [128, NB, 8])
    eg = sb("eg", [128, NB, 24]); bge = sb("bge", [128, NB, 8])
    cs = sb("cs", [128, NB, 64])
    posi = sb("posi", [128, NB], I32); posf = sb("posf", [128, NB])


    AW = 15360
    arena = sb("arena", [128, AW])

    def af(off, n, shape=None):
        v = arena[:, off:off + n]
        return v if shape is None else v.rearrange(shape[0], **shape[1])

    def ab(off, n, shape=None):
        v = arena[:, off:off + n].bitcast(BF16)
        return v if shape is None else v.rearrange(shape[0], **shape[1])

    HT = ("p (h t) -> p h t", dict(h=H))
    qnT = ab(0, 1024, HT); qrT = ab(1024, 1024, HT)
    qT = ab(2048, 1024, HT); kT = ab(3072, 1024, HT); zT = ab(4096, 1024, HT)
    B4 = ("p (b h d) -> p b h d", dict(b=NB, h=H))
    kbd = ab(5120, 1024, B4); kdec = ab(6144, 1024, B4); vbt = ab(7168, 1024, B4)
    SCR = 8192
    xld = af(SCR, 4096, ("p (s d) -> p s d", dict(s=NB)))
    xs = ab(SCR + 4096, 2048, ("p (s d) -> p s d", dict(s=NB)))
    rawb = af(SCR, 2048); sqb = af(SCR + 2048, 2048)
    cn = ab(SCR + 4096, 256); qnb = ab(SCR + 4352, 512, ("p (h d) -> p h d", dict(h=H)))
    qrb = af(SCR + 4864, 512, ("p (h d) -> p h d", dict(h=H)))
    qrr = ab(SCR + 5376, 256, ("p (h d) -> p h d", dict(h=H)))
    knb = ab(SCR + 5632, 512, ("p (h d) -> p h d", dict(h=H)))
    cT = ab(SCR + 6144, 256, ("p (c t) -> p c t", dict(c=4)))
    miscb = af(SCR + 6400, 80); krn = af(SCR + 6480, 64); krr = ab(SCR + 6544, 32)
    rtmp_t = sb("rtmp_t", [128, 64]); rtmp = rtmp_t[:, :]
    ubuf = [af(SCR + i * 260, 259) for i in range(2)]
    accb = [af(SCR + 520 + i * 256, 256) for i in range(2)]
    tnhb = [af(SCR + 1032 + i * 256, 256) for i in range(2)]
    s2b = [af(SCR + 1544 + i * 256, 256) for i in range(2)]
    sq2b = [af(SCR + 2056 + i * 256, 256) for i in range(2)]
    rstb = [af(SCR + 2568 + i * 256, 256) for i in range(2)]
    vTb = ab(SCR + 3080, 1024, HT)
    G4 = ("p (h d) -> p h d", dict(h=4))
    gUb = af(SCR, 512, G4); decb = af(SCR + 512, 512, G4)
    Ab = [af(SCR + 1024 + i * 512, 512, G4) for i in range(2)]
    Mb = [af(SCR + 2048 + i * 512, 512, G4) for i in range(2)]
    Qb = [af(SCR + 3072 + i * 512, 512, G4) for i in range(2)]
    qkb = ab(SCR + 4096, 256, G4); qkTb = ab(SCR + 4352, 256, G4); wTb = ab(SCR + 4608, 256, G4)
    ub = af(SCR + 4864, 512, G4); vnb = ab(SCR + 5376, 256, G4); o2b = af(SCR + 5632, 512, G4)
    ob = af(SCR + 6144, 512, G4); onb = ab(SCR + 6656, 256, G4); Qfin = ab(SCR + 6912, 256, G4)
    x1 = af(0, 4096, ("p (s d) -> p s d", dict(s=NB)))
    uT = ab(4096, 4096, ("p (c t) -> p c t", dict(c=32)))
    xs2 = xs
    yT = ab(SCR + 6144 - 2048 - 2048, 2048, ("p (c t) -> p c t", dict(c=KC)))
    gtb = [af(SCR + 6144 + i * 256, 256) for i in range(4)]
    pst = af(SCR, 512, ("p (s d) -> p s d", dict(s=NB))); pbf = ab(SCR + 512, 256, ("p (s d) -> p s d", dict(s=NB)))
    pT = ab(SCR + 768, 256, ("p (c t) -> p c t", dict(c=2)))
    rlb = [af(SCR + 1024 + i * 256, 256) for i in range(2)]
    wple_sb = ab(SCR + 2048, 2048, ("p (k c) -> p k c", dict(k=8)))
    rinvt = af(SCR, 256)
    obT = ab(5120, 1024, HT)

    banks = [es.enter_context(nc.psum_tensor(f"bank{i}", [128, 512], F32)) for i in range(8)]
    ring = [0]

    def nbank():
        b = ring[0]; ring[0] = (b + 1) % 6
        return b

    def pkeys(b, half=None):
        return [f"ps{b}"]

    A = dict(arena=True)

    def MM(out, pairs, reads, writes, arena=True):
        n = len(pairs)

        def fn(e):
            r = None
            for i, (l, rr) in enumerate(pairs):
                r = e.matmul(out, lhsT=l, rhs=rr, start=(i == 0), stop=(i == n - 1))
            return r
        return sc.add("pe", fn, reads, writes, arena=arena)

    def TR(out, in_, ident, reads, writes, arena=True):
        return sc.add("pe", lambda e: e.transpose(out, in_, ident), reads, writes, arena=arena)

    def TRS(items, ident, reads, writes, arena=True):
        def fn(e):
            r = None
            for o_, i_ in items:
                r = e.transpose(o_, i_, ident)
            return r
        return sc.add("pe", fn, reads, writes, arena=arena)

    def ACT(out, in_, func, reads, writes, arena=True, **kw):
        return sc.add("act", lambda e: e.activation(out=out, in_=in_, func=func, **kw), reads, writes, arena=arena)

    def DVE(name, reads, writes, arena=True, **kw):
        return sc.add("dve", lambda e: getattr(e, name)(**kw), reads, writes, arena=arena)

    def rsqrt_cols(src, dst, n, scale, bias, reads, writes):
        ACT(rtmp[:, 0:n], src, AF.Sqrt, reads, ["rtmp"], scale=scale, bias=bias)
        DVE("reciprocal", ["rtmp"], writes, out=dst, in_=rtmp[:, 0:n])

    def setup():
        P_ = "pool"
        sc.add(P_, lambda e: e.memset(ident_f[:], 0.0), [], ["ident_f"], arena=False)
        sc.add(P_, lambda e: e.affine_select(out=ident_f[:], in_=ident_f[:], compare_op=ALU.not_equal, fill=1.0,
                                             base=0, pattern=[[-1, 128]], channel_multiplier=1), ["ident_f"], ["ident_f"], arena=False)
        sc.add(P_, lambda e: e.memset(ones_f[:], 1.0), [], ["ones_f"], arena=False)
        sc.add(P_, lambda e: e.memset(negones_f[:], -1.0), [], ["negones_f"], arena=False)
        sc.add(P_, lambda e: e.memset(ones_b[:], 1.0), [], ["ones_b"], arena=False)
        sc.add(P_, lambda e: e.affine_select(out=Umask[:], in_=ones_f[:], compare_op=ALU.is_ge, fill=0.0,
                                             base=0, pattern=[[1, 128]], channel_multiplier=-1), ["ones_f"], ["Umask"], arena=False)
        sc.add(P_, lambda e: e.affine_select(out=SUmask[:], in_=ones_f[:], compare_op=ALU.is_gt, fill=0.0,
                                             base=0, pattern=[[-1, 128]], channel_multiplier=1), ["ones_f"], ["SUmask"], arena=False)
        sc.add(P_, lambda e: e.memset(maskb[:], 0.0), [], ["maskb"], arena=False)
        sc.add(P_, lambda e: e.affine_select(out=maskb[:], in_=maskb[:], compare_op=ALU.is_ge, fill=NEG,
                                             base=0, pattern=[[-1, 128]], channel_multiplier=1), ["maskb"], ["maskb"], arena=False)
        sc.add("dve", lambda e: e.tensor_copy(out=ident_b[:], in_=ident_f[:]), ["ident_f"], ["ident_b"], arena=False)
        cw = conv_w.rearrange("k (c p) -> (k c) p", p=128)
        sc.add("sp", lambda e: [e.dma_start(out=rows1[0:96, :], in_=cw),
                                e.dma_start(out=rows1[96:112, :], in_=vec["mix_norm"].rearrange("(c p) -> c p", p=128)),
                                e.dma_start(out=rows1[112:128, :], in_=vec["mlp_norm"].rearrange("(c p) -> c p", p=128)),
                                e.dma_start(out=rows2[:, :], in_=vec["ple_norm"].rearrange("(c p) -> c p", p=128))],
               [], ["rows"], slot="c_rows", ndma=4, arena=False)

        def bc(nm, t, n):
            return lambda e: [e.dma_start(out=t[:], in_=bass.AP(vec[nm].tensor, 0, [[0, 128], [1, n]]))]
        for nm, t, n in (("dt_bias", bc_dtb, 8), ("a_log", bc_alog, 8), ("ckv_norm", bc_ckv, 512), ("q_nope_norm", bc_qn, 128),
                         ("q_rope_norm", bc_qr, 64), ("k_nope_norm", bc_kn, 128), ("k_rope_norm", bc_kr, 64), ("inv_freq", bc_inv, 32)):
            sc.add("sp", bc(nm, t, n), [], ["bc_" + nm], slot="c_" + nm, arena=False)
        sc.add("sp", lambda e: [e.dma_start(out=col_dno[:], in_=bass.AP(vec["dn_out_norm"].tensor, 0, [[1, 128], [1, 1]]))],
               [], ["col_dno"], slot="c_dno", arena=False)
        if STOP == 'S0':
            return
        b = nbank()
        TR(banks[b][:, 0:128], rows1[:], ident_f[:], ["rows", "ident_f"], pkeys(b), arena=False)
        ACT(colsA[:], banks[b][:, 0:128], AF.Copy, pkeys(b), ["colsA"], arena=False)
        b = nbank()
        TR(banks[b][:, 0:16], rows2[:], ident_f[0:16, 0:16], ["rows", "ident_f"], pkeys(b), arena=False)
        ACT(colsB[:], banks[b][:, 0:16], AF.Copy, pkeys(b), ["colsB"], arena=False)
        ACT(negA[:], bc_alog[:], AF.Exp, ["bc_a_log"], ["negA0"], arena=False)
        DVE("tensor_scalar_mul", ["negA0"], ["negA"], arena=False, out=negA[:], in0=negA[:], scalar1=-1.0)
        DVE("tensor_scalar_mul", ["col_dno"], ["half_dno"], arena=False, out=half_dno[:], in0=col_dno[:], scalar1=0.5)

    panel_specs = []
    pan_ptr = [0]
    pan_issued = [0]

    def wsrc(w_ap, ncols_total, row0, nk, col0, n):
        return bass.AP(w_ap.tensor, row0 * ncols_total + col0, [[ncols_total, 128], [128 * ncols_total, nk], [1, n]])

    def spec_simple(w_ap, ncols_total, row0, nk, col0, n):
        def fn(slot):
            return lambda e: [e.dma_start(out=pans[slot][:, 0:nk, 0:n], in_=wsrc(w_ap, ncols_total, row0, nk, col0, n))]
        return (fn, 1)

    def spec_multi(parts):
        def fn(slot):
            return lambda e: [e.dma_start(out=pans[slot][:, kd:kd + nk, cd:cd + n], in_=wsrc(w, nt, r0, nk, c0, n))
                              for (w, nt, r0, nk, c0, n, kd, cd) in parts]
        return (fn, len(parts))

    def tile_panel_list():
        L = []
        for i in range(3):
            L.append(spec_simple(w_in, DIN, 0, KC, 4112 + 512 * i, 512))
        L.append(spec_simple(w_in, DIN, 0, KC, 5648, 512))
        L.append(spec_multi([(w_in, DIN, 0, KC, 4096, 16, 0, 0), (w_in, DIN, 0, KC, 6160, 64, 0, 16)]))
        L.append(spec_multi([(w_kv, 2048, 0, 4, 512 * i, 512, 4 * i, 0) for i in range(4)]))
        for h in range(H):
            L.append(spec_multi([(w_in, DIN, 0, KC, j * 1024 + h * 128, 128, 0, j * 128) for j in range(4)]))
        for g in range(4):
            L.append(spec_multi([(w_ba, D, 0, 8, g * 512, 512, 0, 0), (w_bb, D, 0, 8, g * 512, 512, 8, 0)]))
            L.append(spec_simple(w_in, DIN, 0, KC, 6224 + g * 512, 512))
            L.append(spec_simple(w_in, DIN, 0, KC, 8272 + g * 512, 512))
        for g in range(4):
            L.append(spec_simple(w_out, D, 0, KC, g * 512, 512))
        for hf in range(2):
            for i in range(8):
                L.append(spec_simple(w_up, DFF, 0, KC, hf * 4096 + i * 512, 512))
            for cb in range(4):
                for sp_ in range(2):
                    L.append(spec_simple(w_dn, D, hf * 4096 + sp_ * 2048, KC, cb * 512, 512))
        for g in range(4):
            L.append(spec_simple(w_pg, D, 0, KC, g * 512, 512))
        return L

    NPT = 66
    wbf = nc.dram_tensor("wbf", [NPT, 128, KC * 512], BF16).ap()

    def get_panels(n):
        i = pan_ptr[0]; pan_ptr[0] += n
        assert n <= NPAN
        while pan_issued[0] < len(panel_specs) and pan_issued[0] < i + NPAN:
            j = pan_issued[0]; pan_issued[0] += 1
            fn, nd = panel_specs[j]
            slot = j % NPAN
            jm = j % NPT
            flat = pans[slot][:].rearrange("p k c -> p (k c)")
            if j < NPT:
                sc.add("pool", fn(slot), [], [f"pan{slot}"], slot=f"pan{slot}", ndma=nd, arena=False)
                sc.add("sp", (lambda jm, flat: lambda e: [e.dma_start(out=wbf[jm], in_=flat)])(jm, flat),
                       [f"pan{slot}"], [f"wbf{jm}"], slot=f"wst{slot}", arena=False)
            else:
                sc.add("pool", (lambda jm, flat: lambda e: [e.dma_start(out=flat, in_=wbf[jm])])(jm, flat),
                       [f"wbf{jm}"], [f"pan{slot}"], slot=f"pan{slot}", arena=False)
        return [(pans[(i + k) % NPAN], f"pan{(i + k) % NPAN}") for k in range(n)]

    def get_panel():
        return get_panels(1)[0]

    def norm_to_hT(src, src_keys, gcols, goff, xs_v, tag):
        for s_ in range(NB):
            ACT(xs_v[:, s_, :], src[:, s_, :], AF.Square, [src_keys[s_]], [f"xs{s_}", f"ss{s_}"],
                accum_out=stat[:, s_:s_ + 1])
        for s_ in range(NB):
            ACT(stat[:, 4 + s_:5 + s_], stat[:, s_:s_ + 1], AF.Sqrt, [f"ss{s_}"], [f"sd{s_}"], scale=1.0 / D, bias=EPS)
            DVE("reciprocal", [f"sd{s_}"], [f"rs{s_}"], out=stat[:, 8 + s_:9 + s_], in_=stat[:, 4 + s_:5 + s_])
            DVE("tensor_scalar", [src_keys[s_], f"rs{s_}", f"xs{s_}"], [f"xs{s_}"], out=xs_v[:, s_, :], in0=src[:, s_, :],
                scalar1=stat[:, 8 + s_:9 + s_], scalar2=None, op0=ALU.mult)
        if STOP == 'N0':
            return
        for c4 in range(4):
            b = nbank()
            pb = banks[b][:].bitcast(BF16)
            items = []
            for cc in range(4):
                c = c4 * 4 + cc
                for s_ in range(NB):
                    items.append((pb[:, cc * 256 + s_ * 128: cc * 256 + (s_ + 1) * 128], xs_v[:, s_, c * 128:(c + 1) * 128]))
            TRS(items, ident_b[:], ["xs0", "xs1", "ident_b"], pkeys(b))
            for cc in range(4):
                c = c4 * 4 + cc
                if c4 % 2 == 0:
                    ACT(hT[:, c, :], pb[:, cc * 256:(cc + 1) * 256], AF.Copy, pkeys(b) + [gcols[1]], [f"hT{c}"],
                        scale=gcols[0][:, goff + c:goff + c + 1])
                else:
                    DVE("tensor_scalar", pkeys(b) + [gcols[1]], [f"hT{c}"], out=hT[:, c, :], in0=pb[:, cc * 256:(cc + 1) * 256],
                        scalar1=gcols[0][:, goff + c:goff + c + 1], scalar2=None, op0=ALU.mult)

    HTK = [f"hT{c}" for c in range(KC)]

    def tile_body(q, tt):
        t0 = tt * T
        sc.arena_phase()
        for s_ in range(NB):
            sc.add("sp", (lambda s_: lambda e: [e.dma_start(out=xld[:, s_, :], in_=x_d[q, t0 + s_ * 128:t0 + (s_ + 1) * 128, :])])(s_),
                   [], [f"xld{s_}"], slot=f"xld{s_}", arena=True)
        sc.add("sp", lambda e: [e.dma_start(out=posi[:, :], in_=bass.AP(pos_d.tensor, q * S + t0, [[1, 128], [128, NB]]),
                                            allow_slow_non_contiguous=True)],
               [], ["posi"], slot="posi", arena=False)
        norm_to_hT(xld, ["xld0", "xld1"], (colsA, "colsA"), 96, xs, "n1")

        if STOP == 'N':
            return
        sc.arena_phase()
        DVE("tensor_copy", ["posi"], ["posf"], arena=False, out=posf[:, :], in_=posi[:, :])
        for b_ in range(NB):
            uu = rtmp
            DVE("tensor_scalar", ["posf", "bc_inv_freq", "rtmp"], ["rtmp"], out=uu[:, 0:32], in0=bc_inv[:, :],
                scalar1=posf[:, b_:b_ + 1], scalar2=1.0 / TWO_PI, op0=ALU.mult, op1=ALU.mult)
            DVE("tensor_scalar_add", ["rtmp"], ["rtmp2"], out=uu[:, 32:64], in0=uu[:, 0:32], scalar1=0.25)
            DVE("tensor_copy", ["rtmp", "rtmp2"], ["krn"], out=krn[:, :].bitcast(I32), in_=uu[:, :])
            DVE("tensor_copy", ["krn"], ["sqb"], out=sqb[:, 0:64], in_=krn[:, :].bitcast(I32))
            DVE("tensor_sub", ["rtmp", "rtmp2", "sqb"], ["rtmp", "rtmp2"], out=uu[:, :], in0=uu[:, :], in1=sqb[:, 0:64])
            DVE("scalar_tensor_tensor", ["rtmp", "rtmp2"], ["sqb"], out=sqb[:, 0:64], in0=uu[:, :], scalar=0.0, in1=uu[:, :],
                op0=ALU.is_lt, op1=ALU.add)
            ACT(cs[:, b_, :], sqb[:, 0:64], AF.Sin, ["sqb"], [f"cs{b_}"], scale=-TWO_PI, bias=math.pi)

        def rope(dst, src, nh, b_, rd, wr):
            sin_ = cs[:, b_, 0:32].unsqueeze(1).broadcast_to([128, nh, 32])
            cos_ = cs[:, b_, 32:64].unsqueeze(1).broadcast_to([128, nh, 32])
            t1 = sqb[:, 0:nh * 32].rearrange("p (h d) -> p h d", h=nh)
            t2 = sqb[:, 256:256 + nh * 32].rearrange("p (h d) -> p h d", h=nh)
            t3 = sqb[:, 512:512 + nh * 32].rearrange("p (h d) -> p h d", h=nh)
            t4 = sqb[:, 768:768 + nh * 32].rearrange("p (h d) -> p h d", h=nh)
            x1_ = src[:, :, 0:32]; x2_ = src[:, :, 32:64]
            DVE("tensor_tensor", rd + [f"cs{b_}", "sqb"], ["sqb"], out=t1, in0=x1_, in1=cos_, op=ALU.mult)
            DVE("tensor_tensor", rd + [f"cs{b_}", "sqb"], ["sqb"], out=t2, in0=x2_, in1=sin_, op=ALU.mult)
            DVE("tensor_tensor", rd + [f"cs{b_}", "sqb"], ["sqb"], out=t3, in0=x2_, in1=cos_, op=ALU.mult)
            DVE("tensor_tensor", rd + [f"cs{b_}", "sqb"], ["sqb"], out=t4, in0=x1_, in1=sin_, op=ALU.mult)
            DVE("tensor_sub", ["sqb"], wr, out=dst[:, :, 0:32], in0=t1, in1=t2)
            DVE("tensor_add", ["sqb"] + wr, wr, out=dst[:, :, 32:64], in0=t3, in1=t4)

        qpan = get_panels(3)
        for s_ in range(NB):
            for i in range(3):
                pan, pk = qpan[i]
                b = nbank()
                MM(banks[b][:, :], [(hT[:, kc, s_ * 128:(s_ + 1) * 128], pan[:, kc, :]) for kc in range(KC)], HTK + [pk], pkeys(b))
                if i % 2 == 0:
                    ACT(rawb[:, i * 512:(i + 1) * 512], banks[b][:, :], AF.Copy, pkeys(b), [f"raw{i}"])
                else:
                    DVE("tensor_copy", pkeys(b), [f"raw{i}"], out=rawb[:, i * 512:(i + 1) * 512], in_=banks[b][:, :])
            RAW = ["raw0", "raw1", "raw2", "raw3"]
            ACT(sqb[:, 0:1536], rawb[:, 0:1536], AF.Square, RAW, ["sqb"])
            sq3 = sqb[:, 0:1536].rearrange("p (h d) -> p h d", h=H)
            raw3 = rawb[:, 0:1536].rearrange("p (h d) -> p h d", h=H)
            DVE("tensor_reduce", ["sqb"], ["stat16"], out=stat[:, 16:24], in_=sq3[:, :, 0:128], axis=AX.X, op=ALU.add)
            DVE("tensor_reduce", ["sqb"], ["stat24"], out=stat[:, 24:32], in_=sq3[:, :, 128:192], axis=AX.X, op=ALU.add)
            rsqrt_cols(stat[:, 16:24], stat[:, 32:40], 8, 1.0 / 128, EPS, ["stat16"], ["stat32"])
            rsqrt_cols(stat[:, 24:32], stat[:, 40:48], 8, 1.0 / 64, EPS, ["stat24"], ["stat40"])
            DVE("tensor_tensor", RAW + ["stat32", "sqb"], ["sqb"], out=sqb[:, 0:1024].rearrange("p (h d) -> p h d", h=H),
                in0=raw3[:, :, 0:128], in1=stat[:, 32:40].unsqueeze(2).broadcast_to([128, H, 128]), op=ALU.mult)
            DVE("tensor_tensor", ["sqb", "bc_q_nope_norm"], ["qnb"], out=qnb, in0=sqb[:, 0:1024].rearrange("p (h d) -> p h d", h=H),
                in1=bc_qn[:, :].unsqueeze(1).broadcast_to([128, H, 128]), op=ALU.mult)
            DVE("tensor_tensor", RAW + ["stat40", "sqb"], ["sqb"], out=sqb[:, 1024:1536].rearrange("p (h d) -> p h d", h=H),
                in0=raw3[:, :, 128:192], in1=stat[:, 40:48].unsqueeze(2).broadcast_to([128, H, 64]), op=ALU.mult)
            DVE("tensor_tensor", ["sqb", "bc_q_rope_norm"], ["qrb"], out=qrb, in0=sqb[:, 1024:1536].rearrange("p (h d) -> p h d", h=H),
                in1=bc_qr[:, :].unsqueeze(1).broadcast_to([128, H, 64]), op=ALU.mult)
            rope(qrr, qrb, H, s_, ["qrb", "sqb", "sqb"], ["qrr"])
            b = nbank(); pb = banks[b][:].bitcast(BF16)
            TRS([(pb[:, h * 128:(h + 1) * 128], qnb[:, h, :]) for h in range(H)], ident_b[:], ["qnb", "ident_b"], pkeys(b))
            ACT(qnT[:, :, s_ * 128:(s_ + 1) * 128], pb[:, :].rearrange("p (h t) -> p h t", h=H), AF.Copy, pkeys(b), ["qnT"])
            b = nbank(); pb = banks[b][:].bitcast(BF16)
            TRS([(pb[0:64, h * 128:(h + 1) * 128], qrr[:, h, :]) for h in range(H)], ident_b[:], ["qrr", "ident_b"], pkeys(b))
            DVE("tensor_copy", pkeys(b), ["qrT"], out=qrT[0:64, :, s_ * 128:(s_ + 1) * 128],
                in_=pb[0:64, :].rearrange("p (h t) -> p h t", h=H))

        (cpan, cpk), (mpan, mpk), (kvpan, kvpk) = get_panels(3)
        for s_ in range(NB):
            blk = tt * NB + s_
            tsl = slice(t0 + s_ * 128, t0 + (s_ + 1) * 128)
            b = nbank()
            MM(banks[b][:, 0:80], [(hT[:, kc, s_ * 128:(s_ + 1) * 128], mpan[:, kc, 0:80]) for kc in range(KC)], HTK + [mpk], pkeys(b))
            ACT(miscb[:, :], banks[b][:, 0:80], AF.Copy, pkeys(b), ["miscb"])
            ACT(stat[:, 48:56], miscb[:, 0:8], AF.Tanh, ["miscb"], ["tb"], scale=0.5)
            DVE("tensor_scalar", ["tb"], [f"beta{s_}"], arena=False, out=betas[:, s_, :], in0=stat[:, 48:56], scalar1=0.5, scalar2=0.5,
                op0=ALU.mult, op1=ALU.add)
            DVE("tensor_scalar_mul", [f"beta{s_}"], [f"hbeta{s_}"], arena=False, out=hbeta[:, s_, :], in0=betas[:, s_, :], scalar1=0.5)
            DVE("tensor_add", ["miscb", "bc_dt_bias"], ["ga"], out=stat[:, 56:64], in0=miscb[:, 8:16], in1=bc_dtb[:, :])
            ACT(stat[:, 56:64], stat[:, 56:64], AF.Exp, ["ga"], ["ga"])
            ACT(stat[:, 56:64], stat[:, 56:64], AF.Ln, ["ga"], ["ga"], bias=1.0)
            DVE("tensor_mul", ["ga", "negA"], [f"g{s_}"], arena=False, out=gts[:, s_, :], in0=stat[:, 56:64], in1=negA[:, :])
            b2 = nbank()
            MM(banks[b2][:, 0:8], [(Umask[:, :], gts[:, s_, :])], [f"g{s_}", "Umask"], pkeys(b2, 0), arena=False)
            MM(banks[b2][:, 8:16], [(SUmask[:, :], gts[:, s_, :])], [f"g{s_}", "SUmask"], pkeys(b2, 0), arena=False)
            MM(banks[b2][:, 16:24], [(ones_f[:, :], gts[:, s_, :])], [f"g{s_}", "ones_f"], pkeys(b2, 0), arena=False)
            ACT(eg[:, s_, :], banks[b2][:, 0:24], AF.Exp, pkeys(b2, 0), [f"eg{s_}"], arena=False)
            DVE("tensor_mul", [f"eg{s_}", f"beta{s_}"], [f"bge{s_}"], arena=False, out=bge[:, s_, :], in0=eg[:, s_, 0:8], in1=betas[:, s_, :])
            ACT(sqb[:, 1536:1600], miscb[:, 16:80], AF.Square, ["miscb"], ["sqb", "st_kr"], accum_out=stat[:, 15:16])
            rsqrt_cols(stat[:, 15:16], stat[:, 14:15], 1, 1.0 / 64, EPS, ["st_kr"], ["rs_kr"])
            DVE("scalar_tensor_tensor", ["miscb", "rs_kr", "bc_k_rope_norm"], ["krn"], out=krn[:, :], in0=miscb[:, 16:80],
                scalar=stat[:, 14:15], in1=bc_kr[:, :], op0=ALU.mult, op1=ALU.mult)
            rope(krr.rearrange("p (h d) -> p h d", h=1), krn.rearrange("p (h d) -> p h d", h=1), 1, s_, ["krn"], ["krr"])
            b3 = nbank(); pb = banks[b3][:].bitcast(BF16)
            TR(pb[0:64, 0:128], krr[:, :], ident_b[:], ["krr", "ident_b"], pkeys(b3))
            ACT(krT[0:64, tsl], pb[0:64, 0:128], AF.Copy, pkeys(b3), ["krT"])
            b = nbank()
            MM(banks[b][:, :], [(hT[:, kc, s_ * 128:(s_ + 1) * 128], cpan[:, kc, :]) for kc in range(KC)], HTK + [cpk], pkeys(b))
            ACT(sqb[:, 0:512], banks[b][:, :], AF.Square, pkeys(b), ["sqb", "st_c"], accum_out=stat[:, 13:14])
            rsqrt_cols(stat[:, 13:14], stat[:, 12:13], 1, 1.0 / 512, EPS, ["st_c"], ["rs_c"])
            DVE("scalar_tensor_tensor", pkeys(b) + ["rs_c", "bc_ckv_norm"], ["cn"], out=cn[:, :], in0=banks[b][:, :],
                scalar=stat[:, 12:13], in1=bc_ckv[:, :], op0=ALU.mult, op1=ALU.mult)
            b = nbank(); pb = banks[b][:].bitcast(BF16)
            TRS([(pb[:, c * 128:(c + 1) * 128], cn[:, c * 128:(c + 1) * 128]) for c in range(4)], ident_b[:], ["cn", "ident_b"], pkeys(b))
            ACT(cT, pb[:, 0:512].rearrange("p (c t) -> p c t", c=4), AF.Copy, pkeys(b), ["cT"])
            for i in range(4):
                b = nbank()
                MM(banks[b][:, :], [(cT[:, kc, :], kvpan[:, 4 * i + kc, :]) for kc in range(4)], ["cT", kvpk], pkeys(b))
                if i % 2 == 0:
                    ACT(rawb[:, i * 512:(i + 1) * 512], banks[b][:, :], AF.Copy, pkeys(b), [f"raw{i}"])
                else:
                    DVE("tensor_copy", pkeys(b), [f"raw{i}"], out=rawb[:, i * 512:(i + 1) * 512], in_=banks[b][:, :])
            RAW = ["raw0", "raw1", "raw2", "raw3"]
            kv3 = rawb[:, :].rearrange("p (h d) -> p h d", h=H)
            ACT(sqb[:, :], rawb[:, :], AF.Square, RAW, ["sqb"])
            DVE("tensor_reduce", ["sqb"], ["stat16"], out=stat[:, 16:24], in_=sqb[:, :].rearrange("p (h d) -> p h d", h=H)[:, :, 0:128],
                axis=AX.X, op=ALU.add)
            rsqrt_cols(stat[:, 16:24], stat[:, 32:40], 8, 1.0 / 128, EPS, ["stat16"], ["stat32"])
            DVE("tensor_tensor", RAW + ["stat32", "sqb"], ["sqb"], out=sqb[:, 0:1024].rearrange("p (h d) -> p h d", h=H),
                in0=kv3[:, :, 0:128], in1=stat[:, 32:40].unsqueeze(2).broadcast_to([128, H, 128]), op=ALU.mult)
            DVE("tensor_tensor", ["sqb", "bc_k_nope_norm"], ["knb"], out=knb, in0=sqb[:, 0:1024].rearrange("p (h d) -> p h d", h=H),
                in1=bc_kn[:, :].unsqueeze(1).broadcast_to([128, H, 128]), op=ALU.mult)
            ACT(Vc[:, blk, :, :], kv3[:, :, 128:256], AF.Copy, RAW, ["Vc"])
            b = nbank(); pb = banks[b][:].bitcast(BF16)
            TRS([(pb[:, h * 128:(h + 1) * 128], knb[:, h, :]) for h in range(H)], ident_b[:], ["knb", "ident_b"], pkeys(b))
            DVE("tensor_copy", pkeys(b), ["knT"], out=knT[:, :, tsl], in_=pb[:, :].rearrange("p (h t) -> p h t", h=H))

        if STOP == 'M1':
            return
        sc.arena_phase()
        cnt = [0]
        for h in range(H):
            pan, pk = get_panel()
            for j in range(4):
                b = nbank()
                hb = cnt[0] % 2
                pout = banks[b][:, hb * 256:(hb + 1) * 256]
                MM(pout, [(pan[:, kc, j * 128:(j + 1) * 128], hT[:, kc, :]) for kc in range(KC)], HTK + [pk], pkeys(b, hb))
                if j == 3:
                    r = cnt[0] % 2
                    ACT(tnhb[r][:, :], pout, AF.Tanh, pkeys(b, hb), [f"tnh{r}"], scale=0.5)
                    DVE("scalar_tensor_tensor", pkeys(b, hb) + [f"tnh{r}"], ["zT"], out=zT[:, h, :], in0=tnhb[r][:, :], scalar=1.0,
                        in1=pout, op0=ALU.add, op1=ALU.mult)
                    cnt[0] += 1
                    continue
                r = cnt[0] % 2; cnt[0] += 1
                ch = j * 8 + h
                ACT(ubuf[r][:, 3:259], pout, AF.Copy, pkeys(b, hb), [f"ub{r}"])
                DVE("tensor_copy", ["carry", f"ub{r}"], [f"ub{r}"], out=ubuf[r][:, 0:3], in_=carry[:, ch, :])
                DVE("tensor_copy", [f"ub{r}"], ["carry"], out=carry[:, ch, :], in_=ubuf[r][:, 256:259])
                DVE("tensor_scalar", [f"ub{r}", "colsA"], [f"acc{r}"], out=accb[r][:, :], in0=ubuf[r][:, 3:259],
                    scalar1=colsA[:, 3 * 24 + ch:3 * 24 + ch + 1], scalar2=None, op0=ALU.mult)
                for k_ in (2, 1, 0):
                    DVE("scalar_tensor_tensor", [f"ub{r}", "colsA", f"acc{r}"], [f"acc{r}"], out=accb[r][:, :], in0=ubuf[r][:, k_:k_ + 256],
                        scalar=colsA[:, k_ * 24 + ch:k_ * 24 + ch + 1], in1=accb[r][:, :], op0=ALU.mult, op1=ALU.add)
                ACT(tnhb[r][:, :], accb[r][:, :], AF.Tanh, [f"acc{r}"], [f"tnh{r}"], scale=0.5)
                if j == 2:
                    DVE("scalar_tensor_tensor", [f"acc{r}", f"tnh{r}"], ["vTb"], out=vTb[:, h, :], in0=tnhb[r][:, :], scalar=1.0,
                        in1=accb[r][:, :], op0=ALU.add, op1=ALU.mult)
                    b2 = nbank(); pb = banks[b2][:].bitcast(BF16)
                    TRS([(pb[:, s_ * 128:(s_ + 1) * 128], vTb[:, h, s_ * 128:(s_ + 1) * 128]) for s_ in range(NB)], ident_b[:],
                        ["vTb", "ident_b"], pkeys(b2, 0))
                    for s_ in range(NB):
                        ACT(vbt[:, s_, h, :], pb[:, s_ * 128:(s_ + 1) * 128], AF.Copy, pkeys(b2, 0) + [f"hbeta{s_}"], ["vbt"],
                            scale=hbeta[:, s_, h:h + 1])
                    continue
                DVE("scalar_tensor_tensor", [f"acc{r}", f"tnh{r}"], [f"s2{r}"], out=s2b[r][:, :], in0=tnhb[r][:, :], scalar=1.0,
                    in1=accb[r][:, :], op0=ALU.add, op1=ALU.mult)
                ACT(sq2b[r][:, :], s2b[r][:, :], AF.Square, [f"s2{r}"], [f"sq2{r}"])
                b2 = nbank()
                MM(banks[b2][:, 0:256], [(ones_f[:, :], sq2b[r][:, :])], [f"sq2{r}", "ones_f"], pkeys(b2, 0))
                if j == 0:
                    ACT(rstb[r][:, :], banks[b2][:, 0:256], AF.Sqrt, pkeys(b2, 0), [f"rst{r}"], scale=128.0, bias=512.0 * EPS)
                else:
                    ACT(rstb[r][:, :], banks[b2][:, 0:256], AF.Sqrt, pkeys(b2, 0), [f"rst{r}"], scale=1.0, bias=4.0 * EPS)
                DVE("reciprocal", [f"rst{r}"], [f"rst{r}"], out=rstb[r][:, :], in_=rstb[r][:, :])
                dstT = qT if j == 0 else kT
                DVE("tensor_mul", [f"s2{r}", f"rst{r}"], ["qT" if j == 0 else "kT"], out=dstT[:, h, :], in0=s2b[r][:, :], in1=rstb[r][:, :])
                if j == 1:
                    b3 = nbank(); pb = banks[b3][:].bitcast(BF16)
                    TRS([(pb[:, s_ * 128:(s_ + 1) * 128], kT[:, h, s_ * 128:(s_ + 1) * 128]) for s_ in range(NB)], ident_b[:],
                        ["kT", "ident_b"], pkeys(b3, 0))
                    for s_ in range(NB):
                        ACT(kbd[:, s_, h, :], pb[:, s_ * 128:(s_ + 1) * 128], AF.Copy, pkeys(b3, 0) + [f"bge{s_}"], ["kbd"],
                            scale=bge[:, s_, h:h + 1])
                        DVE("tensor_scalar", pkeys(b3, 0) + [f"eg{s_}"], ["kdec"], out=kdec[:, s_, h, :], in0=pb[:, s_ * 128:(s_ + 1) * 128],
                            scalar1=eg[:, s_, 8 + h:9 + h], scalar2=None, op0=ALU.mult)

        if STOP == 'M2':
            return
        sc.arena_phase()
        for s_ in range(NB):
            bs = slice(s_ * 128, (s_ + 1) * 128)
            for hg in range(2):
                hs = [hg * 4 + i for i in range(4)]
                for i, h in enumerate(hs):
                    DVE("tensor_scalar", ["Umask", f"g{s_}", "gUb"], ["gUb"], out=gUb[:, i, :], in0=Umask[:, :], scalar1=gts[:, s_, h:h + 1],
                        scalar2=None, op0=ALU.mult)
                bD = nbank()
                for i, h in enumerate(hs):
                    MM(banks[bD][:, i * 128:(i + 1) * 128], [(gUb[:, i, :], ones_f[:, :]), (negones_f[:, :], gUb[:, i, :]), (ident_f[:, :], maskb[:, :])],
                       ["gUb", "ones_f", "negones_f", "ident_f", "maskb"], pkeys(bD))
                ACT(decb, banks[bD][:, :].rearrange("p (h d) -> p h d", h=4), AF.Exp, pkeys(bD), ["decb"])
                bK = nbank(); bQ = nbank()
                for i, h in enumerate(hs):
                    MM(banks[bK][:, i * 128:(i + 1) * 128], [(kT[:, h, bs], kT[:, h, bs])], ["kT"], pkeys(bK))
                    MM(banks[bQ][:, i * 128:(i + 1) * 128], [(qT[:, h, bs], kT[:, h, bs])], ["kT", "qT"], pkeys(bQ))
                for i, h in enumerate(hs):
                    DVE("scalar_tensor_tensor", pkeys(bK) + ["decb", f"beta{s_}", "A0"], ["A0"], out=Ab[0][:, i, :], in0=banks[bK][:, i * 128:(i + 1) * 128],
                        scalar=betas[:, s_, h:h + 1], in1=decb[:, i, :], op0=ALU.mult, op1=ALU.mult)
                DVE("tensor_tensor", ["A0", "SUmask"], ["A0"], out=Ab[0], in0=Ab[0], in1=SLmask[:, :].unsqueeze(1).broadcast_to([128, 4, 128]), op=ALU.mult)
                DVE("tensor_tensor", pkeys(bQ) + ["decb"], ["qkb"], out=qkb, in0=banks[bQ][:, :].rearrange("p (h d) -> p h d", h=4), in1=decb, op=ALU.mult)
                bT = nbank()
                for i in range(4):
                    MM(banks[bT][:, i * 128:(i + 1) * 128], [(Ab[0][:, i, :], ident_f[:, :])], ["A0", "ident_f"], pkeys(bT))
                ACT(Mb[0], banks[bT][:, :].rearrange("p (h d) -> p h d", h=4), AF.Copy, pkeys(bT), ["M0"])
                bT2 = nbank(); pbT = banks[bT2][:].bitcast(BF16)
                TRS([(pbT[:, i * 128:(i + 1) * 128], qkb[:, i, :]) for i in range(4)], ident_b[:], ["qkb", "ident_b"], pkeys(bT2))
                DVE("tensor_copy", pkeys(bT2), ["qkTb"], out=qkTb, in_=pbT[:, 0:512].rearrange("p (h d) -> p h d", h=4))
                DVE("tensor_tensor", ["M0", "ident_f"], ["Q0"], out=Qb[0], in0=ident_f[:, :].unsqueeze(1).broadcast_to([128, 4, 128]), in1=Mb[0], op=ALU.subtract)
                cur = 0
                for k_ in range(1, 7):
                    nx = 1 - cur
                    bA = nbank()
                    for i in range(4):
                        MM(banks[bA][:, i * 128:(i + 1) * 128], [(Mb[cur][:, i, :], Ab[cur][:, i, :])], [f"M{cur}", f"A{cur}"], pkeys(bA))
                    ACT(Ab[nx], banks[bA][:, :].rearrange("p (h d) -> p h d", h=4), AF.Copy, pkeys(bA), [f"A{nx}"])
                    if k_ < 6:
                        bM = nbank()
                        for i in range(4):
                            MM(banks[bM][:, i * 128:(i + 1) * 128], [(Ab[cur][:, i, :], Mb[cur][:, i, :])], [f"M{cur}", f"A{cur}"], pkeys(bM))
                        DVE("tensor_copy", pkeys(bM), [f"M{nx}"], out=Mb[nx], in_=banks[bM][:, :].rearrange("p (h d) -> p h d", h=4))
                    bQ2 = nbank()
                    for i in range(4):
                        MM(banks[bQ2][:, i * 128:(i + 1) * 128], [(ident_f[:, :], Qb[cur][:, i, :]), (Ab[nx][:, i, :], Qb[cur][:, i, :])],
                           [f"A{nx}", f"Q{cur}", "ident_f"], pkeys(bQ2))
                    if k_ < 6:
                        DVE("tensor_copy", pkeys(bQ2), [f"Q{nx}"], out=Qb[nx], in_=banks[bQ2][:, :].rearrange("p (h d) -> p h d", h=4))
                    else:
                        DVE("tensor_copy", pkeys(bQ2), ["Qfin"], out=Qfin, in_=banks[bQ2][:, :].rearrange("p (h d) -> p h d", h=4))
                    cur = nx
                TT_ = Qfin; TK = "Qfin"
                bW = nbank(); bU = nbank()
                for i, h in enumerate(hs):
                    MM(banks[bW][:, i * 128:(i + 1) * 128], [(kbd[:, s_, h, :], TT_[:, i, :])], ["kbd", TK], pkeys(bW))
                    MM(banks[bU][:, i * 128:(i + 1) * 128], [(TT_[:, i, :], vbt[:, s_, h, :])], ["vbt", TK], pkeys(bU))
                ACT(wTb, banks[bW][:, :].rearrange("p (h d) -> p h d", h=4), AF.Copy, pkeys(bW), ["wTb"])
                ACT(ub, banks[bU][:, :].rearrange("p (h d) -> p h d", h=4), AF.Copy, pkeys(bU), ["ub"])
                bV = nbank()
                for i, h in enumerate(hs):
                    MM(banks[bV][:, i * 128:(i + 1) * 128], [(wTb[:, i, :], St_b[:, h, :])], ["wTb", f"Stb{hg}"], pkeys(bV))
                DVE("tensor_tensor", pkeys(bV) + ["ub"], ["vnb"], out=vnb, in0=ub, in1=banks[bV][:, :].rearrange("p (h d) -> p h d", h=4), op=ALU.subtract)
                bO1 = nbank(); bO2 = nbank()
                for i, h in enumerate(hs):
                    MM(banks[bO1][:, i * 128:(i + 1) * 128], [(qT[:, h, bs], St_b[:, h, :])], ["qT", f"Stb{hg}"], pkeys(bO1))
                    MM(banks[bO2][:, i * 128:(i + 1) * 128], [(qkTb[:, i, :], vnb[:, i, :])], ["qkTb", "vnb"], pkeys(bO2))
                ACT(o2b, banks[bO2][:, :].rearrange("p (h d) -> p h d", h=4), AF.Copy, pkeys(bO2), ["o2b"])
                for i, h in enumerate(hs):
                    DVE("scalar_tensor_tensor", pkeys(bO1) + ["o2b", f"eg{s_}", "ob"], ["ob"], out=ob[:, i, :], in0=banks[bO1][:, i * 128:(i + 1) * 128],
                        scalar=eg[:, s_, h:h + 1], in1=o2b[:, i, :], op0=ALU.mult, op1=ALU.add)
                bS = nbank()
                for i, h in enumerate(hs):
                    MM(banks[bS][:, i * 128:(i + 1) * 128], [(kdec[:, s_, h, :], vnb[:, i, :])], ["kdec", "vnb"], pkeys(bS))
                for i, h in enumerate(hs):
                    DVE("scalar_tensor_tensor", pkeys(bS) + [f"eg{s_}", f"St{hg}"], [f"St{hg}"], arena=False, out=St[:, h, :], in0=St[:, h, :],
                        scalar=eg[:, s_, 16 + h:17 + h], in1=banks[bS][:, i * 128:(i + 1) * 128], op0=ALU.mult, op1=ALU.add)
                ACT(St_b[:, hg * 4:hg * 4 + 4, :], St[:, hg * 4:hg * 4 + 4, :], AF.Copy, [f"St{hg}"], [f"Stb{hg}"], arena=False)
                ACT(o2b, ob, AF.Square, ["ob", "o2b"], ["o2b"])
                DVE("tensor_reduce", ["o2b"], ["stat16"], out=stat[:, 16:20], in_=o2b, axis=AX.X, op=ALU.add)
                rsqrt_cols(stat[:, 16:20], stat[:, 32:36], 4, 1.0 / 128, EPS, ["stat16"], ["stat32"])
                DVE("tensor_tensor", ["ob", "stat32"], ["onb"], out=onb, in0=ob, in1=stat[:, 32:36].unsqueeze(2).broadcast_to([128, 4, 128]), op=ALU.mult)
                bN = nbank(); pbN = banks[bN][:].bitcast(BF16)
                TRS([(pbN[:, i * 128:(i + 1) * 128], onb[:, i, :]) for i in range(4)], ident_b[:], ["onb", "ident_b"], pkeys(bN, 0))
                for i, h in enumerate(hs):
                    DVE("scalar_tensor_tensor", pkeys(bN, 0) + ["zT", "half_dno"], ["oaT"], out=oaT[:, h, bs], in0=pbN[:, i * 128:(i + 1) * 128],
                        scalar=half_dno[:, 0:1], in1=zT[:, h, bs], op0=ALU.mult, op1=ALU.mult)

        if STOP == 'B':
            return
        SCALE = 192.0 ** -0.5
        nkb = tt * NB + NB
        pcnt = [0]
        for h in range(H):
            OT = banks[6][:, 0:256]; RT = banks[7][:, 0:256]
            for kb in range(nkb):
                m = kb - tt * NB
                q0 = 0 if m <= 0 else m * 128
                nq = T - q0
                ksl = slice(kb * 128, (kb + 1) * 128)
                b = nbank()
                MM(banks[b][:, 0:nq], [(knT[:, h, ksl], qnT[:, h, q0:T]), (krT[0:64, ksl], qrT[0:64, h, q0:T])],
                   ["knT", "krT", "qnT", "qrT"], pkeys(b, 0))
                pi = pcnt[0] % 3; pcnt[0] += 1
                ACT(Pt[pi][:, 0:nq], banks[b][:, 0:nq], AF.Exp, pkeys(b, 0), [f"Pt{pi}"], arena=False, scale=SCALE)
                if m >= 0:
                    DVE("memset", [f"Pt{pi}"], [f"Pt{pi}"], arena=False, ap=Pt[pi][64:128, 0:64], constant=0.0)
                first = (kb == 0)
                last = (kb == nkb - 1)

                def fn(e, pi=pi, h=h, kb=kb, q0=q0, nq=nq, first=first, last=last, OT=OT, RT=RT):
                    e.matmul(OT[:, q0:T], lhsT=Vc[:, kb, h, :], rhs=Pt[pi][:, 0:nq], start=first, stop=last)
                    return e.matmul(RT[:, q0:T], lhsT=ones_b[:, :], rhs=Pt[pi][:, 0:nq], start=first, stop=last)
                sc.add("pe", fn, [f"Pt{pi}", "Vc", "ones_b"], pkeys(6) + pkeys(7), arena=False)
            DVE("reciprocal", pkeys(7), ["rinv"], arena=True, out=rinvt[:, :], in_=RT)
            DVE("tensor_tensor", pkeys(6) + ["rinv"], ["obT"], arena=True, out=obT[:, h, :], in0=OT, in1=rinvt[:, :], op=ALU.mult)

        if STOP == 'C':
            return
        sc.arena_phase()
        for s_ in range(NB):
            sc.add("sp", (lambda s_: lambda e: [e.dma_start(out=x1[:, s_, :], in_=x_d[q, t0 + s_ * 128:t0 + (s_ + 1) * 128, :])])(s_),
                   [], [f"x1_{s_}"], slot=f"x1_{s_}", arena=True)
        for g in range(4):
            (wab, wabk), (gap, gapk), (gbp, gbpk) = get_panels(3)
            for cc in range(4):
                c = g * 4 + cc
                csl = slice(cc * 128, (cc + 1) * 128)
                bA = nbank(); bB = nbank()
                MM(banks[bA][:, 0:256], [(wab[:, kc, csl], oaT[:, kc, :]) for kc in range(8)], ["oaT", wabk], pkeys(bA, 0), arena=False)
                MM(banks[bA][:, 256:512], [(wab[:, 8 + kc, csl], obT[:, kc, :]) for kc in range(8)], ["obT", wabk], pkeys(bA, 1), arena=True)
                MM(banks[bB][:, 0:256], [(gap[:, kc, csl], hT[:, kc, :]) for kc in range(KC)], HTK + [gapk], pkeys(bB, 0), arena=False)
                MM(banks[bB][:, 256:512], [(gbp[:, kc, csl], hT[:, kc, :]) for kc in range(KC)], HTK + [gbpk], pkeys(bB, 1), arena=False)
                ACT(gtb[0][:, :], banks[bB][:, 0:256], AF.Tanh, pkeys(bB, 0), ["gt0"], scale=0.5)
                ACT(gtb[1][:, :], banks[bB][:, 256:512], AF.Tanh, pkeys(bB, 1), ["gt1"], scale=0.5)
                DVE("scalar_tensor_tensor", pkeys(bA, 0) + ["gt0", "gt2"], ["gt2"], out=gtb[2][:, :], in0=gtb[0][:, :], scalar=1.0,
                    in1=banks[bA][:, 0:256], op0=ALU.add, op1=ALU.mult)
                DVE("scalar_tensor_tensor", pkeys(bA, 1) + ["gt1", "gt3"], ["gt3"], out=gtb[3][:, :], in0=gtb[1][:, :], scalar=1.0,
                    in1=banks[bA][:, 256:512], op0=ALU.add, op1=ALU.mult)
                DVE("tensor_add", ["gt2", "gt3"], [f"yT{c}"], out=yT[:, c, :], in0=gtb[2][:, :], in1=gtb[3][:, :])
        YK = [f"yT{c}" for c in range(KC)]
        for g in range(4):
            pan, pk = get_panel()
            for s_ in range(NB):
                b = nbank()
                MM(banks[b][:, :], [(yT[:, kc, s_ * 128:(s_ + 1) * 128], pan[:, kc, :]) for kc in range(KC)], YK + [pk], pkeys(b))
                DVE("scalar_tensor_tensor", pkeys(b) + [f"x1_{s_}"], [f"x1_{s_}"], out=x1[:, s_, g * 512:(g + 1) * 512], in0=banks[b][:, :], scalar=0.5,
                    in1=x1[:, s_, g * 512:(g + 1) * 512], op0=ALU.mult, op1=ALU.add)

        if STOP == 'D':
            return
        sc.arena_phase()
        norm_to_hT(x1, ["x1_0", "x1_1"], (colsA, "colsA"), 112, xs, "n2")
        rc = [0]
        for hf in range(2):
            for i in range(8):
                pan, pk = get_panel()
                for cc in range(4):
                    c = i * 4 + cc
                    b = nbank()
                    hb = cc % 2
                    MM(banks[b][:, hb * 256:(hb + 1) * 256], [(pan[:, kc, cc * 128:(cc + 1) * 128], hT[:, kc, :]) for kc in range(KC)], HTK + [pk], pkeys(b, hb))
                    r = rc[0] % 2; rc[0] += 1
                    DVE("tensor_scalar_max", pkeys(b, hb) + [f"rl{r}"], [f"rl{r}"], out=rlb[r][:, :], in0=banks[b][:, hb * 256:(hb + 1) * 256], scalar1=0.0)
                    ACT(uT[:, c, :], rlb[r][:, :], AF.Square, [f"rl{r}"], [f"uT{c}"])
            UK = [f"uT{c}" for c in range(32)]
            for cb in range(4):
                bs2 = [nbank() for _ in range(NB)]
                for sp_ in range(2):
                    pan, pk = get_panel()
                    for s_ in range(NB):
                        def fn(e, pan=pan, s_=s_, sp_=sp_, bb=bs2[s_]):
                            r_ = None
                            for kc in range(KC):
                                r_ = e.matmul(banks[bb][:, :], lhsT=uT[:, sp_ * 16 + kc, s_ * 128:(s_ + 1) * 128], rhs=pan[:, kc, :],
                                              start=(sp_ == 0 and kc == 0), stop=(sp_ == 1 and kc == KC - 1))
                            return r_
                        sc.add("pe", fn, UK + [pk], pkeys(bs2[s_]), arena=True)
                for s_ in range(NB):
                    DVE("tensor_add", pkeys(bs2[s_]) + [f"x1_{s_}"], [f"x1_{s_}"], out=x1[:, s_, cb * 512:(cb + 1) * 512], in0=banks[bs2[s_]][:, :],
                        in1=x1[:, s_, cb * 512:(cb + 1) * 512])

        if STOP == 'E':
            return
        norm_to_hT(x1, ["x1_0", "x1_1"], (colsB, "colsB"), 0, xs, "n3")
        sc.add("sp", lambda e: [e.dma_start(out=pst[:, s_, :], in_=p_d[q, t0 + s_ * 128:t0 + (s_ + 1) * 128, :]) for s_ in range(NB)],
               [], ["pst"], slot="pst", ndma=NB, arena=True)
        DVE("tensor_copy", ["pst"], ["pbf"], out=pbf, in_=pst)
        b = nbank(); pb = banks[b][:].bitcast(BF16)
        TRS([(pb[:, c * 256 + s_ * 128:c * 256 + (s_ + 1) * 128], pbf[:, s_, c * 128:(c + 1) * 128]) for c in range(2) for s_ in range(NB)],
            ident_b[:], ["pbf", "ident_b"], pkeys(b, 0))
        ACT(pT, pb[:, 0:512].rearrange("p (c t) -> p c t", c=2), AF.Copy, pkeys(b, 0), ["pT"])
        wpl, wplk = wple_sb, "wple_sb"
        sc.add("pool", lambda e: [e.dma_start(out=wple_sb[:, 2 * i:2 * i + 2, :], in_=wsrc(w_ple, D, 0, 2, 512 * i, 512)) for i in range(4)],
               [], ["wple_sb"] + [f"yT{c}" for c in range(KC)], slot="wple", ndma=4, arena=True)
        for g in range(4):
            pan, pk = get_panel()
            for s_ in range(NB):
                bG = nbank(); bE = nbank()
                MM(banks[bG][:, :], [(hT[:, kc, s_ * 128:(s_ + 1) * 128], pan[:, kc, :]) for kc in range(KC)], HTK + [pk], pkeys(bG))
                MM(banks[bE][:, :], [(pT[:, kc, s_ * 128:(s_ + 1) * 128], wpl[:, 2 * g + kc, :]) for kc in range(2)], ["pT", wplk], pkeys(bE))
                gt_ = af(SCR + 6144, 512); gt2_ = af(SCR + 6656, 512)
                ACT(gt_, banks[bG][:, :], AF.Tanh, pkeys(bG) + ["gt0", "gt1"], ["gt0", "gt1"], scale=0.5)
                DVE("scalar_tensor_tensor", pkeys(bE) + ["gt0", "gt1", "gt2", "gt3"], ["gt2", "gt3"], out=gt2_, in0=gt_, scalar=1.0,
                    in1=banks[bE][:, :], op0=ALU.add, op1=ALU.mult)
                DVE("scalar_tensor_tensor", ["gt2", "gt3", f"x1_{s_}"], [f"x1_{s_}"], out=x1[:, s_, g * 512:(g + 1) * 512], in0=gt2_, scalar=0.5,
                    in1=x1[:, s_, g * 512:(g + 1) * 512], op0=ALU.mult, op1=ALU.add)
        for s_ in range(NB):
            sc.add("sp", (lambda s_: lambda e: [e.dma_start(out=out_d[q, t0 + s_ * 128:t0 + (s_ + 1) * 128, :], in_=x1[:, s_, :])])(s_),
                   [f"x1_{s_}"], [], slot=f"out{s_}", arena=True)

    setup()
    for q in range(NSEQ if STOP not in ('S0', 'S1') else 0):
        sc.add("dve", lambda e: e.memset(St[:].rearrange("p h d -> p (h d)"), 0.0), [], ["St0", "St1"], arena=False)
        sc.add("dve", lambda e: e.memset(St_b[:].rearrange("p h d -> p (h d)"), 0.0), [], ["Stb0", "Stb1"], arena=False)
        sc.add("dve", lambda e: e.memset(carry[:].rearrange("p c k -> p (c k)"), 0.0), [], ["carry"], arena=False)
        for tt in range(NTT):
            panel_specs.extend(tile_panel_list())
            tile_body(q, tt)
    assert STOP or pan_ptr[0] == len(panel_specs), (pan_ptr[0], len(panel_specs))
    sc.emit(nc, es, final_slots=["out0", "out1"])
    es.close()
    return nc


_CACHE = {}


def _inv_freq():
    half = 32
    return (10000.0 ** (-np.arange(half, dtype=np.float32) / half)).astype(np.float32)


def _maps(inputs, NSEQ, ncores):
    x = np.asarray(inputs["x"]); p = np.asarray(inputs["p"])[0]; pos = np.asarray(inputs["positions"]).astype(np.int32)
    shared = {}
    for nm in ("w_in", "conv_w", "w_kv_up", "w_branch_a", "w_branch_b", "w_out", "w_mlp_up", "w_mlp_down", "w_ple_gate", "w_ple",
               "mix_norm", "mlp_norm", "ple_norm", "dt_bias", "a_log", "dn_out_norm", "ckv_norm", "q_nope_norm", "q_rope_norm",
               "k_nope_norm", "k_rope_norm"):
        a = np.ascontiguousarray(np.asarray(inputs[nm])[0], dtype=np.float32)
        shared[nm] = a
    shared["inv_freq"] = _inv_freq()
    maps = []
    for c in range(ncores):
        m = dict(shared)
        m["x"] = np.ascontiguousarray(x[c * NSEQ:(c + 1) * NSEQ], dtype=np.float32)
        m["p"] = np.ascontiguousarray(p[c * NSEQ:(c + 1) * NSEQ], dtype=np.float32)
        m["positions"] = np.ascontiguousarray(pos[c * NSEQ:(c + 1) * NSEQ])
        maps.append(m)
    return maps


def kernel(**inputs):
    x = np.asarray(inputs["x"])
    Bt, S, _ = x.shape
    ncores = 8 if Bt % 8 == 0 else 1
    NSEQ = Bt // ncores
    key = (NSEQ, S)
    if key not in _CACHE:
        _CACHE[key] = build(NSEQ, S)
    nc = _CACHE[key]
    res = run_bass_kernel_spmd(nc, _maps(inputs, NSEQ, ncores), core_ids=list(range(ncores)))
    return np.concatenate([r["out"] for r in res.results], axis=0).astype(np.float32)
```

```python
import math
from contextlib import ExitStack
import numpy as np
import concourse.bass as bass
import concourse.mybir as mybir
from concourse.bass_utils import run_bass_kernel_spmd

F32 = mybir.dt.float32
BF16 = mybir.dt.bfloat16
I32 = mybir.dt.int32
AF = mybir.ActivationFunctionType
ALU = mybir.AluOpType
AX = mybir.AxisListType

D = 2048
KC = 16
DIN = 10320
DFF = 8192
H = 8
EPS = 1e-6
T = 256
NB = 2
NEG = -30000.0
STOP = None
TWO_PI = 2.0 * math.pi


class Op:
    __slots__ = ("eng", "fn", "deps", "signal", "pos", "dma", "slot", "ndma", "ticket", "idx")


class Sched:
    ENGS = ("pe", "act", "dve", "pool", "sp")

    def __init__(self):
        self.ops = []
        self.last_w = {}
        self.readers = {}
        self.cnt = {e: 0 for e in self.ENGS}
        self.arena_users = {}
        self.arena_prev = {}

    def arena_phase(self):
        self.arena_prev.update(self.arena_users)
        self.arena_users = {}

    def add(self, eng, fn, reads=(), writes=(), slot=None, ndma=1, arena=False):
        op = Op()
        op.eng = eng; op.fn = fn; op.deps = set(); op.signal = slot is not None
        op.dma = slot is not None; op.slot = slot; op.ndma = ndma; op.ticket = None
        op.pos = self.cnt[eng]; self.cnt[eng] += 1
        op.idx = len(self.ops)
        psr = [k for k in reads if isinstance(k, str) and k.startswith("ps")]
        if psr:
            reads = [k for k in reads if k not in psr]
            writes = list(writes) + psr
        cand = []
        for k in reads:
            w = self.last_w.get(k)
            if w is not None:
                cand.append(w)
        for k in writes:
            w = self.last_w.get(k)
            if w is not None:
                cand.append(w)
            r = self.readers.get(k)
            if r:
                cand.extend(r.values())
        if arena:
            cand.extend(self.arena_prev.values())
        for p in cand:
            if p is op:
                continue
            need = True
            if not p.dma and p.eng == eng:
                if eng == "pe" and not op.dma:
                    need = False
                elif False and op.pos - p.pos > 2:
                    need = False
            if need:
                op.deps.add(p.idx)
                p.signal = True
        for k in reads:
            self.readers.setdefault(k, {})[eng if not op.dma else ("dma", op.idx)] = op
        for k in writes:
            self.last_w[k] = op
            self.readers[k] = {}
        if arena:
            self.arena_users[eng if not op.dma else ("dma", op.slot)] = op
        self.ops.append(op)
        return op

    def emit(self, nc, es, final_slots=()):
        sems = {e: es.enter_context(nc.semaphore("s_" + e)) for e in ("pe", "act", "dve", "pool")}
        slot_sems = {}
        slot_cnt = {}
        eng_cnt = {e: 0 for e in self.ENGS}
        for op in self.ops:
            if op.dma:
                if op.slot not in slot_sems:
                    slot_sems[op.slot] = es.enter_context(nc.semaphore("d_" + str(op.slot)))
                    slot_cnt[op.slot] = 0
                slot_cnt[op.slot] += 16 * op.ndma
                op.ticket = (slot_sems[op.slot], slot_cnt[op.slot])
            elif op.signal:
                eng_cnt[op.eng] += 1
                op.ticket = (sems[op.eng], eng_cnt[op.eng])
        streams = {e: [o for o in self.ops if o.eng == e] for e in self.ENGS}
        ops = self.ops

        def run(e, lst, finals=False):
            waited = {}
            for op in lst:
                need = {}
                for d in op.deps:
                    s, t = ops[d].ticket
                    if need.get(s, (None, 0))[1] < t:
                        need[s] = (s, t)
                for s, t in need.values():
                    if waited.get(s, 0) < t:
                        e.wait_ge(s, t)
                        waited[s] = t
                r = op.fn(e)
                if op.dma:
                    for ins in r:
                        ins.then_inc(op.ticket[0], 16)
                elif op.signal:
                    r.then_inc(op.ticket[0], 1)
            if finals:
                for sl in final_slots:
                    if sl in slot_sems:
                        e.wait_ge(slot_sems[sl], slot_cnt[sl])

        with nc.Block() as block:
            @block.sync
            def _(e):
                run(e, streams["sp"], True)

            @block.gpsimd
            def _(e):
                run(e, streams["pool"])

            @block.tensor
            def _(e):
                run(e, streams["pe"])

            @block.scalar
            def _(e):
                run(e, streams["act"])

            @block.vector
            def _(e):
                run(e, streams["dve"])


def build(NSEQ, S):
    nc = bass.Bass("TRN2", target_bir_lowering=False)
    es = ExitStack()
    sc = Sched()
    NTT = S // T

    def din(name, shape, dt=F32):
        return nc.dram_tensor(name, shape, dt, kind="ExternalInput").ap()

    x_d = din("x", [NSEQ, S, D])
    p_d = din("p", [NSEQ, S, 256])
    pos_d = din("positions", [NSEQ, S], I32)
    w_in = din("w_in", [D, DIN])
    conv_w = din("conv_w", [4, 3072])
    w_kv = din("w_kv_up", [512, 2048])
    w_ba = din("w_branch_a", [1024, D])
    w_bb = din("w_branch_b", [1024, D])
    w_out = din("w_out", [D, D])
    w_up = din("w_mlp_up", [D, DFF])
    w_dn = din("w_mlp_down", [DFF, D])
    w_pg = din("w_ple_gate", [D, D])
    w_ple = din("w_ple", [256, D])
    vec = {}
    for nm, n in (("mix_norm", D), ("mlp_norm", D), ("ple_norm", D), ("dt_bias", 8), ("a_log", 8),
                  ("dn_out_norm", 128), ("ckv_norm", 512), ("q_nope_norm", 128), ("q_rope_norm", 64),
                  ("k_nope_norm", 128), ("k_rope_norm", 64), ("inv_freq", 32)):
        vec[nm] = din(nm, [n])
    out_d = nc.dram_tensor("out", [NSEQ, S, D], F32, kind="ExternalOutput").ap()

    def sb(name, shape, dt=F32):
        return es.enter_context(nc.sbuf_tensor(name, shape, dt))

    def rowsz(t):
        n = 1
        for s_ in t.shape[1:]:
            n *= s_
        return n

    def bview(t, off, dims, np_=128):
        return bass.AP(t, off, [[rowsz(t), np_]] + dims)

    ident_f = sb("ident_f", [128, 128]); ident_b = sb("ident_b", [128, 128], BF16)
    ones_f = sb("ones_f", [128, 128]); negones_f = sb("negones_f", [128, 128]); ones_b = sb("ones_b", [128, 128], BF16)
    Umask = sb("Umask", [128, 128]); SUmask = sb("SUmask", [128, 128]); SLmask = SUmask
    maskb = sb("maskb", [128, 128])
    rows1 = sb("rows1", [128, 128]); rows2 = sb("rows2", [16, 128])
    colsA = sb("colsA", [128, 128]); colsB = sb("colsB", [128, 16])
    bc_dtb = sb("bc_dtb", [128, 8]); bc_alog = sb("bc_alog", [128, 8]); negA = sb("negA", [128, 8])
    col_dno = sb("col_dno", [128, 1]); half_dno = sb("half_dno", [128, 1])
    bc_ckv = sb("bc_ckv", [128, 512]); bc_qn = sb("bc_qn", [128, 128]); bc_qr = sb("bc_qr", [128, 64])
    bc_kn = sb("bc_kn", [128, 128]); bc_kr = sb("bc_kr", [128, 64]); bc_inv = sb("bc_inv", [128, 32])

    knT = sb("knT", [128, H, S], BF16)
    krT = sb("krT", [64, S], BF16)
    Vc = sb("Vc", [128, S // 128, H, 128], BF16)
    St = sb("St", [128, H, 128]); St_b = sb("St_b", [128, H, 128], BF16)
    carry = sb("carry", [128, 24, 3])
    hT = sb("hT", [128, KC, T], BF16)
    oaT = sb("oaT", [128, H, T], BF16)
    NPAN = 3
    pans = [sb(f"pan{i}", [128, KC, 512], BF16) for i in range(NPAN)]
    Pt = [sb(f"Pt{i}", [128, T], BF16) for i in range(3)]
    stat = sb("stat", [128, 64])
    betas = sb("betas", [128, NB, 8]); hbeta = sb("hbeta", [128, NB, 8]); gts = sb("gts", [128, NB, 8])
    eg = sb("eg", [128, NB, 24]); bge = sb("bge", [128, NB, 8])
    cs = sb("cs", [128, NB, 64])
    posi = sb("posi", [128, NB], I32); posf = sb("posf", [128, NB])


    AW = 15360
    arena = sb("arena", [128, AW])

    def af(off, n, shape=None):
        v = arena[:, off:off + n]
        return v if shape is None else v.rearrange(shape[0], **shape[1])

    def ab(off, n, shape=None):
        v = arena[:, off:off + n].bitcast(BF16)
        return v if shape is None else v.rearrange(shape[0], **shape[1])

    HT = ("p (h t) -> p h t", dict(h=H))
    qnT = ab(0, 1024, HT); qrT = ab(1024, 1024, HT)
    qT = ab(2048, 1024, HT); kT = ab(3072, 1024, HT); zT = ab(4096, 1024, HT)
    B4 = ("p (b h d) -> p b h d", dict(b=NB, h=H))
    kbd = ab(5120, 1024, B4); kdec = ab(6144, 1024, B4); vbt = ab(7168, 1024, B4)
    SCR = 8192
    xld = af(SCR, 4096, ("p (s d) -> p s d", dict(s=NB)))
    xs = ab(SCR + 4096, 2048, ("p (s d) -> p s d", dict(s=NB)))
    rawb = af(SCR, 2048); sqb = af(SCR + 2048, 2048)
    cn = ab(SCR + 4096, 256); qnb = ab(SCR + 4352, 512, ("p (h d) -> p h d", dict(h=H)))
    qrb = af(SCR + 4864, 512, ("p (h d) -> p h d", dict(h=H)))
    qrr = ab(SCR + 5376, 256, ("p (h d) -> p h d", dict(h=H)))
    knb = ab(SCR + 5632, 512, ("p (h d) -> p h d", dict(h=H)))
    cT = ab(SCR + 6144, 256, ("p (c t) -> p c t", dict(c=4)))
    miscb = af(SCR + 6400, 80); krn = af(SCR + 6480, 64); krr = ab(SCR + 6544, 32)
    rtmp_t = sb("rtmp_t", [128, 64]); rtmp = rtmp_t[:, :]
    ubuf = [af(SCR + i * 260, 259) for i in range(2)]
    accb = [af(SCR + 520 + i * 256, 256) for i in range(2)]
    tnhb = [af(SCR + 1032 + i * 256, 256) for i in range(2)]
    s2b = [af(SCR + 1544 + i * 256, 256) for i in range(2)]
    sq2b = [af(SCR + 2056 + i * 256, 256) for i in range(2)]
    rstb = [af(SCR + 2568 + i * 256, 256) for i in range(2)]
    vTb = ab(SCR + 3080, 1024, HT)
    G4 = ("p (h d) -> p h d", dict(h=4))
    gUb = af(SCR, 512, G4); decb = af(SCR + 512, 512, G4)
    Ab = [af(SCR + 1024 + i * 512, 512, G4) for i in range(2)]
    Mb = [af(SCR + 2048 + i * 512, 512, G4) for i in range(2)]
    Qb = [af(SCR + 3072 + i * 512, 512, G4) for i in range(2)]
    qkb = ab(SCR + 4096, 256, G4); qkTb = ab(SCR + 4352, 256, G4); wTb = ab(SCR + 4608, 256, G4)
    ub = af(SCR + 4864, 512, G4); vnb = ab(SCR + 5376, 256, G4); o2b = af(SCR + 5632, 512, G4)
    ob = af(SCR + 6144, 512, G4); onb = ab(SCR + 6656, 256, G4); Qfin = ab(SCR + 6912, 256, G4)
    x1 = af(0, 4096, ("p (s d) -> p s d", dict(s=NB)))
    uT = ab(4096, 4096, ("p (c t) -> p c t", dict(c=32)))
    xs2 = xs
    yT = ab(SCR + 6144 - 2048 - 2048, 2048, ("p (c t) -> p c t", dict(c=KC)))
    gtb = [af(SCR + 6144 + i * 256, 256) for i in range(4)]
    pst = af(SCR, 512, ("p (s d) -> p s d", dict(s=NB))); pbf = ab(SCR + 512, 256, ("p (s d) -> p s d", dict(s=NB)))
    pT = ab(SCR + 768, 256, ("p (c t) -> p c t", dict(c=2)))
    rlb = [af(SCR + 1024 + i * 256, 256) for i in range(2)]
    wple_sb = ab(SCR + 2048, 2048, ("p (k c) -> p k c", dict(k=8)))
    rinvt = af(SCR, 256)
    obT = ab(5120, 1024, HT)

    banks = [es.enter_context(nc.psum_tensor(f"bank{i}", [128, 512], F32)) for i in range(8)]
    ring = [0]

    def nbank():
        b = ring[0]; ring[0] = (b + 1) % 6
        return b

    def pkeys(b, half=None):
        return [f"ps{b}"]

    A = dict(arena=True)

    def MM(out, pairs, reads, writes, arena=True):
        n = len(pairs)

        def fn(e):
            r = None
            for i, (l, rr) in enumerate(pairs):
                r = e.matmul(out, lhsT=l, rhs=rr, start=(i == 0), stop=(i == n - 1))
            return r
        return sc.add("pe", fn, reads, writes, arena=arena)

    def TR(out, in_, ident, reads, writes, arena=True):
        return sc.add("pe", lambda e: e.transpose(out, in_, ident), reads, writes, arena=arena)

    def TRS(items, ident, reads, writes, arena=True):
        def fn(e):
            r = None
            for o_, i_ in items:
                r = e.transpose(o_, i_, ident)
            return r
        return sc.add("pe", fn, reads, writes, arena=arena)

    def ACT(out, in_, func, reads, writes, arena=True, **kw):
        return sc.add("act", lambda e: e.activation(out=out, in_=in_, func=func, **kw), reads, writes, arena=arena)

    def DVE(name, reads, writes, arena=True, **kw):
        return sc.add("dve", lambda e: getattr(e, name)(**kw), reads, writes, arena=arena)

    def rsqrt_cols(src, dst, n, scale, bias, reads, writes):
        ACT(rtmp[:, 0:n], src, AF.Sqrt, reads, ["rtmp"], scale=scale, bias=bias)
        DVE("reciprocal", ["rtmp"], writes, out=dst, in_=rtmp[:, 0:n])

    def setup():
        P_ = "pool"
        sc.add(P_, lambda e: e.memset(ident_f[:], 0.0), [], ["ident_f"], arena=False)
        sc.add(P_, lambda e: e.affine_select(out=ident_f[:], in_=ident_f[:], compare_op=ALU.not_equal, fill=1.0,
                                             base=0, pattern=[[-1, 128]], channel_multiplier=1), ["ident_f"], ["ident_f"], arena=False)
        sc.add(P_, lambda e: e.memset(ones_f[:], 1.0), [], ["ones_f"], arena=False)
        sc.add(P_, lambda e: e.memset(negones_f[:], -1.0), [], ["negones_f"], arena=False)
        sc.add(P_, lambda e: e.memset(ones_b[:], 1.0), [], ["ones_b"], arena=False)
        sc.add(P_, lambda e: e.affine_select(out=Umask[:], in_=ones_f[:], compare_op=ALU.is_ge, fill=0.0,
                                             base=0, pattern=[[1, 128]], channel_multiplier=-1), ["ones_f"], ["Umask"], arena=False)
        sc.add(P_, lambda e: e.affine_select(out=SUmask[:], in_=ones_f[:], compare_op=ALU.is_gt, fill=0.0,
                                             base=0, pattern=[[-1, 128]], channel_multiplier=1), ["ones_f"], ["SUmask"], arena=False)
        sc.add(P_, lambda e: e.memset(maskb[:], 0.0), [], ["maskb"], arena=False)
        sc.add(P_, lambda e: e.affine_select(out=maskb[:], in_=maskb[:], compare_op=ALU.is_ge, fill=NEG,
                                             base=0, pattern=[[-1, 128]], channel_multiplier=1), ["maskb"], ["maskb"], arena=False)
        sc.add("dve", lambda e: e.tensor_copy(out=ident_b[:], in_=ident_f[:]), ["ident_f"], ["ident_b"], arena=False)
        cw = conv_w.rearrange("k (c p) -> (k c) p", p=128)
        sc.add("sp", lambda e: [e.dma_start(out=rows1[0:96, :], in_=cw),
                                e.dma_start(out=rows1[96:112, :], in_=vec["mix_norm"].rearrange("(c p) -> c p", p=128)),
                                e.dma_start(out=rows1[112:128, :], in_=vec["mlp_norm"].rearrange("(c p) -> c p", p=128)),
                                e.dma_start(out=rows2[:, :], in_=vec["ple_norm"].rearrange("(c p) -> c p", p=128))],
               [], ["rows"], slot="c_rows", ndma=4, arena=False)

        def bc(nm, t, n):
            return lambda e: [e.dma_start(out=t[:], in_=bass.AP(vec[nm].tensor, 0, [[0, 128], [1, n]]))]
        for nm, t, n in (("dt_bias", bc_dtb, 8), ("a_log", bc_alog, 8), ("ckv_norm", bc_ckv, 512), ("q_nope_norm", bc_qn, 128),
                         ("q_rope_norm", bc_qr, 64), ("k_nope_norm", bc_kn, 128), ("k_rope_norm", bc_kr, 64), ("inv_freq", bc_inv, 32)):
            sc.add("sp", bc(nm, t, n), [], ["bc_" + nm], slot="c_" + nm, arena=False)
        sc.add("sp", lambda e: [e.dma_start(out=col_dno[:], in_=bass.AP(vec["dn_out_norm"].tensor, 0, [[1, 128], [1, 1]]))],
               [], ["col_dno"], slot="c_dno", arena=False)
        if STOP == 'S0':
            return
        b = nbank()
        TR(banks[b][:, 0:128], rows1[:], ident_f[:], ["rows", "ident_f"], pkeys(b), arena=False)
        ACT(colsA[:], banks[b][:, 0:128], AF.Copy, pkeys(b), ["colsA"], arena=False)
        b = nbank()
        TR(banks[b][:, 0:16], rows2[:], ident_f[0:16, 0:16], ["rows", "ident_f"], pkeys(b), arena=False)
        ACT(colsB[:], banks[b][:, 0:16], AF.Copy, pkeys(b), ["colsB"], arena=False)
        ACT(negA[:], bc_alog[:], AF.Exp, ["bc_a_log"], ["negA0"], arena=False)
        DVE("tensor_scalar_mul", ["negA0"], ["negA"], arena=False, out=negA[:], in0=negA[:], scalar1=-1.0)
        DVE("tensor_scalar_mul", ["col_dno"], ["half_dno"], arena=False, out=half_dno[:], in0=col_dno[:], scalar1=0.5)

    panel_specs = []
    pan_ptr = [0]
    pan_issued = [0]

    def wsrc(w_ap, ncols_total, row0, nk, col0, n):
        return bass.AP(w_ap.tensor, row0 * ncols_total + col0, [[ncols_total, 128], [128 * ncols_total, nk], [1, n]])

    def spec_simple(w_ap, ncols_total, row0, nk, col0, n):
        def fn(slot):
            return lambda e: [e.dma_start(out=pans[slot][:, 0:nk, 0:n], in_=wsrc(w_ap, ncols_total, row0, nk, col0, n))]
        return (fn, 1)

    def spec_multi(parts):
        def fn(slot):
            return lambda e: [e.dma_start(out=pans[slot][:, kd:kd + nk, cd:cd + n], in_=wsrc(w, nt, r0, nk, c0, n))
                              for (w, nt, r0, nk, c0, n, kd, cd) in parts]
        return (fn, len(parts))

    def tile_panel_list():
        L = []
        for i in range(3):
            L.append(spec_simple(w_in, DIN, 0, KC, 4112 + 512 * i, 512))
        L.append(spec_simple(w_in, DIN, 0, KC, 5648, 512))
        L.append(spec_multi([(w_in, DIN, 0, KC, 4096, 16, 0, 0), (w_in, DIN, 0, KC, 6160, 64, 0, 16)]))
        L.append(spec_multi([(w_kv, 2048, 0, 4, 512 * i, 512, 4 * i, 0) for i in range(4)]))
        for h in range(H):
            L.append(spec_multi([(w_in, DIN, 0, KC, j * 1024 + h * 128, 128, 0, j * 128) for j in range(4)]))
        for g in range(4):
            L.append(spec_multi([(w_ba, D, 0, 8, g * 512, 512, 0, 0), (w_bb, D, 0, 8, g * 512, 512, 8, 0)]))
            L.append(spec_simple(w_in, DIN, 0, KC, 6224 + g * 512, 512))
            L.append(spec_simple(w_in, DIN, 0, KC, 8272 + g * 512, 512))
        for g in range(4):
            L.append(spec_simple(w_out, D, 0, KC, g * 512, 512))
        for hf in range(2):
            for i in range(8):
                L.append(spec_simple(w_up, DFF, 0, KC, hf * 4096 + i * 512, 512))
            for cb in range(4):
                for sp_ in range(2):
                    L.append(spec_simple(w_dn, D, hf * 4096 + sp_ * 2048, KC, cb * 512, 512))
        for g in range(4):
            L.append(spec_simple(w_pg, D, 0, KC, g * 512, 512))
        return L

    NPT = 66
    wbf = nc.dram_tensor("wbf", [NPT, 128, KC * 512], BF16).ap()

    def get_panels(n):
        i = pan_ptr[0]; pan_ptr[0] += n
        assert n <= NPAN
        while pan_issued[0] < len(panel_specs) and pan_issued[0] < i + NPAN:
            j = pan_issued[0]; pan_issued[0] += 1
            fn, nd = panel_specs[j]
            slot = j % NPAN
            jm = j % NPT
            flat = pans[slot][:].rearrange("p k c -> p (k c)")
            if j < NPT:
                sc.add("pool", fn(slot), [], [f"pan{slot}"], slot=f"pan{slot}", ndma=nd, arena=False)
                sc.add("sp", (lambda jm, flat: lambda e: [e.dma_start(out=wbf[jm], in_=flat)])(jm, flat),
                       [f"pan{slot}"], [f"wbf{jm}"], slot=f"wst{slot}", arena=False)
            else:
                sc.add("pool", (lambda jm, flat: lambda e: [e.dma_start(out=flat, in_=wbf[jm])])(jm, flat),
                       [f"wbf{jm}"], [f"pan{slot}"], slot=f"pan{slot}", arena=False)
        return [(pans[(i + k) % NPAN], f"pan{(i + k) % NPAN}") for k in range(n)]

    def get_panel():
        return get_panels(1)[0]

    def norm_to_hT(src, src_keys, gcols, goff, xs_v, tag):
        for s_ in range(NB):
            ACT(xs_v[:, s_, :], src[:, s_, :], AF.Square, [src_keys[s_]], [f"xs{s_}", f"ss{s_}"],
                accum_out=stat[:, s_:s_ + 1])
        for s_ in range(NB):
            ACT(stat[:, 4 + s_:5 + s_], stat[:, s_:s_ + 1], AF.Sqrt, [f"ss{s_}"], [f"sd{s_}"], scale=1.0 / D, bias=EPS)
            DVE("reciprocal", [f"sd{s_}"], [f"rs{s_}"], out=stat[:, 8 + s_:9 + s_], in_=stat[:, 4 + s_:5 + s_])
            DVE("tensor_scalar", [src_keys[s_], f"rs{s_}", f"xs{s_}"], [f"xs{s_}"], out=xs_v[:, s_, :], in0=src[:, s_, :],
                scalar1=stat[:, 8 + s_:9 + s_], scalar2=None, op0=ALU.mult)
        if STOP == 'N0':
            return
        for c4 in range(4):
            b = nbank()
            pb = banks[b][:].bitcast(BF16)
            items = []
            for cc in range(4):
                c = c4 * 4 + cc
                for s_ in range(NB):
                    items.append((pb[:, cc * 256 + s_ * 128: cc * 256 + (s_ + 1) * 128], xs_v[:, s_, c * 128:(c + 1) * 128]))
            TRS(items, ident_b[:], ["xs0", "xs1", "ident_b"], pkeys(b))
            for cc in range(4):
                c = c4 * 4 + cc
                if c4 % 2 == 0:
                    ACT(hT[:, c, :], pb[:, cc * 256:(cc + 1) * 256], AF.Copy, pkeys(b) + [gcols[1]], [f"hT{c}"],
                        scale=gcols[0][:, goff + c:goff + c + 1])
                else:
                    DVE("tensor_scalar", pkeys(b) + [gcols[1]], [f"hT{c}"], out=hT[:, c, :], in0=pb[:, cc * 256:(cc + 1) * 256],
                        scalar1=gcols[0][:, goff + c:goff + c + 1], scalar2=None, op0=ALU.mult)

    HTK = [f"hT{c}" for c in range(KC)]

    def tile_body(q, tt):
        t0 = tt * T
        sc.arena_phase()
        for s_ in range(NB):
            sc.add("sp", (lambda s_: lambda e: [e.dma_start(out=xld[:, s_, :], in_=x_d[q, t0 + s_ * 128:t0 + (s_ + 1) * 128, :])])(s_),
                   [], [f"xld{s_}"], slot=f"xld{s_}", arena=True)
        sc.add("sp", lambda e: [e.dma_start(out=posi[:, :], in_=bass.AP(pos_d.tensor, q * S + t0, [[1, 128], [128, NB]]),
                                            allow_slow_non_contiguous=True)],
               [], ["posi"], slot="posi", arena=False)
        norm_to_hT(xld, ["xld0", "xld1"], (colsA, "colsA"), 96, xs, "n1")

        if STOP == 'N':
            return
        sc.arena_phase()
        DVE("tensor_copy", ["posi"], ["posf"], arena=False, out=posf[:, :], in_=posi[:, :])
        for b_ in range(NB):
            uu = rtmp
            DVE("tensor_scalar", ["posf", "bc_inv_freq", "rtmp"], ["rtmp"], out=uu[:, 0:32], in0=bc_inv[:, :],
                scalar1=posf[:, b_:b_ + 1], scalar2=1.0 / TWO_PI, op0=ALU.mult, op1=ALU.mult)
            DVE("tensor_scalar_add", ["rtmp"], ["rtmp2"], out=uu[:, 32:64], in0=uu[:, 0:32], scalar1=0.25)
            DVE("tensor_copy", ["rtmp", "rtmp2"], ["krn"], out=krn[:, :].bitcast(I32), in_=uu[:, :])
            DVE("tensor_copy", ["krn"], ["sqb"], out=sqb[:, 0:64], in_=krn[:, :].bitcast(I32))
            DVE("tensor_sub", ["rtmp", "rtmp2", "sqb"], ["rtmp", "rtmp2"], out=uu[:, :], in0=uu[:, :], in1=sqb[:, 0:64])
            DVE("scalar_tensor_tensor", ["rtmp", "rtmp2"], ["sqb"], out=sqb[:, 0:64], in0=uu[:, :], scalar=0.0, in1=uu[:, :],
                op0=ALU.is_lt, op1=ALU.add)
            ACT(cs[:, b_, :], sqb[:, 0:64], AF.Sin, ["sqb"], [f"cs{b_}"], scale=-TWO_PI, bias=math.pi)

        def rope(dst, src, nh, b_, rd, wr):
            sin_ = cs[:, b_, 0:32].unsqueeze(1).broadcast_to([128, nh, 32])
            cos_ = cs[:, b_, 32:64].unsqueeze(1).broadcast_to([128, nh, 32])
            t1 = sqb[:, 0:nh * 32].rearrange("p (h d) -> p h d", h=nh)
            t2 = sqb[:, 256:256 + nh * 32].rearrange("p (h d) -> p h d", h=nh)
            t3 = sqb[:, 512:512 + nh * 32].rearrange("p (h d) -> p h d", h=nh)
            t4 = sqb[:, 768:768 + nh * 32].rearrange("p (h d) -> p h d", h=nh)
            x1_ = src[:, :, 0:32]; x2_ = src[:, :, 32:64]
            DVE("tensor_tensor", rd + [f"cs{b_}", "sqb"], ["sqb"], out=t1, in0=x1_, in1=cos_, op=ALU.mult)
            DVE("tensor_tensor", rd + [f"cs{b_}", "sqb"], ["sqb"], out=t2, in0=x2_, in1=sin_, op=ALU.mult)
            DVE("tensor_tensor", rd + [f"cs{b_}", "sqb"], ["sqb"], out=t3, in0=x2_, in1=cos_, op=ALU.mult)
            DVE("tensor_tensor", rd + [f"cs{b_}", "sqb"], ["sqb"], out=t4, in0=x1_, in1=sin_, op=ALU.mult)
            DVE("tensor_sub", ["sqb"], wr, out=dst[:, :, 0:32], in0=t1, in1=t2)
            DVE("tensor_add", ["sqb"] + wr, wr, out=dst[:, :, 32:64], in0=t3, in1=t4)

        qpan = get_panels(3)
        for s_ in range(NB):
            for i in range(3):
                pan, pk = qpan[i]
                b = nbank()
                MM(banks[b][:, :], [(hT[:, kc, s_ * 128:(s_ + 1) * 128], pan[:, kc, :]) for kc in range(KC)], HTK + [pk], pkeys(b))
                if i % 2 == 0:
                    ACT(rawb[:, i * 512:(i + 1) * 512], banks[b][:, :], AF.Copy, pkeys(b), [f"raw{i}"])
                else:
                    DVE("tensor_copy", pkeys(b), [f"raw{i}"], out=rawb[:, i * 512:(i + 1) * 512], in_=banks[b][:, :])
            RAW = ["raw0", "raw1", "raw2", "raw3"]
            ACT(sqb[:, 0:1536], rawb[:, 0:1536], AF.Square, RAW, ["sqb"])
            sq3 = sqb[:, 0:1536].rearrange("p (h d) -> p h d", h=H)
            raw3 = rawb[:, 0:1536].rearrange("p (h d) -> p h d", h=H)
            DVE("tensor_reduce", ["sqb"], ["stat16"], out=stat[:, 16:24], in_=sq3[:, :, 0:128], axis=AX.X, op=ALU.add)
            DVE("tensor_reduce", ["sqb"], ["stat24"], out=stat[:, 24:32], in_=sq3[:, :, 128:192], axis=AX.X, op=ALU.add)
            rsqrt_cols(stat[:, 16:24], stat[:, 32:40], 8, 1.0 / 128, EPS, ["stat16"], ["stat32"])
            rsqrt_cols(stat[:, 24:32], stat[:, 40:48], 8, 1.0 / 64, EPS, ["stat24"], ["stat40"])
            DVE("tensor_tensor", RAW + ["stat32", "sqb"], ["sqb"], out=sqb[:, 0:1024].rearrange("p (h d) -> p h d", h=H),
                in0=raw3[:, :, 0:128], in1=stat[:, 32:40].unsqueeze(2).broadcast_to([128, H, 128]), op=ALU.mult)
            DVE("tensor_tensor", ["sqb", "bc_q_nope_norm"], ["qnb"], out=qnb, in0=sqb[:, 0:1024].rearrange("p (h d) -> p h d", h=H),
                in1=bc_qn[:, :].unsqueeze(1).broadcast_to([128, H, 128]), op=ALU.mult)
            DVE("tensor_tensor", RAW + ["stat40", "sqb"], ["sqb"], out=sqb[:, 1024:1536].rearrange("p (h d) -> p h d", h=H),
                in0=raw3[:, :, 128:192], in1=stat[:, 40:48].unsqueeze(2).broadcast_to([128, H, 64]), op=ALU.mult)
            DVE("tensor_tensor", ["sqb", "bc_q_rope_norm"], ["qrb"], out=qrb, in0=sqb[:, 1024:1536].rearrange("p (h d) -> p h d", h=H),
                in1=bc_qr[:, :].unsqueeze(1).broadcast_to([128, H, 64]), op=ALU.mult)
            rope(qrr, qrb, H, s_, ["qrb", "sqb", "sqb"], ["qrr"])
            b = nbank(); pb = banks[b][:].bitcast(BF16)
            TRS([(pb[:, h * 128:(h + 1) * 128], qnb[:, h, :]) for h in range(H)], ident_b[:], ["qnb", "ident_b"], pkeys(b))
            ACT(qnT[:, :, s_ * 128:(s_ + 1) * 128], pb[:, :].rearrange("p (h t) -> p h t", h=H), AF.Copy, pkeys(b), ["qnT"])
            b = nbank(); pb = banks[b][:].bitcast(BF16)
            TRS([(pb[0:64, h * 128:(h + 1) * 128], qrr[:, h, :]) for h in range(H)], ident_b[:], ["qrr", "ident_b"], pkeys(b))
            DVE("tensor_copy", pkeys(b), ["qrT"], out=qrT[0:64, :, s_ * 128:(s_ + 1) * 128],
                in_=pb[0:64, :].rearrange("p (h t) -> p h t", h=H))

        (cpan, cpk), (mpan, mpk), (kvpan, kvpk) = get_panels(3)
        for s_ in range(NB):
            blk = tt * NB + s_
            tsl = slice(t0 + s_ * 128, t0 + (s_ + 1) * 128)
            b = nbank()
            MM(banks[b][:, 0:80], [(hT[:, kc, s_ * 128:(s_ + 1) * 128], mpan[:, kc, 0:80]) for kc in range(KC)], HTK + [mpk], pkeys(b))
            ACT(miscb[:, :], banks[b][:, 0:80], AF.Copy, pkeys(b), ["miscb"])
            ACT(stat[:, 48:56], miscb[:, 0:8], AF.Tanh, ["miscb"], ["tb"], scale=0.5)
            DVE("tensor_scalar", ["tb"], [f"beta{s_}"], arena=False, out=betas[:, s_, :], in0=stat[:, 48:56], scalar1=0.5, scalar2=0.5,
                op0=ALU.mult, op1=ALU.add)
            DVE("tensor_scalar_mul", [f"beta{s_}"], [f"hbeta{s_}"], arena=False, out=hbeta[:, s_, :], in0=betas[:, s_, :], scalar1=0.5)
            DVE("tensor_add", ["miscb", "bc_dt_bias"], ["ga"], out=stat[:, 56:64], in0=miscb[:, 8:16], in1=bc_dtb[:, :])
            ACT(stat[:, 56:64], stat[:, 56:64], AF.Exp, ["ga"], ["ga"])
            ACT(stat[:, 56:64], stat[:, 56:64], AF.Ln, ["ga"], ["ga"], bias=1.0)
            DVE("tensor_mul", ["ga", "negA"], [f"g{s_}"], arena=False, out=gts[:, s_, :], in0=stat[:, 56:64], in1=negA[:, :])
            b2 = nbank()
            MM(banks[b2][:, 0:8], [(Umask[:, :], gts[:, s_, :])], [f"g{s_}", "Umask"], pkeys(b2, 0), arena=False)
            MM(banks[b2][:, 8:16], [(SUmask[:, :], gts[:, s_, :])], [f"g{s_}", "SUmask"], pkeys(b2, 0), arena=False)
            MM(banks[b2][:, 16:24], [(ones_f[:, :], gts[:, s_, :])], [f"g{s_}", "ones_f"], pkeys(b2, 0), arena=False)
            ACT(eg[:, s_, :], banks[b2][:, 0:24], AF.Exp, pkeys(b2, 0), [f"eg{s_}"], arena=False)
            DVE("tensor_mul", [f"eg{s_}", f"beta{s_}"], [f"bge{s_}"], arena=False, out=bge[:, s_, :], in0=eg[:, s_, 0:8], in1=betas[:, s_, :])
            ACT(sqb[:, 1536:1600], miscb[:, 16:80], AF.Square, ["miscb"], ["sqb", "st_kr"], accum_out=stat[:, 15:16])
            rsqrt_cols(stat[:, 15:16], stat[:, 14:15], 1, 1.0 / 64, EPS, ["st_kr"], ["rs_kr"])
            DVE("scalar_tensor_tensor", ["miscb", "rs_kr", "bc_k_rope_norm"], ["krn"], out=krn[:, :], in0=miscb[:, 16:80],
                scalar=stat[:, 14:15], in1=bc_kr[:, :], op0=ALU.mult, op1=ALU.mult)
            rope(krr.rearrange("p (h d) -> p h d", h=1), krn.rearrange("p (h d) -> p h d", h=1), 1, s_, ["krn"], ["krr"])
            b3 = nbank(); pb = banks[b3][:].bitcast(BF16)
            TR(pb[0:64, 0:128], krr[:, :], ident_b[:], ["krr", "ident_b"], pkeys(b3))
            ACT(krT[0:64, tsl], pb[0:64, 0:128], AF.Copy, pkeys(b3), ["krT"])
            b = nbank()
            MM(banks[b][:, :], [(hT[:, kc, s_ * 128:(s_ + 1) * 128], cpan[:, kc, :]) for kc in range(KC)], HTK + [cpk], pkeys(b))
            ACT(sqb[:, 0:512], banks[b][:, :], AF.Square, pkeys(b), ["sqb", "st_c"], accum_out=stat[:, 13:14])
            rsqrt_cols(stat[:, 13:14], stat[:, 12:13], 1, 1.0 / 512, EPS, ["st_c"], ["rs_c"])
            DVE("scalar_tensor_tensor", pkeys(b) + ["rs_c", "bc_ckv_norm"], ["cn"], out=cn[:, :], in0=banks[b][:, :],
                scalar=stat[:, 12:13], in1=bc_ckv[:, :], op0=ALU.mult, op1=ALU.mult)
            b = nbank(); pb = banks[b][:].bitcast(BF16)
            TRS([(pb[:, c * 128:(c + 1) * 128], cn[:, c * 128:(c + 1) * 128]) for c in range(4)], ident_b[:], ["cn", "ident_b"], pkeys(b))
            ACT(cT, pb[:, 0:512].rearrange("p (c t) -> p c t", c=4), AF.Copy, pkeys(b), ["cT"])
            for i in range(4):
                b = nbank()
                MM(banks[b][:, :], [(cT[:, kc, :], kvpan[:, 4 * i + kc, :]) for kc in range(4)], ["cT", kvpk], pkeys(b))
                if i % 2 == 0:
                    ACT(rawb[:, i * 512:(i + 1) * 512], banks[b][:, :], AF.Copy, pkeys(b), [f"raw{i}"])
                else:
                    DVE("tensor_copy", pkeys(b), [f"raw{i}"], out=rawb[:, i * 512:(i + 1) * 512], in_=banks[b][:, :])
            RAW = ["raw0", "raw1", "raw2", "raw3"]
            kv3 = rawb[:, :].rearrange("p (h d) -> p h d", h=H)
            ACT(sqb[:, :], rawb[:, :], AF.Square, RAW, ["sqb"])
            DVE("tensor_reduce", ["sqb"], ["stat16"], out=stat[:, 16:24], in_=sqb[:, :].rearrange("p (h d) -> p h d", h=H)[:, :, 0:128],
                axis=AX.X, op=ALU.add)
            rsqrt_cols(stat[:, 16:24], stat[:, 32:40], 8, 1.0 / 128, EPS, ["stat16"], ["stat32"])
            DVE("tensor_tensor", RAW + ["stat32", "sqb"], ["sqb"], out=sqb[:, 0:1024].rearrange("p (h d) -> p h d", h=H),
                in0=kv3[:, :, 0:128], in1=stat[:, 32:40].unsqueeze(2).broadcast_to([128, H, 128]), op=ALU.mult)
            DVE("tensor_tensor", ["sqb", "bc_k_nope_norm"], ["knb"], out=knb, in0=sqb[:, 0:1024].rearrange("p (h d) -> p h d", h=H),
                in1=bc_kn[:, :].unsqueeze(1).broadcast_to([128, H, 128]), op=ALU.mult)
            ACT(Vc[:, blk, :, :], kv3[:, :, 128:256], AF.Copy, RAW, ["Vc"])
            b = nbank(); pb = banks[b][:].bitcast(BF16)
            TRS([(pb[:, h * 128:(h + 1) * 128], knb[:, h, :]) for h in range(H)], ident_b[:], ["knb", "ident_b"], pkeys(b))
            DVE("tensor_copy", pkeys(b), ["knT"], out=knT[:, :, tsl], in_=pb[:, :].rearrange("p (h t) -> p h t", h=H))

        if STOP == 'M1':
            return
        sc.arena_phase()
        cnt = [0]
        for h in range(H):
            pan, pk = get_panel()
            for j in range(4):
                b = nbank()
                hb = cnt[0] % 2
                pout = banks[b][:, hb * 256:(hb + 1) * 256]
                MM(pout, [(pan[:, kc, j * 128:(j + 1) * 128], hT[:, kc, :]) for kc in range(KC)], HTK + [pk], pkeys(b, hb))
                if j == 3:
                    r = cnt[0] % 2
                    ACT(tnhb[r][:, :], pout, AF.Tanh, pkeys(b, hb), [f"tnh{r}"], scale=0.5)
                    DVE("scalar_tensor_tensor", pkeys(b, hb) + [f"tnh{r}"], ["zT"], out=zT[:, h, :], in0=tnhb[r][:, :], scalar=1.0,
                        in1=pout, op0=ALU.add, op1=ALU.mult)
                    cnt[0] += 1
                    continue
                r = cnt[0] % 2; cnt[0] += 1
                ch = j * 8 + h
                ACT(ubuf[r][:, 3:259], pout, AF.Copy, pkeys(b, hb), [f"ub{r}"])
                DVE("tensor_copy", ["carry", f"ub{r}"], [f"ub{r}"], out=ubuf[r][:, 0:3], in_=carry[:, ch, :])
                DVE("tensor_copy", [f"ub{r}"], ["carry"], out=carry[:, ch, :], in_=ubuf[r][:, 256:259])
                DVE("tensor_scalar", [f"ub{r}", "colsA"], [f"acc{r}"], out=accb[r][:, :], in0=ubuf[r][:, 3:259],
                    scalar1=colsA[:, 3 * 24 + ch:3 * 24 + ch + 1], scalar2=None, op0=ALU.mult)
                for k_ in (2, 1, 0):
                    DVE("scalar_tensor_tensor", [f"ub{r}", "colsA", f"acc{r}"], [f"acc{r}"], out=accb[r][:, :], in0=ubuf[r][:, k_:k_ + 256],
                        scalar=colsA[:, k_ * 24 + ch:k_ * 24 + ch + 1], in1=accb[r][:, :], op0=ALU.mult, op1=ALU.add)
                ACT(tnhb[r][:, :], accb[r][:, :], AF.Tanh, [f"acc{r}"], [f"tnh{r}"], scale=0.5)
                if j == 2:
                    DVE("scalar_tensor_tensor", [f"acc{r}", f"tnh{r}"], ["vTb"], out=vTb[:, h, :], in0=tnhb[r][:, :], scalar=1.0,
                        in1=accb[r][:, :], op0=ALU.add, op1=ALU.mult)
                    b2 = nbank(); pb = banks[b2][:].bitcast(BF16)
                    TRS([(pb[:, s_ * 128:(s_ + 1) * 128], vTb[:, h, s_ * 128:(s_ + 1) * 128]) for s_ in range(NB)], ident_b[:],
                        ["vTb", "ident_b"], pkeys(b2, 0))
                    for s_ in range(NB):
                        ACT(vbt[:, s_, h, :], pb[:, s_ * 128:(s_ + 1) * 128], AF.Copy, pkeys(b2, 0) + [f"hbeta{s_}"], ["vbt"],
                            scale=hbeta[:, s_, h:h + 1])
                    continue
                DVE("scalar_tensor_tensor", [f"acc{r}", f"tnh{r}"], [f"s2{r}"], out=s2b[r][:, :], in0=tnhb[r][:, :], scalar=1.0,
                    in1=accb[r][:, :], op0=ALU.add, op1=ALU.mult)
                ACT(sq2b[r][:, :], s2b[r][:, :], AF.Square, [f"s2{r}"], [f"sq2{r}"])
                b2 = nbank()
                MM(banks[b2][:, 0:256], [(ones_f[:, :], sq2b[r][:, :])], [f"sq2{r}", "ones_f"], pkeys(b2, 0))
                if j == 0:
                    ACT(rstb[r][:, :], banks[b2][:, 0:256], AF.Sqrt, pkeys(b2, 0), [f"rst{r}"], scale=128.0, bias=512.0 * EPS)
                else:
                    ACT(rstb[r][:, :], banks[b2][:, 0:256], AF.Sqrt, pkeys(b2, 0), [f"rst{r}"], scale=1.0, bias=4.0 * EPS)
                DVE("reciprocal", [f"rst{r}"], [f"rst{r}"], out=rstb[r][:, :], in_=rstb[r][:, :])
                dstT = qT if j == 0 else kT
                DVE("tensor_mul", [f"s2{r}", f"rst{r}"], ["qT" if j == 0 else "kT"], out=dstT[:, h, :], in0=s2b[r][:, :], in1=rstb[r][:, :])
                if j == 1:
                    b3 = nbank(); pb = banks[b3][:].bitcast(BF16)
                    TRS([(pb[:, s_ * 128:(s_ + 1) * 128], kT[:, h, s_ * 128:(s_ + 1) * 128]) for s_ in range(NB)], ident_b[:],
                        ["kT", "ident_b"], pkeys(b3, 0))
                    for s_ in range(NB):
                        ACT(kbd[:, s_, h, :], pb[:, s_ * 128:(s_ + 1) * 128], AF.Copy, pkeys(b3, 0) + [f"bge{s_}"], ["kbd"],
                            scale=bge[:, s_, h:h + 1])
                        DVE("tensor_scalar", pkeys(b3, 0) + [f"eg{s_}"], ["kdec"], out=kdec[:, s_, h, :], in0=pb[:, s_ * 128:(s_ + 1) * 128],
                            scalar1=eg[:, s_, 8 + h:9 + h], scalar2=None, op0=ALU.mult)

        if STOP == 'M2':
            return
        sc.arena_phase()
        for s_ in range(NB):
            bs = slice(s_ * 128, (s_ + 1) * 128)
            for hg in range(2):
                hs = [hg * 4 + i for i in range(4)]
                for i, h in enumerate(hs):
                    DVE("tensor_scalar", ["Umask", f"g{s_}", "gUb"], ["gUb"], out=gUb[:, i, :], in0=Umask[:, :], scalar1=gts[:, s_, h:h + 1],
                        scalar2=None, op0=ALU.mult)
                bD = nbank()
                for i, h in enumerate(hs):
                    MM(banks[bD][:, i * 128:(i + 1) * 128], [(gUb[:, i, :], ones_f[:, :]), (negones_f[:, :], gUb[:, i, :]), (ident_f[:, :], maskb[:, :])],
                       ["gUb", "ones_f", "negones_f", "ident_f", "maskb"], pkeys(bD))
                ACT(decb, banks[bD][:, :].rearrange("p (h d) -> p h d", h=4), AF.Exp, pkeys(bD), ["decb"])
                bK = nbank(); bQ = nbank()
                for i, h in enumerate(hs):
                    MM(banks[bK][:, i * 128:(i + 1) * 128], [(kT[:, h, bs], kT[:, h, bs])], ["kT"], pkeys(bK))
                    MM(banks[bQ][:, i * 128:(i + 1) * 128], [(qT[:, h, bs], kT[:, h, bs])], ["kT", "qT"], pkeys(bQ))
                for i, h in enumerate(hs):
                    DVE("scalar_tensor_tensor", pkeys(bK) + ["decb", f"beta{s_}", "A0"], ["A0"], out=Ab[0][:, i, :], in0=banks[bK][:, i * 128:(i + 1) * 128],
                        scalar=betas[:, s_, h:h + 1], in1=decb[:, i, :], op0=ALU.mult, op1=ALU.mult)
                DVE("tensor_tensor", ["A0", "SUmask"], ["A0"], out=Ab[0], in0=Ab[0], in1=SLmask[:, :].unsqueeze(1).broadcast_to([128, 4, 128]), op=ALU.mult)
                DVE("tensor_tensor", pkeys(bQ) + ["decb"], ["qkb"], out=qkb, in0=banks[bQ][:, :].rearrange("p (h d) -> p h d", h=4), in1=decb, op=ALU.mult)
                bT = nbank()
                for i in range(4):
                    MM(banks[bT][:, i * 128:(i + 1) * 128], [(Ab[0][:, i, :], ident_f[:, :])], ["A0", "ident_f"], pkeys(bT))
                ACT(Mb[0], banks[bT][:, :].rearrange("p (h d) -> p h d", h=4), AF.Copy, pkeys(bT), ["M0"])
                bT2 = nbank(); pbT = banks[bT2][:].bitcast(BF16)
                TRS([(pbT[:, i * 128:(i + 1) * 128], qkb[:, i, :]) for i in range(4)], ident_b[:], ["qkb", "ident_b"], pkeys(bT2))
                DVE("tensor_copy", pkeys(bT2), ["qkTb"], out=qkTb, in_=pbT[:, 0:512].rearrange("p (h d) -> p h d", h=4))
                DVE("tensor_tensor", ["M0", "ident_f"], ["Q0"], out=Qb[0], in0=ident_f[:, :].unsqueeze(1).broadcast_to([128, 4, 128]), in1=Mb[0], op=ALU.subtract)
                cur = 0
                for k_ in range(1, 7):
                    nx = 1 - cur
                    bA = nbank()
                    for i in range(4):
                        MM(banks[bA][:, i * 128:(i + 1) * 128], [(Mb[cur][:, i, :], Ab[cur][:, i, :])], [f"M{cur}", f"A{cur}"], pkeys(bA))
                    ACT(Ab[nx], banks[bA][:, :].rearrange("p (h d) -> p h d", h=4), AF.Copy, pkeys(bA), [f"A{nx}"])
                    if k_ < 6:
                        bM = nbank()
                        for i in range(4):
                            MM(banks[bM][:, i * 128:(i + 1) * 128], [(Ab[cur][:, i, :], Mb[cur][:, i, :])], [f"M{cur}", f"A{cur}"], pkeys(bM))
                        DVE("tensor_copy", pkeys(bM), [f"M{nx}"], out=Mb[nx], in_=banks[bM][:, :].rearrange("p (h d) -> p h d", h=4))
                    bQ2 = nbank()
                    for i in range(4):
                        MM(banks[bQ2][:, i * 128:(i + 1) * 128], [(Ab[nx][:, i, :], Qb[cur][:, i, :])], [f"A{nx}", f"Q{cur}"], pkeys(bQ2))
                    if k_ < 6:
                        DVE("tensor_tensor", pkeys(bQ2) + [f"Q{cur}"], [f"Q{nx}"], out=Qb[nx], in0=banks[bQ2][:, :].rearrange("p (h d) -> p h d", h=4),
                            in1=Qb[cur], op=ALU.add)
                    else:
                        DVE("tensor_tensor", pkeys(bQ2) + [f"Q{cur}"], ["Qfin"], out=Qfin, in0=banks[bQ2][:, :].rearrange("p (h d) -> p h d", h=4),
                            in1=Qb[cur], op=ALU.add)
                    cur = nx
                TT_ = Qfin; TK = "Qfin"
                bW = nbank(); bU = nbank()
                for i, h in enumerate(hs):
                    MM(banks[bW][:, i * 128:(i + 1) * 128], [(kbd[:, s_, h, :], TT_[:, i, :])], ["kbd", TK], pkeys(bW))
                    MM(banks[bU][:, i * 128:(i + 1) * 128], [(TT_[:, i, :], vbt[:, s_, h, :])], ["vbt", TK], pkeys(bU))
                ACT(wTb, banks[bW][:, :].rearrange("p (h d) -> p h d", h=4), AF.Copy, pkeys(bW), ["wTb"])
                ACT(ub, banks[bU][:, :].rearrange("p (h d) -> p h d", h=4), AF.Copy, pkeys(bU), ["ub"])
                bV = nbank()
                for i, h in enumerate(hs):
                    MM(banks[bV][:, i * 128:(i + 1) * 128], [(wTb[:, i, :], St_b[:, h, :])], ["wTb", f"Stb{hg}"], pkeys(bV))
                DVE("tensor_tensor", pkeys(bV) + ["ub"], ["vnb"], out=vnb, in0=ub, in1=banks[bV][:, :].rearrange("p (h d) -> p h d", h=4), op=ALU.subtract)
                bO1 = nbank(); bO2 = nbank()
                for i, h in enumerate(hs):
                    MM(banks[bO1][:, i * 128:(i + 1) * 128], [(qT[:, h, bs], St_b[:, h, :])], ["qT", f"Stb{hg}"], pkeys(bO1))
                    MM(banks[bO2][:, i * 128:(i + 1) * 128], [(qkTb[:, i, :], vnb[:, i, :])], ["qkTb", "vnb"], pkeys(bO2))
                ACT(o2b, banks[bO2][:, :].rearrange("p (h d) -> p h d", h=4), AF.Copy, pkeys(bO2), ["o2b"])
                for i, h in enumerate(hs):
                    DVE("scalar_tensor_tensor", pkeys(bO1) + ["o2b", f"eg{s_}", "ob"], ["ob"], out=ob[:, i, :], in0=banks[bO1][:, i * 128:(i + 1) * 128],
                        scalar=eg[:, s_, h:h + 1], in1=o2b[:, i, :], op0=ALU.mult, op1=ALU.add)
                bS = nbank()
                for i, h in enumerate(hs):
                    MM(banks[bS][:, i * 128:(i + 1) * 128], [(kdec[:, s_, h, :], vnb[:, i, :])], ["kdec", "vnb"], pkeys(bS))
                for i, h in enumerate(hs):
                    DVE("scalar_tensor_tensor", pkeys(bS) + [f"eg{s_}", f"St{hg}"], [f"St{hg}"], arena=False, out=St[:, h, :], in0=St[:, h, :],
                        scalar=eg[:, s_, 16 + h:17 + h], in1=banks[bS][:, i * 128:(i + 1) * 128], op0=ALU.mult, op1=ALU.add)
                ACT(St_b[:, hg * 4:hg * 4 + 4, :], St[:, hg * 4:hg * 4 + 4, :], AF.Copy, [f"St{hg}"], [f"Stb{hg}"], arena=False)
                ACT(o2b, ob, AF.Square, ["ob", "o2b"], ["o2b"])
                DVE("tensor_reduce", ["o2b"], ["stat16"], out=stat[:, 16:20], in_=o2b, axis=AX.X, op=ALU.add)
                rsqrt_cols(stat[:, 16:20], stat[:, 32:36], 4, 1.0 / 128, EPS, ["stat16"], ["stat32"])
                DVE("tensor_tensor", ["ob", "stat32"], ["onb"], out=onb, in0=ob, in1=stat[:, 32:36].unsqueeze(2).broadcast_to([128, 4, 128]), op=ALU.mult)
                bN = nbank(); pbN = banks[bN][:].bitcast(BF16)
                TRS([(pbN[:, i * 128:(i + 1) * 128], onb[:, i, :]) for i in range(4)], ident_b[:], ["onb", "ident_b"], pkeys(bN, 0))
                for i, h in enumerate(hs):
                    DVE("scalar_tensor_tensor", pkeys(bN, 0) + ["zT", "half_dno"], ["oaT"], out=oaT[:, h, bs], in0=pbN[:, i * 128:(i + 1) * 128],
                        scalar=half_dno[:, 0:1], in1=zT[:, h, bs], op0=ALU.mult, op1=ALU.mult)

        if STOP == 'B':
            return
        SCALE = 192.0 ** -0.5
        nkb = tt * NB + NB
        pcnt = [0]
        for h in range(H):
            OT = banks[6][:, 0:256]; RT = banks[7][:, 0:256]
            for kb in range(nkb):
                m = kb - tt * NB
                q0 = 0 if m <= 0 else m * 128
                nq = T - q0
                ksl = slice(kb * 128, (kb + 1) * 128)
                b = nbank()
                MM(banks[b][:, 0:nq], [(knT[:, h, ksl], qnT[:, h, q0:T]), (krT[0:64, ksl], qrT[0:64, h, q0:T])],
                   ["knT", "krT", "qnT", "qrT"], pkeys(b, 0))
                pi = pcnt[0] % 3; pcnt[0] += 1
                ACT(Pt[pi][:, 0:nq], banks[b][:, 0:nq], AF.Exp, pkeys(b, 0), [f"Pt{pi}"], arena=False, scale=SCALE)
                if m >= 0:
                    DVE("memset", [f"Pt{pi}"], [f"Pt{pi}"], arena=False, ap=Pt[pi][64:128, 0:64], constant=0.0)
                first = (kb == 0)
                last = (kb == nkb - 1)

                def fn(e, pi=pi, h=h, kb=kb, q0=q0, nq=nq, first=first, last=last, OT=OT, RT=RT):
                    e.matmul(OT[:, q0:T], lhsT=Vc[:, kb, h, :], rhs=Pt[pi][:, 0:nq], start=first, stop=last)
                    return e.matmul(RT[:, q0:T], lhsT=ones_b[:, :], rhs=Pt[pi][:, 0:nq], start=first, stop=last)
                sc.add("pe", fn, [f"Pt{pi}", "Vc", "ones_b"], pkeys(6) + pkeys(7), arena=False)
            DVE("reciprocal", pkeys(7), ["rinv"], arena=True, out=rinvt[:, :], in_=RT)
            DVE("tensor_tensor", pkeys(6) + ["rinv"], ["obT"], arena=True, out=obT[:, h, :], in0=OT, in1=rinvt[:, :], op=ALU.mult)

        if STOP == 'C':
            return
        sc.arena_phase()
        for s_ in range(NB):
            sc.add("sp", (lambda s_: lambda e: [e.dma_start(out=x1[:, s_, :], in_=x_d[q, t0 + s_ * 128:t0 + (s_ + 1) * 128, :])])(s_),
                   [], [f"x1_{s_}"], slot=f"x1_{s_}", arena=True)
        for g in range(4):
            (wab, wabk), (gap, gapk), (gbp, gbpk) = get_panels(3)
            for cc in range(4):
                c = g * 4 + cc
                csl = slice(cc * 128, (cc + 1) * 128)
                bA = nbank(); bB = nbank()
                MM(banks[bA][:, 0:256], [(wab[:, kc, csl], oaT[:, kc, :]) for kc in range(8)], ["oaT", wabk], pkeys(bA, 0), arena=False)
                MM(banks[bA][:, 256:512], [(wab[:, 8 + kc, csl], obT[:, kc, :]) for kc in range(8)], ["obT", wabk], pkeys(bA, 1), arena=True)
                MM(banks[bB][:, 0:256], [(gap[:, kc, csl], hT[:, kc, :]) for kc in range(KC)], HTK + [gapk], pkeys(bB, 0), arena=False)
                MM(banks[bB][:, 256:512], [(gbp[:, kc, csl], hT[:, kc, :]) for kc in range(KC)], HTK + [gbpk], pkeys(bB, 1), arena=False)
                ACT(gtb[0][:, :], banks[bB][:, 0:256], AF.Tanh, pkeys(bB, 0), ["gt0"], scale=0.5)
                ACT(gtb[1][:, :], banks[bB][:, 256:512], AF.Tanh, pkeys(bB, 1), ["gt1"], scale=0.5)
                DVE("scalar_tensor_tensor", pkeys(bA, 0) + ["gt0", "gt2"], ["gt2"], out=gtb[2][:, :], in0=gtb[0][:, :], scalar=1.0,
                    in1=banks[bA][:, 0:256], op0=ALU.add, op1=ALU.mult)
                DVE("scalar_tensor_tensor", pkeys(bA, 1) + ["gt1", "gt3"], ["gt3"], out=gtb[3][:, :], in0=gtb[1][:, :], scalar=1.0,
                    in1=banks[bA][:, 256:512], op0=ALU.add, op1=ALU.mult)
                DVE("tensor_add", ["gt2", "gt3"], [f"yT{c}"], out=yT[:, c, :], in0=gtb[2][:, :], in1=gtb[3][:, :])
        YK = [f"yT{c}" for c in range(KC)]
        for g in range(4):
            pan, pk = get_panel()
            for s_ in range(NB):
                b = nbank()
                MM(banks[b][:, :], [(yT[:, kc, s_ * 128:(s_ + 1) * 128], pan[:, kc, :]) for kc in range(KC)], YK + [pk], pkeys(b))
                DVE("scalar_tensor_tensor", pkeys(b) + [f"x1_{s_}"], [f"x1_{s_}"], out=x1[:, s_, g * 512:(g + 1) * 512], in0=banks[b][:, :], scalar=0.5,
                    in1=x1[:, s_, g * 512:(g + 1) * 512], op0=ALU.mult, op1=ALU.add)

        if STOP == 'D':
            return
        sc.arena_phase()
        norm_to_hT(x1, ["x1_0", "x1_1"], (colsA, "colsA"), 112, xs, "n2")
        rc = [0]
        for hf in range(2):
            for i in range(8):
                pan, pk = get_panel()
                for cc in range(4):
                    c = i * 4 + cc
                    b = nbank()
                    hb = cc % 2
                    MM(banks[b][:, hb * 256:(hb + 1) * 256], [(pan[:, kc, cc * 128:(cc + 1) * 128], hT[:, kc, :]) for kc in range(KC)], HTK + [pk], pkeys(b, hb))
                    r = rc[0] % 2; rc[0] += 1
                    DVE("tensor_scalar_max", pkeys(b, hb) + [f"rl{r}"], [f"rl{r}"], out=rlb[r][:, :], in0=banks[b][:, hb * 256:(hb + 1) * 256], scalar1=0.0)
                    ACT(uT[:, c, :], rlb[r][:, :], AF.Square, [f"rl{r}"], [f"uT{c}"])
            UK = [f"uT{c}" for c in range(32)]
            for cb in range(4):
                bs2 = [nbank() for _ in range(NB)]
                for sp_ in range(2):
                    pan, pk = get_panel()
                    for s_ in range(NB):
                        def fn(e, pan=pan, s_=s_, sp_=sp_, bb=bs2[s_]):
                            r_ = None
                            for kc in range(KC):
                                r_ = e.matmul(banks[bb][:, :], lhsT=uT[:, sp_ * 16 + kc, s_ * 128:(s_ + 1) * 128], rhs=pan[:, kc, :],
                                              start=(sp_ == 0 and kc == 0), stop=(sp_ == 1 and kc == KC - 1))
                            return r_
                        sc.add("pe", fn, UK + [pk], pkeys(bs2[s_]), arena=True)
                for s_ in range(NB):
                    DVE("tensor_add", pkeys(bs2[s_]) + [f"x1_{s_}"], [f"x1_{s_}"], out=x1[:, s_, cb * 512:(cb + 1) * 512], in0=banks[bs2[s_]][:, :],
                        in1=x1[:, s_, cb * 512:(cb + 1) * 512])

        if STOP == 'E':
            return
        norm_to_hT(x1, ["x1_0", "x1_1"], (colsB, "colsB"), 0, xs, "n3")
        sc.add("sp", lambda e: [e.dma_start(out=pst[:, s_, :], in_=p_d[q, t0 + s_ * 128:t0 + (s_ + 1) * 128, :]) for s_ in range(NB)],
               [], ["pst"], slot="pst", ndma=NB, arena=True)
        DVE("tensor_copy", ["pst"], ["pbf"], out=pbf, in_=pst)
        b = nbank(); pb = banks[b][:].bitcast(BF16)
        TRS([(pb[:, c * 256 + s_ * 128:c * 256 + (s_ + 1) * 128], pbf[:, s_, c * 128:(c + 1) * 128]) for c in range(2) for s_ in range(NB)],
            ident_b[:], ["pbf", "ident_b"], pkeys(b, 0))
        ACT(pT, pb[:, 0:512].rearrange("p (c t) -> p c t", c=2), AF.Copy, pkeys(b, 0), ["pT"])
        wpl, wplk = wple_sb, "wple_sb"
        sc.add("pool", lambda e: [e.dma_start(out=wple_sb[:, 2 * i:2 * i + 2, :], in_=wsrc(w_ple, D, 0, 2, 512 * i, 512)) for i in range(4)],
               [], ["wple_sb"] + [f"yT{c}" for c in range(KC)], slot="wple", ndma=4, arena=True)
        for g in range(4):
            pan, pk = get_panel()
            for s_ in range(NB):
                bG = nbank(); bE = nbank()
                MM(banks[bG][:, :], [(hT[:, kc, s_ * 128:(s_ + 1) * 128], pan[:, kc, :]) for kc in range(KC)], HTK + [pk], pkeys(bG))
                MM(banks[bE][:, :], [(pT[:, kc, s_ * 128:(s_ + 1) * 128], wpl[:, 2 * g + kc, :]) for kc in range(2)], ["pT", wplk], pkeys(bE))
                gt_ = af(SCR + 6144, 512); gt2_ = af(SCR + 6656, 512)
                ACT(gt_, banks[bG][:, :], AF.Tanh, pkeys(bG) + ["gt0", "gt1"], ["gt0", "gt1"], scale=0.5)
                DVE("scalar_tensor_tensor", pkeys(bE) + ["gt0", "gt1", "gt2", "gt3"], ["gt2", "gt3"], out=gt2_, in0=gt_, scalar=1.0,
                    in1=banks[bE][:, :], op0=ALU.add, op1=ALU.mult)
                DVE("scalar_tensor_tensor", ["gt2", "gt3", f"x1_{s_}"], [f"x1_{s_}"], out=x1[:, s_, g * 512:(g + 1) * 512], in0=gt2_, scalar=0.5,
                    in1=x1[:, s_, g * 512:(g + 1) * 512], op0=ALU.mult, op1=ALU.add)
        for s_ in range(NB):
            sc.add("sp", (lambda s_: lambda e: [e.dma_start(out=out_d[q, t0 + s_ * 128:t0 + (s_ + 1) * 128, :], in_=x1[:, s_, :])])(s_),
                   [f"x1_{s_}"], [], slot=f"out{s_}", arena=True)

    setup()
    for q in range(NSEQ if STOP not in ('S0', 'S1') else 0):
        sc.add("dve", lambda e: e.memset(St[:].rearrange("p h d -> p (h d)"), 0.0), [], ["St0", "St1"], arena=False)
        sc.add("dve", lambda e: e.memset(St_b[:].rearrange("p h d -> p (h d)"), 0.0), [], ["Stb0", "Stb1"], arena=False)
        sc.add("dve", lambda e: e.memset(carry[:].rearrange("p c k -> p (c k)"), 0.0), [], ["carry"], arena=False)
        for tt in range(NTT):
            panel_specs.extend(tile_panel_list())
            tile_body(q, tt)
    assert STOP or pan_ptr[0] == len(panel_specs), (pan_ptr[0], len(panel_specs))
    sc.emit(nc, es, final_slots=["out0", "out1"])
    es.close()
    return nc


_CACHE = {}


def _inv_freq():
    half = 32
    return (10000.0 ** (-np.arange(half, dtype=np.float32) / half)).astype(np.float32)


def _maps(inputs, NSEQ, ncores):
    x = np.asarray(inputs["x"]); p = np.asarray(inputs["p"])[0]; pos = np.asarray(inputs["positions"]).astype(np.int32)
    shared = {}
    for nm in ("w_in", "conv_w", "w_kv_up", "w_branch_a", "w_branch_b", "w_out", "w_mlp_up", "w_mlp_down", "w_ple_gate", "w_ple",
               "mix_norm", "mlp_norm", "ple_norm", "dt_bias", "a_log", "dn_out_norm", "ckv_norm", "q_nope_norm", "q_rope_norm",
               "k_nope_norm", "k_rope_norm"):
        a = np.ascontiguousarray(np.asarray(inputs[nm])[0], dtype=np.float32)
        shared[nm] = a
    shared["inv_freq"] = _inv_freq()
    maps = []
    for c in range(ncores):
        m = dict(shared)
        m["x"] = np.ascontiguousarray(x[c * NSEQ:(c + 1) * NSEQ], dtype=np.float32)
        m["p"] = np.ascontiguousarray(p[c * NSEQ:(c + 1) * NSEQ], dtype=np.float32)
        m["positions"] = np.ascontiguousarray(pos[c * NSEQ:(c + 1) * NSEQ])
        maps.append(m)
    return maps


def kernel(**inputs):
    x = np.asarray(inputs["x"])
    Bt, S, _ = x.shape
    ncores = 8 if Bt % 8 == 0 else 1
    NSEQ = Bt // ncores
    key = (NSEQ, S)
    if key not in _CACHE:
        _CACHE[key] = build(NSEQ, S)
    nc = _CACHE[key]
    res = run_bass_kernel_spmd(nc, _maps(inputs, NSEQ, ncores), core_ids=list(range(ncores)))
    return np.concatenate([r["out"] for r in res.results], axis=0).astype(np.float32)
```
